# Optimizing a Trainium2 kernel written in Bass

```python
import math
import jax, jax.numpy as jnp
from jax import lax
import numpy as np

D_MODEL = 1024
BATCH = 16
SEQ = 4096
DEPTH = 4

DN_ALPHA = (2.0 * DEPTH) ** 0.25
DN_BETA = (8.0 * DEPTH) ** -0.25
NORM_EPS = 1e-5
N_MOD = 9
D_FF = ((8 * D_MODEL // 3 + 127) // 128) * 128

SSD_HEAD_DIM = 64
SSD_HEADS = D_MODEL // SSD_HEAD_DIM
SSD_D_INNER = SSD_HEADS * SSD_HEAD_DIM
SSD_GROUPS = 2
SSD_STATE = 128
SSD_CONV = 4
SSD_CHUNK = 128
SSD_CONV_DIM = SSD_D_INNER + 2 * SSD_GROUPS * SSD_STATE
SSD_IN = SSD_D_INNER + SSD_CONV_DIM + SSD_HEADS

POOL_WINDOWS = (2, 4, 8, 16)
POOL_GROUPS = len(POOL_WINDOWS)
POOL_GROUP_DIM = D_MODEL // 8
POOL_DIM = POOL_GROUPS * POOL_GROUP_DIM

EVEN_IN = SSD_IN + POOL_DIM
EVEN_MIX = SSD_D_INNER + POOL_DIM

CONF_DIM = D_MODEL // 2
CONF_KERNEL = 31
LRU_DIM = D_MODEL
LRU_HEADS = 8
LRU_HEAD_DIM = LRU_DIM // LRU_HEADS
LRU_CONV = 4
LRU_C = 8.0

ODD_IN = 2 * CONF_DIM + 2 * LRU_DIM
ODD_MIX = CONF_DIM + LRU_DIM

N_EVEN = (DEPTH + 1) // 2
N_ODD = DEPTH // 2

kernel_name = "hybrid_ssd_pool_conformer_rglru_trunk"


def layer_norm(x, g, b):
    xf = x.astype(jnp.float32)
    mu = jnp.mean(xf, axis=-1, keepdims=True)
    xc = xf - mu
    var = jnp.mean(xc * xc, axis=-1, keepdims=True)
    return (xc * lax.rsqrt(var + NORM_EPS) * g.astype(jnp.float32) + b.astype(jnp.float32)).astype(x.dtype)


def rms_norm(x, g):
    xf = x.astype(jnp.float32)
    ms = jnp.mean(xf * xf, axis=-1, keepdims=True)
    return xf * lax.rsqrt(ms + NORM_EPS) * g.astype(jnp.float32)


def causal_dwconv(x, w, b):
    k, ch = w.shape
    y = lax.conv_general_dilated(
        x, w[:, None, :].astype(x.dtype), window_strides=(1,), padding=[(k - 1, 0)],
        dimension_numbers=("NWC", "WIO", "NWC"), feature_group_count=ch)
    return y + b.astype(x.dtype)


def modulate(h, shift, scale):
    return h * (1.0 + scale[:, None, :]) + shift[:, None, :]


def post_norm(x, y, g, b):
    return layer_norm(DN_ALPHA * x + y, g, b)


def swiglu(h, w_in, w_out):
    gu = jnp.einsum("btd,df->btf", h, w_in)
    gate, up = jnp.split(gu, 2, axis=-1)
    return jnp.einsum("btf,fd->btd", jax.nn.silu(gate) * up, w_out)


def ssd_mixer(zxbcdt, conv_w, conv_b, dt_bias, a_log, d_skip, norm_g):
    bsz, t_len, _ = zxbcdt.shape
    z = zxbcdt[..., :SSD_D_INNER]
    xbc = zxbcdt[..., SSD_D_INNER:SSD_D_INNER + SSD_CONV_DIM]
    dt = zxbcdt[..., SSD_D_INNER + SSD_CONV_DIM:]
    xbc = jax.nn.silu(causal_dwconv(xbc, conv_w, conv_b)).astype(jnp.float32)
    g_n = SSD_GROUPS * SSD_STATE
    nc = t_len // SSD_CHUNK
    r_h = SSD_HEADS // SSD_GROUPS
    x = xbc[..., :SSD_D_INNER].reshape(bsz, nc, SSD_CHUNK, SSD_GROUPS, r_h, SSD_HEAD_DIM)
    bm = xbc[..., SSD_D_INNER:SSD_D_INNER + g_n].reshape(bsz, nc, SSD_CHUNK, SSD_GROUPS, SSD_STATE)
    cm = xbc[..., SSD_D_INNER + g_n:].reshape(bsz, nc, SSD_CHUNK, SSD_GROUPS, SSD_STATE)
    dt = jax.nn.softplus(dt.astype(jnp.float32) + dt_bias.astype(jnp.float32))
    dt = dt.reshape(bsz, nc, SSD_CHUNK, SSD_GROUPS, r_h)
    a = -jnp.exp(a_log.astype(jnp.float32)).reshape(SSD_GROUPS, r_h)
    a_cum = jnp.cumsum(dt * a, axis=2)
    xdt = x * dt[..., None]
    seg = a_cum[:, :, :, None] - a_cum[:, :, None]
    causal = jnp.tril(jnp.ones((SSD_CHUNK, SSD_CHUNK), dtype=bool))[:, :, None, None]
    l_mat = jnp.exp(jnp.where(causal, seg, -jnp.inf))
    cb = jnp.einsum("bclgn,bcsgn->bclsg", cm, bm)
    y_diag = jnp.einsum("bclsg,bclsgr,bcsgrp->bclgrp", cb, l_mat, xdt)
    decay_s = jnp.exp(a_cum[:, :, -1:] - a_cum)
    states = jnp.einsum("bcsgn,bcsgr,bcsgrp->bcgrpn", bm, decay_s, xdt)
    chunk_decay = jnp.exp(a_cum[:, :, -1])

    def step(h, inp):
        s_c, dec = inp
        return h * dec[..., None, None] + s_c, h

    h0 = jnp.zeros((bsz, SSD_GROUPS, r_h, SSD_HEAD_DIM, SSD_STATE), jnp.float32)
    _, h_prev = lax.scan(step, h0, (jnp.moveaxis(states, 1, 0), jnp.moveaxis(chunk_decay, 1, 0)))
    h_prev = jnp.moveaxis(h_prev, 0, 1)
    y_off = jnp.einsum("bclgn,bcgrpn,bclgr->bclgrp", cm, h_prev, jnp.exp(a_cum))
    y = y_diag + y_off + x * d_skip.astype(jnp.float32).reshape(SSD_GROUPS, r_h)[:, :, None]
    y = y.reshape(bsz, t_len, SSD_D_INNER) * jax.nn.silu(z.astype(jnp.float32))
    return rms_norm(y, norm_g).astype(zxbcdt.dtype)


def pool_mixer(u, w_grp, scale):
    bsz, t_len, _ = u.shape
    uf = u.astype(jnp.float32).reshape(bsz, t_len, POOL_GROUPS, POOL_GROUP_DIM)
    cs = jnp.cumsum(uf, axis=1)
    pos = jnp.arange(1, t_len + 1, dtype=jnp.float32)
    outs = []
    for g, w in enumerate(POOL_WINDOWS):
        c_g = cs[:, :, g]
        lo = jnp.pad(c_g[:, :t_len - w], ((0, 0), (w, 0), (0, 0)))
        cnt = jnp.minimum(pos, float(w))[None, :, None]
        outs.append((c_g - lo) / cnt - uf[:, :, g])
    pooled = jnp.stack(outs, axis=2)
    mixed = jnp.einsum("btgc,gcd->btgd", pooled, w_grp.astype(jnp.float32))
    return (mixed.reshape(bsz, t_len, POOL_DIM) * scale.astype(jnp.float32)).astype(u.dtype)


def conformer_conv(v, gate, dw_w, dw_b, ln_g, ln_b):
    h = v * jax.nn.sigmoid(gate)
    h = causal_dwconv(h, dw_w, dw_b)
    h = layer_norm(h, ln_g, ln_b)
    return jax.nn.silu(h)


def _lin_combine(e1, e2):
    a1, b1 = e1
    a2, b2 = e2
    return a1 * a2, a2 * b1 + b2


def rglru_mixer(xr, gate, conv_w, conv_b, wa, ba, wx, bx, lam):
    bsz, t_len, _ = xr.shape
    xr = causal_dwconv(xr, conv_w, conv_b)
    xh = xr.reshape(bsz, t_len, LRU_HEADS, LRU_HEAD_DIM)
    r = jax.nn.sigmoid(jnp.einsum("bthi,hij->bthj", xh, wa).reshape(bsz, t_len, LRU_DIM) + ba)
    i = jax.nn.sigmoid(jnp.einsum("bthi,hij->bthj", xh, wx).reshape(bsz, t_len, LRU_DIM) + bx)
    log_a = -LRU_C * r.astype(jnp.float32) * jax.nn.softplus(-lam.astype(jnp.float32))
    a = jnp.exp(log_a)
    b = jnp.sqrt(-jnp.expm1(2.0 * log_a)) * (i * xr).astype(jnp.float32)
    _, h = lax.associative_scan(_lin_combine, (a, b), axis=1)
    return h.astype(xr.dtype) * jax.nn.gelu(gate)


def setup_inputs(seed: int = 0) -> dict:
    key = jax.random.key(seed)
    ks = iter(jax.random.split(key, 48))

    def nrm(shape, scale):
        return jax.random.normal(next(ks), shape, jnp.float32) * scale

    def unif(shape, lo, hi):
        return jax.random.uniform(next(ks), shape, jnp.float32, minval=lo, maxval=hi)

    x = nrm((BATCH, SEQ, D_MODEL), 1.0)
    c = nrm((BATCH, D_MODEL), 1.0)
    ada_w = nrm((DEPTH, D_MODEL, N_MOD * D_MODEL), 0.1 * D_MODEL ** -0.5)
    ada_b = nrm((DEPTH, N_MOD * D_MODEL), 0.01)
    ln_g = 1.0 + nrm((DEPTH, 3, D_MODEL), 0.01)
    ln_b = nrm((DEPTH, 3, D_MODEL), 0.01)
    ffn_w_in = nrm((DEPTH, 2, D_MODEL, 2 * D_FF), D_MODEL ** -0.5)
    ffn_w_out = nrm((DEPTH, 2, D_FF, D_MODEL), DN_BETA * D_FF ** -0.5)
    ev_w_in = nrm((N_EVEN, D_MODEL, EVEN_IN), D_MODEL ** -0.5)
    ssd_conv_w = nrm((N_EVEN, SSD_CONV, SSD_CONV_DIM), SSD_CONV ** -0.5)
    ssd_conv_b = nrm((N_EVEN, SSD_CONV_DIM), 0.01)
    dt0 = jnp.exp(unif((N_EVEN, SSD_HEADS), math.log(1e-3), math.log(1e-1)))
    ssd_dt_bias = dt0 + jnp.log(-jnp.expm1(-dt0))
    ssd_a_log = jnp.log(unif((N_EVEN, SSD_HEADS), 1.0, 16.0))
    ssd_d = 1.0 + nrm((N_EVEN, SSD_HEADS), 0.01)
    ssd_norm_g = 1.0 + nrm((N_EVEN, SSD_D_INNER), 0.01)
    pool_w = nrm((N_EVEN, POOL_GROUPS, POOL_GROUP_DIM, POOL_GROUP_DIM), POOL_GROUP_DIM ** -0.5)
    pool_scale = 1.0 + nrm((N_EVEN, POOL_DIM), 0.01)
    ev_w_out = nrm((N_EVEN, EVEN_MIX, D_MODEL), DN_BETA * EVEN_MIX ** -0.5)
    od_w_in = nrm((N_ODD, D_MODEL, ODD_IN), D_MODEL ** -0.5)
    conf_dw_w = nrm((N_ODD, CONF_KERNEL, CONF_DIM), CONF_KERNEL ** -0.5)
    conf_dw_b = nrm((N_ODD, CONF_DIM), 0.01)
    conf_ln_g = 1.0 + nrm((N_ODD, CONF_DIM), 0.01)
    conf_ln_b = nrm((N_ODD, CONF_DIM), 0.01)
    lru_conv_w = nrm((N_ODD, LRU_CONV, LRU_DIM), LRU_CONV ** -0.5)
    lru_conv_b = nrm((N_ODD, LRU_DIM), 0.01)
    lru_wa = nrm((N_ODD, LRU_HEADS, LRU_HEAD_DIM, LRU_HEAD_DIM), LRU_HEAD_DIM ** -0.5)
    lru_ba = nrm((N_ODD, LRU_DIM), 0.01)
    lru_wx = nrm((N_ODD, LRU_HEADS, LRU_HEAD_DIM, LRU_HEAD_DIM), LRU_HEAD_DIM ** -0.5)
    lru_bx = nrm((N_ODD, LRU_DIM), 0.01)
    a_c = unif((N_ODD, LRU_DIM), 0.9, 0.999)
    a_base = a_c ** (1.0 / LRU_C)
    lru_lambda = jnp.log(a_base) - jnp.log1p(-a_base)
    od_w_out = nrm((N_ODD, ODD_MIX, D_MODEL), DN_BETA * ODD_MIX ** -0.5)
    return {
        "x": x, "c": c, "ada_w": ada_w, "ada_b": ada_b, "ln_g": ln_g, "ln_b": ln_b,
        "ffn_w_in": ffn_w_in, "ffn_w_out": ffn_w_out,
        "ev_w_in": ev_w_in, "ssd_conv_w": ssd_conv_w, "ssd_conv_b": ssd_conv_b,
        "ssd_dt_bias": ssd_dt_bias, "ssd_a_log": ssd_a_log, "ssd_d": ssd_d, "ssd_norm_g": ssd_norm_g,
        "pool_w": pool_w, "pool_scale": pool_scale, "ev_w_out": ev_w_out,
        "od_w_in": od_w_in, "conf_dw_w": conf_dw_w, "conf_dw_b": conf_dw_b,
        "conf_ln_g": conf_ln_g, "conf_ln_b": conf_ln_b,
        "lru_conv_w": lru_conv_w, "lru_conv_b": lru_conv_b, "lru_wa": lru_wa, "lru_ba": lru_ba,
        "lru_wx": lru_wx, "lru_bx": lru_bx, "lru_lambda": lru_lambda, "od_w_out": od_w_out,
    }


def reference(x, c, ada_w, ada_b, ln_g, ln_b, ffn_w_in, ffn_w_out,
              ev_w_in, ssd_conv_w, ssd_conv_b, ssd_dt_bias, ssd_a_log, ssd_d, ssd_norm_g,
              pool_w, pool_scale, ev_w_out,
              od_w_in, conf_dw_w, conf_dw_b, conf_ln_g, conf_ln_b,
              lru_conv_w, lru_conv_b, lru_wa, lru_ba, lru_wx, lru_bx, lru_lambda, od_w_out):
    cond = jax.nn.silu(c)
    for layer in range(DEPTH):
        mod = cond @ ada_w[layer] + ada_b[layer]
        sh1, sc1, g1, sh2, sc2, g2, sh3, sc3, g3 = jnp.split(mod, N_MOD, axis=-1)
        y = swiglu(modulate(x, sh1, sc1), ffn_w_in[layer, 0], ffn_w_out[layer, 0])
        x = post_norm(x, 0.5 * (1.0 + g1[:, None, :]) * y, ln_g[layer, 0], ln_b[layer, 0])
        h = modulate(x, sh2, sc2)
        if layer % 2 == 0:
            e = layer // 2
            proj = jnp.einsum("btd,de->bte", h, ev_w_in[e])
            y_a = ssd_mixer(proj[..., :SSD_IN], ssd_conv_w[e], ssd_conv_b[e], ssd_dt_bias[e],
                            ssd_a_log[e], ssd_d[e], ssd_norm_g[e])
            y_b = pool_mixer(proj[..., SSD_IN:], pool_w[e], pool_scale[e])
            y = jnp.einsum("bte,ed->btd", jnp.concatenate([y_a, y_b], axis=-1), ev_w_out[e])
        else:
            o = layer // 2
            proj = jnp.einsum("btd,de->bte", h, od_w_in[o])
            v = proj[..., :CONF_DIM]
            gt = proj[..., CONF_DIM:2 * CONF_DIM]
            xr = proj[..., 2 * CONF_DIM:2 * CONF_DIM + LRU_DIM]
            gr = proj[..., 2 * CONF_DIM + LRU_DIM:]
            y_c = conformer_conv(v, gt, conf_dw_w[o], conf_dw_b[o], conf_ln_g[o], conf_ln_b[o])
            y_d = rglru_mixer(xr, gr, lru_conv_w[o], lru_conv_b[o], lru_wa[o], lru_ba[o],
                              lru_wx[o], lru_bx[o], lru_lambda[o])
            y = jnp.einsum("bte,ed->btd", jnp.concatenate([y_c, y_d], axis=-1), od_w_out[o])
        x = post_norm(x, (1.0 + g2[:, None, :]) * y, ln_g[layer, 1], ln_b[layer, 1])
        y = swiglu(modulate(x, sh3, sc3), ffn_w_in[layer, 1], ffn_w_out[layer, 1])
        x = post_norm(x, 0.5 * (1.0 + g3[:, None, :]) * y, ln_g[layer, 2], ln_b[layer, 2])
    return x
```

```python
import numpy as np
from contextlib import ExitStack
import concourse.bass as bass
import concourse.mybir as mybir
from concourse.bass_utils import run_bass_kernel_spmd

F32 = mybir.dt.float32
BF16 = mybir.dt.bfloat16
AF = mybir.ActivationFunctionType
ALU = mybir.AluOpType

D = 1024
DFF = 2816
NJ = 22
L_FULL = 4
ALPHA = float((2.0 * 4) ** 0.25)
EPS = 1e-5
EV_IN = 3088
OD_IN = 3072
POOL_W = (2, 4, 8, 16)

ENGS = ("pe", "act", "dve", "pool", "sp")
N_DMA_SEMS = 8


class Buf:
    __slots__ = ("name", "last_w", "readers")

    def __init__(self, name=""):
        self.name = name
        self.last_w = None
        self.readers = []


class Op:
    __slots__ = ("eng", "fn", "dma", "deps", "signal", "ordinal", "waits", "dsem", "dval", "prewait")

    def __init__(self, eng, fn, dma):
        self.eng = eng
        self.fn = fn
        self.dma = dma
        self.deps = []
        self.signal = False
        self.ordinal = 0
        self.waits = []
        self.dsem = None
        self.dval = 0
        self.prewait = None


class Prog:
    def __init__(self, nc):
        self.nc = nc
        self.ops = {e: [] for e in ENGS}
        self.all = []

    def add(self, eng, fn, reads=(), writes=(), dma=False):
        op = Op(eng, fn, dma)
        deps = op.deps
        for b in reads:
            if b.last_w is not None:
                deps.append((b.last_w, True))
        for b in writes:
            if b.last_w is not None:
                deps.append((b.last_w, False))
            for r in b.readers:
                deps.append((r, False))
        for b in reads:
            b.readers.append(op)
        for b in writes:
            b.last_w = op
            b.readers = []
        self.ops[eng].append(op)
        self.all.append(op)
        return op

    def finalize(self):
        for op in self.all:
            need = []
            for (p, raw) in op.deps:
                if p is op:
                    continue
                if p.dma or p.eng != op.eng:
                    need.append(p)
                elif op.dma or op.eng != "pe":
                    need.append(p)
            op.deps = need
            for p in need:
                if not p.dma:
                    p.signal = True
        for e in ENGS:
            n = 0
            for op in self.ops[e]:
                if (not op.dma) and op.signal:
                    n += 1
                    op.ordinal = n
        for e in ENGS:
            k = 0
            cnt = [0] * N_DMA_SEMS
            for op in self.ops[e]:
                if not op.dma:
                    continue
                s = k % N_DMA_SEMS
                k += 1
                if cnt[s] > 0:
                    op.prewait = (("dma", e, s), cnt[s] * 16)
                cnt[s] += 1
                op.dsem = ("dma", e, s)
                op.dval = cnt[s] * 16
        for e in ENGS:
            sn = {}
            for op in self.ops[e]:
                w = {}
                if op.prewait is not None:
                    k, v = op.prewait
                    w[k] = v
                for p in op.deps:
                    if p.dma:
                        k, v = p.dsem, p.dval
                    else:
                        k, v = ("eng", p.eng), p.ordinal
                    if w.get(k, 0) < v:
                        w[k] = v
                for k, v in w.items():
                    if sn.get(k, 0) >= v:
                        continue
                    sn[k] = v
                    op.waits.append((k, v))

    def emit(self):
        nc = self.nc
        self.finalize()
        with ExitStack() as st:
            sems = {}
            for e in ENGS:
                if any(op.signal for op in self.ops[e]):
                    sems[("eng", e)] = st.enter_context(nc.semaphore("s_" + e))
                if any(op.dma for op in self.ops[e]):
                    for s in range(N_DMA_SEMS):
                        sems[("dma", e, s)] = st.enter_context(nc.semaphore("d_%s%d" % (e, s)))
            block = st.enter_context(nc.Block())

            def run(engname):
                def body(eng):
                    for op in self.ops[engname]:
                        for (k, v) in op.waits:
                            eng.wait_ge(sems[k], v)
                        ins = op.fn(eng)
                        if op.dma:
                            ins.then_inc(sems[op.dsem], 16)
                        elif op.signal:
                            ins.then_inc(sems[("eng", engname)], 1)
                    last = {}
                    for op in self.ops[engname]:
                        if op.dma:
                            last[op.dsem] = op.dval
                    for k, v in last.items():
                        eng.wait_ge(sems[k], v)
                return body

            block.sync(run("sp"))
            block.scalar(run("act"))
            block.vector(run("dve"))
            block.gpsimd(run("pool"))
            block.tensor(run("pe"))


class Tl:
    __slots__ = ("ap", "bufs", "off", "esz")

    def __init__(self, ap, bufs, off=0, esz=4):
        self.ap = ap
        self.bufs = bufs
        self.off = off
        self.esz = esz

    def __getitem__(self, k):
        return self.ap[k]


PAGE = 2048


class Builder:
    def __init__(self, seq, layers, n_sub=3):
        self.SEQ = seq
        self.layers = layers
        self.n_sub = n_sub
        self.NT = 2 * seq // 128
        self.TPS = seq // 128
        self.nc = bass.Bass("TRN2", target_bir_lowering=False)
        self.P = Prog(self.nc)
        self.st = ExitStack()
        self.rr = 0

    def op(self, eng, fn, r=(), w=(), dma=False):
        rb = []
        for t in r:
            rb.extend(t.bufs)
        wb = []
        for t in w:
            wb.extend(t.bufs)
        return self.P.add(eng, fn, rb, wb, dma)

    def sb(self, name, shape, dt):
        t = self.st.enter_context(self.nc.sbuf_tensor(name, shape, dt))
        return Tl(t, [Buf(name)])

    def ps(self, name):
        t = self.st.enter_context(self.nc.psum_tensor(name, [128, 1024], F32))
        return (Tl(t[:, 0:512], [Buf(name + "a")]), Tl(t[:, 512:1024], [Buf(name + "b")]),
                Tl(t[:, :], None))

    def dram(self, name, shape, dt, kind="Internal"):
        return self.nc.dram_tensor(name, shape, dt, kind=kind).ap()

    def av(self, off, shape, dt):
        n = 1
        for s in shape[1:]:
            n *= s
        esz = 4 if dt == F32 else 2
        nbytes = n * esz
        assert off % 4 == 0 and nbytes % 4 == 0
        assert off + nbytes <= self.arena_bytes, (off, nbytes, self.arena_bytes)
        ap = self.arena[0:shape[0], off // 4:(off + nbytes) // 4]
        if dt != F32:
            ap = ap.bitcast(dt)
        if len(shape) == 3:
            ap = ap.rearrange("p (a b) -> p a b", b=shape[2])
        elif len(shape) == 4:
            ap = ap.rearrange("p (a b c) -> p a b c", b=shape[2], c=shape[3])
        bufs = self.pages[off // PAGE:(off + nbytes - 1) // PAGE + 1]
        return Tl(ap, bufs, off, esz)

    def nar(self, tl, e0, n, ap):
        o0 = tl.off + e0 * tl.esz
        o1 = tl.off + (e0 + n) * tl.esz - 1
        return Tl(ap, self.pages[o0 // PAGE:o1 // PAGE + 1], o0, tl.esz)

    class Alloc:
        def __init__(self, b):
            self.b = b
            self.off = 0

        def __call__(self, shape, dt):
            n = 1
            for s in shape[1:]:
                n *= s
            nbytes = n * (4 if dt == F32 else 2)
            nbytes = (nbytes + 3) // 4 * 4
            t = self.b.av(self.off, shape, dt)
            self.off += nbytes
            return t

        def align(self):
            self.off = (self.off + PAGE - 1) // PAGE * PAGE

    def sub(self, tl, ap):
        return Tl(ap, tl.bufs)

    def build(self):
        nc = self.nc
        SEQ, NT = self.SEQ, self.NT
        L = len(self.layers)
        NE = (L_FULL + 1) // 2
        NO = L_FULL // 2
        dr = {}
        dr["x"] = self.dram("x", [NT * 128, D], F32, "ExternalInput")
        dr["cT"] = self.dram("cT", [128, 8, 2], F32, "ExternalInput")
        dr["ada_w"] = self.dram("ada_w", [L_FULL, D, 9 * D], F32, "ExternalInput")
        dr["ada_bT"] = self.dram("ada_bT", [L_FULL, 128, 72], F32, "ExternalInput")
        dr["ada_b"] = self.dram("ada_b", [L_FULL, 9 * D], F32, "ExternalInput")
        dr["ln_g"] = self.dram("ln_g", [L_FULL, 3, D], F32, "ExternalInput")
        dr["ln_b"] = self.dram("ln_b", [L_FULL, 3, D], F32, "ExternalInput")
        dr["ffn_w_in"] = self.dram("ffn_w_in", [L_FULL, 2, D, 2 * DFF], F32, "ExternalInput")
        dr["ffn_w_out"] = self.dram("ffn_w_out", [L_FULL, 2, DFF, D], F32, "ExternalInput")
        dr["ev_w_in"] = self.dram("ev_w_in", [NE, D, EV_IN], F32, "ExternalInput")
        dr["ssd_conv_wT"] = self.dram("ssd_conv_wT", [NE, 128, 12, 4], F32, "ExternalInput")
        dr["ssd_conv_b"] = self.dram("ssd_conv_b", [NE, 1536], F32, "ExternalInput")
        dr["ssd_dt_bias"] = self.dram("ssd_dt_bias", [NE, 16], F32, "ExternalInput")
        dr["ssd_a_log"] = self.dram("ssd_a_log", [NE, 16], F32, "ExternalInput")
        dr["ssd_d"] = self.dram("ssd_d", [NE, 16], F32, "ExternalInput")
        dr["ssd_norm_gT"] = self.dram("ssd_norm_gT", [NE, 128, 8], F32, "ExternalInput")
        dr["pool_wT"] = self.dram("pool_wT", [NE, 128, 4, 128], F32, "ExternalInput")
        dr["pool_scale"] = self.dram("pool_scale", [NE, 512], F32, "ExternalInput")
        dr["ev_w_out"] = self.dram("ev_w_out", [NE, 1536, D], F32, "ExternalInput")
        dr["od_w_in"] = self.dram("od_w_in", [NO, D, OD_IN], F32, "ExternalInput")
        dr["od_w_out"] = self.dram("od_w_out", [NO, 1536, D], F32, "ExternalInput")
        dr["conf_dw_wT"] = self.dram("conf_dw_wT", [NO, 128, 4, 31], F32, "ExternalInput")
        dr["conf_dw_b"] = self.dram("conf_dw_b", [NO, 512], F32, "ExternalInput")
        dr["conf_ln_gT"] = self.dram("conf_ln_gT", [NO, 128, 4], F32, "ExternalInput")
        dr["conf_ln_bT"] = self.dram("conf_ln_bT", [NO, 128, 4], F32, "ExternalInput")
        dr["lru_conv_wT"] = self.dram("lru_conv_wT", [NO, 128, 8, 4], F32, "ExternalInput")
        dr["lru_conv_b"] = self.dram("lru_conv_b", [NO, D], F32, "ExternalInput")
        dr["lru_waT"] = self.dram("lru_waT", [NO, 128, 8, 128], F32, "ExternalInput")
        dr["lru_ba"] = self.dram("lru_ba", [NO, D], F32, "ExternalInput")
        dr["lru_wxT"] = self.dram("lru_wxT", [NO, 128, 8, 128], F32, "ExternalInput")
        dr["lru_bx"] = self.dram("lru_bx", [NO, D], F32, "ExternalInput")
        dr["lru_lamT"] = self.dram("lru_lamT", [NO, 128, 8], F32, "ExternalInput")
        dr["out"] = self.dram("out", [NT * 128, D], F32, "ExternalOutput")
        dr["xs"] = self.dram("xs", [NT * 128, D], F32)
        dr["gates"] = self.dram("gates", [L_FULL, 3, 2, D], F32)
        dr["yT"] = self.dram("yT", [NT, 128, 1536], BF16)
        self.ybuf_g = [Buf("yT%d" % t) for t in range(NT)]
        self.dr = dr
        self.xbuf = [Buf("xd%d" % t) for t in range(NT)]
        self.gates_buf = Buf("gates")

        self.ident = self.sb("ident", [128, 128], F32)
        self.ones = self.sb("ones", [128, 128], F32)
        self.tri = self.sb("tri", [128, 128], F32)
        self.ustr = self.sb("ustr", [128, 128], F32)
        self.m05 = self.sb("m05", [128, 128], F32)
        self.onesb = self.sb("onesb", [128, 128], BF16)
        self.ones1 = self.sb("ones1", [1, 128], F32)
        self.condT = self.sb("condT", [128, 8, 2], F32)
        self.modT = self.sb("modT", [128, L_FULL, 6, 8, 2], F32)
        self.adabT = self.sb("adabT", [128, L_FULL, 72], F32)
        self.xl = [self.sb("xl%d" % i, [128, D], F32) for i in range(4)]
        self.xe = [self.sb("xe%d" % i, [128, D], F32) for i in range(2)]
        self.zt = [self.sb("zt%d" % i, [128, D], F32) for i in range(2)]
        self.gate_row = self.sb("gate_row", [128, D], F32)
        self.lng_row = self.sb("lng_row", [128, D], F32)
        self.lnb_row = self.sb("lnb_row", [128, D], F32)
        self.bst = [self.sb("bst%d" % i, [128, 2, 6], F32) for i in range(2)]
        self.mv = [self.sb("mv%d" % i, [128, 2], F32) for i in range(2)]
        self.rstd = [self.sb("rstd%d" % i, [128, 2], F32) for i in range(2)]
        self.pst = [self.ps("ps%d" % i) for i in range(4)]
        self.banks = []
        for p in self.pst:
            self.banks.append(p[0])
            self.banks.append(p[1])
        rem = int(nc.sbuf_bytes_remaining) - 256
        self.arena_bytes = rem // PAGE * PAGE
        t = self.st.enter_context(nc.sbuf_tensor("arena", [128, self.arena_bytes // 4], F32))
        self.arena = t
        self.pages = [Buf("pg%d" % i) for i in range(self.arena_bytes // PAGE)]

        self.setup_consts()
        for _ in self.mods_gen(self.layers[0], self.Alloc(self)):
            pass
        src = "x"
        nsub_total = L * self.n_sub
        k = 0
        for li, l in enumerate(self.layers):
            for s in range(self.n_sub):
                k += 1
                dst = "out" if k == nsub_total else "xs"
                if s == 0:
                    self.ffn(l, 0, src, dst)
                elif s == 1:
                    if l % 2 == 0:
                        self.even_mixer(l, src, dst)
                    else:
                        self.odd_mixer(l, src, dst)
                else:
                    self.ffn(l, 1, src, dst)
                src = dst
        self.P.emit()
        self.st.close()
        return nc

    def setup_consts(self):
        op = self.op
        ident, ones, tri, ustr, m05 = self.ident, self.ones, self.tri, self.ustr, self.m05
        op("pool", lambda e: e.memset(ones.ap[:], 1.0), w=[ones])
        op("pool", lambda e: e.memset(m05.ap[:], -0.5), w=[m05])
        op("pool", lambda e: e.memset(self.onesb.ap[:], 1.0), w=[self.onesb])
        op("pool", lambda e: e.memset(self.ones1.ap[:], 1.0), w=[self.ones1])
        op("pool", lambda e: e.affine_select(out=ident.ap[:], in_=ones.ap[:], pattern=[[-1, 128]],
                                             compare_op=ALU.is_equal, fill=0.0, base=0, channel_multiplier=1),
           r=[ones], w=[ident])
        op("pool", lambda e: e.affine_select(out=tri.ap[:], in_=ones.ap[:], pattern=[[1, 128]],
                                             compare_op=ALU.is_ge, fill=0.0, base=0, channel_multiplier=-1),
           r=[ones], w=[tri])
        op("pool", lambda e: e.affine_select(out=ustr.ap[:], in_=ones.ap[:], pattern=[[-1, 128]],
                                             compare_op=ALU.is_gt, fill=0.0, base=0, channel_multiplier=1),
           r=[ones], w=[ustr])
        op("sp", lambda e: e.dma_start(out=self.condT.ap[:], in_=self.dr["cT"][:, :, :]), w=[self.condT], dma=True)
        op("act", lambda e: e.activation(out=self.condT.ap[:], in_=self.condT.ap[:], func=AF.Silu),
           r=[self.condT], w=[self.condT])
        op("sp", lambda e: e.dma_start(out=self.adabT.ap[:], in_=self.dr["ada_bT"].rearrange("l p c -> p l c")),
           w=[self.adabT], dma=True)

    def nbank(self):
        b = self.banks[self.rr % 8]
        self.rr += 1
        return b

    def mods_gen(self, l, al):
        op = self.op
        stage = [al([128, 8, 512], F32) for _ in range(2)]
        al.align()
        grow = [al([2, 512], F32) for _ in range(2)]
        brow = [al([2, 512], F32) for _ in range(2)]
        condT = self.condT
        k = 0
        if True:
            for cg in range(18):
                stg = stage[k % 2]
                k += 1
                v, hf = cg // 2, cg % 2
                for kc in range(8):
                    op("sp", lambda e, stg=stg, kc=kc, l=l, cg=cg: e.dma_start(
                        out=stg.ap[:, kc, :], in_=self.dr["ada_w"][l, kc * 128:(kc + 1) * 128, cg * 512:(cg + 1) * 512]),
                        w=[stg], dma=True)
                if v % 3 == 2:
                    j = v // 3
                    bk = self.nbank()
                    for kc in range(8):
                        op("pe", lambda e, bk=bk, stg=stg, kc=kc: e.matmul(
                            bk.ap[0:2, :], lhsT=condT.ap[:, kc, :], rhs=stg.ap[:, kc, :],
                            start=(kc == 0), stop=(kc == 7)), r=[condT, stg], w=[bk])
                    g_, b_ = grow[hf], brow[hf]
                    for bb in range(2):
                        op("sp", lambda e, b_=b_, l=l, cg=cg, bb=bb: e.dma_start(
                            out=b_.ap[bb:bb + 1, :], in_=self.dr["ada_b"][l:l + 1, cg * 512:(cg + 1) * 512]),
                            w=[b_], dma=True)
                    op("dve", lambda e, g_=g_, bk=bk, b_=b_: e.tensor_tensor(
                        out=g_.ap[:, :], in0=bk.ap[0:2, :], in1=b_.ap[:, :], op=ALU.add), r=[bk, b_], w=[g_])
                    mul = 1.0 if j == 1 else 0.5
                    op("dve", lambda e, g_=g_, mul=mul: e.tensor_scalar(
                        out=g_.ap[:, :], in0=g_.ap[:, :], scalar1=1.0, scalar2=mul, op0=ALU.add, op1=ALU.mult),
                        r=[g_], w=[g_])
                    op("sp", lambda e, g_=g_, l=l, j=j, hf=hf: e.dma_start(
                        out=self.dr["gates"][l, j, :, hf * 512:(hf + 1) * 512], in_=g_.ap[:, :]),
                        r=[g_], w=[Tl(None, [self.gates_buf])], dma=True)
                else:
                    j = v // 3
                    vi = j * 2 + (v % 3)
                    bk = self.nbank()
                    for fc in range(4):
                        for kc in range(8):
                            op("pe", lambda e, bk=bk, stg=stg, kc=kc, fc=fc: e.matmul(
                                bk.ap[:, fc * 2:fc * 2 + 2], lhsT=stg.ap[:, kc, fc * 128:(fc + 1) * 128],
                                rhs=condT.ap[:, kc, :], start=(kc == 0), stop=(kc == 7)),
                                r=[condT, stg], w=[bk])
                    for fc in range(4):
                        ch = hf * 4 + fc
                        addc = 1.0 if (v % 3) == 1 else 0.0
                        op("dve", lambda e, bk=bk, l=l, vi=vi, ch=ch, fc=fc, cg=cg, addc=addc: e.tensor_scalar(
                            out=self.modT.ap[:, l, vi, ch, :], in0=bk.ap[:, fc * 2:fc * 2 + 2],
                            scalar1=self.adabT.ap[:, l, cg * 4 + fc:cg * 4 + fc + 1], scalar2=addc,
                            op0=ALU.add, op1=ALU.add), r=[bk, self.adabT], w=[self.modT])
                yield

    def load_rows(self, l, j, b):
        op = self.op
        op("sp", lambda e: e.dma_start(out=self.gate_row.ap[:, :],
                                       in_=self.dr["gates"][l, j, b, :].partition_broadcast(128)),
           r=[Tl(None, [self.gates_buf])], w=[self.gate_row], dma=True)
        if b == 0:
            op("sp", lambda e: e.dma_start(out=self.lng_row.ap[:, :],
                                           in_=self.dr["ln_g"][l, j, :].partition_broadcast(128)),
               w=[self.lng_row], dma=True)
            op("sp", lambda e: e.dma_start(out=self.lnb_row.ap[:, :],
                                           in_=self.dr["ln_b"][l, j, :].partition_broadcast(128)),
               w=[self.lnb_row], dma=True)

    def load_x(self, t, src, slot):
        xl = self.xl[slot]
        self.op("sp", lambda e: e.dma_start(out=xl.ap[:, :], in_=self.dr[src][t * 128:(t + 1) * 128, :]),
                r=[Tl(None, [self.xbuf[t]])], w=[xl], dma=True)
        return xl

    def transpose_mod(self, xl, l, j, b, hT, col0):
        op = self.op
        for half in range(2):
            bk = self.nbank()
            for q in range(4):
                kc = half * 4 + q
                op("pe", lambda e, bk=bk, q=q, kc=kc: e.transpose(
                    out=bk.ap[:, q * 128:(q + 1) * 128], in_=xl.ap[:, kc * 128:(kc + 1) * 128],
                    identity=self.ident.ap[:, :]), r=[xl, self.ident], w=[bk])
            for q in range(4):
                kc = half * 4 + q
                op("act", lambda e, bk=bk, q=q, kc=kc: e.activation(
                    out=hT.ap[:, kc, col0:col0 + 128], in_=bk.ap[:, q * 128:(q + 1) * 128], func=AF.Identity,
                    bias=self.modT.ap[:, l, 2 * j, kc, b:b + 1], scale=self.modT.ap[:, l, 2 * j + 1, kc, b:b + 1]),
                    r=[bk, self.modT], w=[hT])

    def epilogue(self, t, l, j, src, dst, ybanks, par):
        op = self.op
        xe, zt, bst, mv, rs = self.xe[par], self.zt[par], self.bst[par], self.mv[par], self.rstd[par]
        xg = Tl(None, [self.xbuf[t]])
        op("sp", lambda e: e.dma_start(out=xe.ap[:, :], in_=self.dr[src][t * 128:(t + 1) * 128, :]),
           r=[xg], w=[xe], dma=True)
        for h in range(2):
            op("dve", lambda e, h=h: e.tensor_tensor(out=zt.ap[:, h * 512:(h + 1) * 512], in0=ybanks[h].ap[:, :],
                                                     in1=self.gate_row.ap[:, h * 512:(h + 1) * 512], op=ALU.mult),
               r=[ybanks[h], self.gate_row], w=[zt])
        op("dve", lambda e: e.scalar_tensor_tensor(out=zt.ap[:, :], in0=xe.ap[:, :], scalar=ALPHA, in1=zt.ap[:, :],
                                                   op0=ALU.mult, op1=ALU.add), r=[xe, zt], w=[zt])
        for h in range(2):
            op("dve", lambda e, h=h: e.bn_stats(out=bst.ap[:, h, :], in_=zt.ap[:, h * 512:(h + 1) * 512]),
               r=[zt], w=[bst])
        op("dve", lambda e: e.bn_aggr(out=mv.ap[:, :], in_=bst.ap[:, :, :].rearrange("p a b -> p (a b)")),
           r=[bst], w=[mv])
        op("pool", lambda e: e.tensor_scalar(out=rs.ap[:, 0:1], in0=mv.ap[:, 1:2], scalar1=EPS, scalar2=None,
                                             op0=ALU.add), r=[mv], w=[rs])
        op("pool", lambda e: e.tensor_tensor(out=rs.ap[:, 0:1], in0=rs.ap[:, 0:1], in1=self.m05.ap[:, 0:1],
                                             op=ALU.pow), r=[rs, self.m05], w=[rs])
        op("pool", lambda e: e.tensor_tensor(out=rs.ap[:, 1:2], in0=mv.ap[:, 0:1], in1=rs.ap[:, 0:1],
                                             op=ALU.mult), r=[mv, rs], w=[rs])
        op("pool", lambda e: e.tensor_scalar(out=rs.ap[:, 1:2], in0=rs.ap[:, 1:2], scalar1=-1.0, scalar2=None,
                                             op0=ALU.mult), r=[rs], w=[rs])
        op("act", lambda e: e.activation(out=xe.ap[:, :], in_=zt.ap[:, :], func=AF.Identity,
                                         bias=rs.ap[:, 1:2], scale=rs.ap[:, 0:1]), r=[zt, rs], w=[xe])
        op("pool", lambda e: e.tensor_tensor(out=xe.ap[:, :], in0=xe.ap[:, :], in1=self.lng_row.ap[:, :],
                                             op=ALU.mult), r=[xe, self.lng_row], w=[xe])
        op("pool", lambda e: e.tensor_tensor(out=xe.ap[:, :], in0=xe.ap[:, :], in1=self.lnb_row.ap[:, :],
                                             op=ALU.add), r=[xe, self.lnb_row], w=[xe])
        op("sp", lambda e: e.dma_start(out=self.dr[dst][t * 128:(t + 1) * 128, :], in_=xe.ap[:, :]),
           r=[xe], w=[xg], dma=True)

    def pipeline(self, gens, extra=None, every=3):
        live = []
        n = len(gens)
        i = 0
        step = 0
        while i < n or live:
            step += 1
            if extra is not None and step % every == 0:
                try:
                    next(extra)
                except StopIteration:
                    extra = None
            if i < n:
                live.append(gens[i])
                i += 1
            nxt = []
            for g in live:
                try:
                    next(g)
                    nxt.append(g)
                except StopIteration:
                    pass
            live = nxt
        if extra is not None:
            for _ in extra:
                pass

    def ffn(self, l, k, src, dst):
        op = self.op
        j = 0 if k == 0 else 2
        G = 256
        NGT = G // 128
        NG = self.NT // NGT
        al = self.Alloc(self)
        w_in = al([128, 8, 2 * DFF], BF16)
        al.align()
        w_out = al([128, NJ, D], BF16)
        al.align()
        hT = al([128, 8, G], BF16)
        al.align()
        actT = al([128, NJ, G], BF16)
        al.align()
        sg = [al([128, G], BF16) for _ in range(2)]
        wi_d = self.dr["ffn_w_in"]
        wo_d = self.dr["ffn_w_out"]
        for q in range(6):
            wdt = 512 if q < 5 else 256
            for half in range(2):
                c0 = half * DFF + q * 512
                for kc in range(8):
                    off = (kc * 2 * DFF + c0) * 2
                    dst_t = Tl(w_in.ap[:, kc, c0:c0 + wdt], self.pages[off // PAGE:(off + wdt * 2 - 1) // PAGE + 1])
                    op("pool", lambda e, dst_t=dst_t, kc=kc, c0=c0, wdt=wdt: e.dma_start(
                        out=dst_t.ap, in_=wi_d[l, k, kc * 128:(kc + 1) * 128, c0:c0 + wdt]),
                        w=[dst_t], dma=True)
        wo_base = w_out.bufs
        for jj in range(NJ):
            for hh in range(2):
                off0 = (self.arena_off(w_out)) + (jj * D + hh * 512) * 2
                dst_t = Tl(w_out.ap[:, jj, hh * 512:(hh + 1) * 512],
                           self.pages[off0 // PAGE:(off0 + 1024 - 1) // PAGE + 1])
                op("pool", lambda e, dst_t=dst_t, jj=jj, hh=hh: e.dma_start(
                    out=dst_t.ap, in_=wo_d[l, k, jj * 128:(jj + 1) * 128, hh * 512:(hh + 1) * 512]),
                    w=[dst_t], dma=True)

        def w_in_slice(kc, c0):
            off = (kc * 2 * DFF + c0) * 2
            return Tl(w_in.ap[:, kc, c0:c0 + 128], self.pages[off // PAGE:(off + 255) // PAGE + 1])

        def w_out_slice(jj, hh):
            off0 = self.arena_off(w_out) + (jj * D + hh * 512) * 2
            return Tl(w_out.ap[:, jj, hh * 512:(hh + 1) * 512], self.pages[off0 // PAGE:(off0 + 1023) // PAGE + 1])

        def group(g):
            t0 = g * NGT
            b = (t0 * 128) // self.SEQ
            xls = []
            for i in range(NGT):
                xls.append(self.load_x(t0 + i, src, (g % 2) * NGT + i))
            yield
            for i in range(NGT):
                self.transpose_mod(xls[i], l, j, b, hT, i * 128)
            yield
            for jj in range(NJ):
                bg = self.nbank()
                bu = self.nbank()
                for kc in range(8):
                    ws = w_in_slice(kc, jj * 128)
                    op("pe", lambda e, bg=bg, ws=ws, kc=kc: e.matmul(
                        bg.ap[:, 0:G], lhsT=ws.ap, rhs=hT.ap[:, kc, :], start=(kc == 0), stop=(kc == 7)),
                        r=[ws, hT], w=[bg])
                for kc in range(8):
                    ws = w_in_slice(kc, DFF + jj * 128)
                    op("pe", lambda e, bu=bu, ws=ws, kc=kc: e.matmul(
                        bu.ap[:, 0:G], lhsT=ws.ap, rhs=hT.ap[:, kc, :], start=(kc == 0), stop=(kc == 7)),
                        r=[ws, hT], w=[bu])
                s_ = sg[jj % 2]
                op("act", lambda e, s_=s_, bg=bg: e.activation(out=s_.ap[:, :], in_=bg.ap[:, 0:G], func=AF.Silu),
                   r=[bg], w=[s_])
                at = self.sub(actT, actT.ap[:, jj, :])
                op("dve", lambda e, s_=s_, bu=bu, at=at: e.tensor_tensor(
                    out=at.ap, in0=bu.ap[:, 0:G], in1=s_.ap[:, :], op=ALU.mult), r=[bu, s_], w=[at])
            yield
            for i in range(NGT):
                t = t0 + i
                if (t * 128) % self.SEQ == 0:
                    self.load_rows(l, j, b)
                yb = [self.nbank(), self.nbank()]
                for hh in range(2):
                    for jj in range(NJ):
                        ws = w_out_slice(jj, hh)
                        op("pe", lambda e, hh=hh, jj=jj, ws=ws, yb=yb, i=i: e.matmul(
                            yb[hh].ap[:, :], lhsT=actT.ap[:, jj, i * 128:(i + 1) * 128], rhs=ws.ap,
                            start=(jj == 0), stop=(jj == NJ - 1)), r=[actT, ws], w=[yb[hh]])
                self.epilogue(t, l, j, src, dst, yb, t % 2)
            yield

        self.pipeline([group(g) for g in range(NG)])

    def arena_off(self, tl):
        return self.pages.index(tl.bufs[0]) * PAGE


    def load_w_cast(self, wt, ncols, dram_rows_fn, n_kc, splits):
        for kc in range(n_kc):
            for (c0, c1) in splits:
                d = self.nar(wt, kc * ncols + c0, c1 - c0, wt.ap[:, kc, c0:c1])
                self.op("pool", lambda e, d=d, kc=kc, c0=c0, c1=c1: e.dma_start(
                    out=d.ap, in_=dram_rows_fn(kc)[:, c0:c1]), w=[d], dma=True)

    def out_proj_epilogue(self, t, l, src, dst, w_out, chunks):
        op = self.op
        b = (t * 128) // self.SEQ
        if (t * 128) % self.SEQ == 0:
            self.load_rows(l, 1, b)
        yb = [self.nbank(), self.nbank()]
        n = len(chunks)
        for hh in range(2):
            for ci, (ct, ap) in enumerate(chunks):
                ws = self.nar(w_out, ci * D + hh * 512, 512, w_out.ap[:, ci, hh * 512:(hh + 1) * 512])
                op("pe", lambda e, hh=hh, ci=ci, ws=ws, ap=ap: e.matmul(
                    yb[hh].ap[:, :], lhsT=ap, rhs=ws.ap, start=(ci == 0), stop=(ci == n - 1)),
                    r=[ct, ws], w=[yb[hh]])
        self.epilogue(t, l, 1, src, dst, yb, t % 2)

    def even_mixer(self, l, src, dst):
        op = self.op
        e_ = l // 2
        dr = self.dr
        NT, TPS = self.NT, self.TPS
        ident, ones, tri, ustr = self.ident, self.ones, self.tri, self.ustr
        al = self.Alloc(self)
        w_in = al([128, 8, EV_IN], BF16); al.align()
        cdiag = al([128, 48, 128], BF16); al.align()
        pw = al([128, 4, 128], BF16)
        cb = al([1, 1536], BF16); al.align()
        dtb = al([128, 16], F32)
        arow = al([128, 16], F32)
        dvec = al([128, 16], F32)
        invf = al([128, 4, 128], F32)
        normgT = al([128, 8], F32); al.align()
        work0 = al.off
        cw = al([128, 48], F32)
        pwf = al([128, 4, 128], F32)
        psr = al([128, 512], F32); al.align()
        cbf = al([1, 1536], F32); al.align()
        iot = al([128, 128], F32)
        self.load_w_cast(w_in, EV_IN, lambda kc: dr["ev_w_in"][e_, kc * 128:(kc + 1) * 128, :], 8,
                         [(0, 1024), (1024, 2560), (2560, EV_IN)])
        op("sp", lambda e: e.dma_start(out=cw.ap[:, :], in_=dr["ssd_conv_wT"][e_].rearrange("p c k -> p (c k)")),
           w=[cw], dma=True)
        op("sp", lambda e: e.dma_start(out=pwf.ap[:, :, :], in_=dr["pool_wT"][e_]), w=[pwf], dma=True)
        op("sp", lambda e: e.dma_start(out=psr.ap[:, :], in_=dr["pool_scale"][e_, :].partition_broadcast(128)),
           w=[psr], dma=True)
        op("sp", lambda e: e.dma_start(out=cbf.ap[:, :], in_=dr["ssd_conv_b"][e_:e_ + 1, :]), w=[cbf], dma=True)
        op("sp", lambda e: e.dma_start(out=dtb.ap[:, :], in_=dr["ssd_dt_bias"][e_, :].partition_broadcast(128)),
           w=[dtb], dma=True)
        op("sp", lambda e: e.dma_start(out=arow.ap[:, :], in_=dr["ssd_a_log"][e_, :].partition_broadcast(128)),
           w=[arow], dma=True)
        op("sp", lambda e: e.dma_start(out=dvec.ap[:, :], in_=dr["ssd_d"][e_, :].partition_broadcast(128)),
           w=[dvec], dma=True)
        op("sp", lambda e: e.dma_start(out=normgT.ap[:, :], in_=dr["ssd_norm_gT"][e_]), w=[normgT], dma=True)
        op("dve", lambda e: e.tensor_tensor(
            out=cdiag.ap[:, :, :], in0=ident.ap[:, :].unsqueeze(1).broadcast_to([128, 48, 128]),
            in1=cw.ap[:, :].unsqueeze(2).broadcast_to([128, 48, 128]), op=ALU.mult), r=[ident, cw], w=[cdiag])
        op("dve", lambda e: e.tensor_tensor(
            out=pw.ap[:, :, :], in0=pwf.ap[:, :, :], in1=psr.ap[:, :].rearrange("p (g d) -> p g d", d=128),
            op=ALU.mult), r=[pwf, psr], w=[pw])
        op("dve", lambda e: e.tensor_copy(out=cb.ap[:, :], in_=cbf.ap[:, :]), r=[cbf], w=[cb])
        op("act", lambda e: e.activation(out=arow.ap[:, :], in_=arow.ap[:, :], func=AF.Exp), r=[arow], w=[arow])
        op("dve", lambda e: e.tensor_scalar(out=arow.ap[:, :], in0=arow.ap[:, :], scalar1=-1.0, scalar2=None,
                                            op0=ALU.mult), r=[arow], w=[arow])
        op("pool", lambda e: e.iota(iot.ap[:, :], pattern=[[1, 128]], base=1, channel_multiplier=0,
                                    allow_small_or_imprecise_dtypes=True), w=[iot])
        for g, wdw in enumerate(POOL_W):
            op("dve", lambda e, g=g, wdw=wdw: e.tensor_scalar(out=invf.ap[:, g, :], in0=iot.ap[:, :],
                                                              scalar1=float(wdw), scalar2=None, op0=ALU.min),
               r=[iot], w=[invf])
        op("dve", lambda e: e.reciprocal(out=invf.ap[:, :, :], in_=invf.ap[:, :, :]), r=[invf], w=[invf])
        al.off = work0
        hT = [al([128, 8, 128], BF16)] * 2
        cin = [al([128, 12, 132], BF16) for _ in range(2)]
        uin = [al([128, 4, 144], F32) for _ in range(2)]
        al.align()
        bcT = [al([128, 4, 128], BF16) for _ in range(2)]
        sz = [al([128, D], BF16) for _ in range(3)]
        sm = [al([128, 8, 16], F32) for _ in range(3)]
        al.align()
        xtok = al([128, D], F32)
        xdt = al([128, D], BF16)
        xdd = al([128, D], BF16)
        btok = al([128, 256], BF16)
        al.align()
        Dl = al([128, 16, 128], F32)
        xc = al([128, 12, 128], F32)
        t1 = al([128, D], F32)
        yn = al([128, D], F32)
        LT = al([128, 16, 128], BF16)
        MT = LT
        cbm = al([128, 2, 128], F32)
        al.align()
        yaT = [al([128, 8, 128], BF16)] * 2
        ybT = [al([128, 4, 128], BF16) for _ in range(2)]
        al.align()
        pta = al([128, 3, 144], F32)
        ptb = al([128, 2, 144], F32)
        prr = al([128, 4, 128], F32)
        pl = al([128, 4, 128], BF16)
        al.align()
        hst = al([128, D], F32)
        hb = al([128, D], BF16)
        ssq = al([128, 2], F32)

        def v3(ap, inner):
            return ap.rearrange("p (a b) -> p a b", b=inner)

        def bc(ap2, n):
            return ap2.unsqueeze(2).broadcast_to([128, ap2.shape[1], n])

        def tile(t):
            c = t % TPS
            b = t // TPS
            par = t % 2
            hT_, cin_, uin_, bcT_, sm_, yaT_, ybT_, sz_ = hT[par], cin[par], uin[par], bcT[par], sm[t % 3], yaT[par], ybT[par], sz[t % 3]
            cinp, uinp = cin[1 - par], uin[1 - par]
            xl = self.load_x(t, src, t % 4)
            yield
            self.transpose_mod(xl, l, 1, b, hT_, 0)
            yield
            if c == 0:
                op("pool", lambda e: e.memset(cin_.ap[:, :, 0:3], 0.0), w=[cin_])
                op("pool", lambda e: e.memset(uin_.ap[:, :, 0:15], 0.0), w=[uin_])
            else:
                op("pool", lambda e: e.tensor_copy(out=cin_.ap[:, :, 0:3], in_=cinp.ap[:, :, 128:131]),
                   r=[cinp], w=[cin_])
                op("pool", lambda e: e.tensor_copy(out=uin_.ap[:, :, 0:15], in_=uinp.ap[:, :, 128:143]),
                   r=[uinp], w=[uin_])
            for q in range(4):
                bk = self.nbank()
                for i in range(4):
                    cc = q * 4 + i
                    col = (1024 + cc * 128) if cc < 12 else (2576 + (cc - 12) * 128)
                    for kc in range(8):
                        ws = self.nar(w_in, kc * EV_IN + col, 128, w_in.ap[:, kc, col:col + 128])
                        op("pe", lambda e, bk=bk, i=i, ws=ws, kc=kc: e.matmul(
                            bk.ap[:, i * 128:(i + 1) * 128], lhsT=ws.ap, rhs=hT_.ap[:, kc, :],
                            start=(kc == 0), stop=(kc == 7)), r=[ws, hT_], w=[bk])
                if q < 3:
                    op("act", lambda e, bk=bk, q=q: e.activation(
                        out=cin_.ap[:, q * 4:(q + 1) * 4, 3:131], in_=v3(bk.ap[:, :], 128), func=AF.Identity),
                        r=[bk], w=[cin_])
                else:
                    op("dve", lambda e, bk=bk: e.tensor_copy(out=uin_.ap[:, :, 15:143], in_=v3(bk.ap[:, :], 128)),
                       r=[bk], w=[uin_])
            zb = [self.nbank(), self.nbank()]
            for hh in range(2):
                for kc in range(8):
                    ws = self.nar(w_in, kc * EV_IN + hh * 512, 512, w_in.ap[:, kc, hh * 512:(hh + 1) * 512])
                    op("pe", lambda e, hh=hh, ws=ws, kc=kc: e.matmul(
                        zb[hh].ap[:, :], lhsT=hT_.ap[:, kc, :], rhs=ws.ap, start=(kc == 0), stop=(kc == 7)),
                        r=[ws, hT_], w=[zb[hh]])
            db = self.nbank()
            for kc in range(8):
                ws = self.nar(w_in, kc * EV_IN + 2560, 16, w_in.ap[:, kc, 2560:2576])
                op("pe", lambda e, ws=ws, kc=kc: e.matmul(
                    db.ap[:, 0:16], lhsT=hT_.ap[:, kc, :], rhs=ws.ap, start=(kc == 0), stop=(kc == 7)),
                    r=[ws, hT_], w=[db])
            for hh in range(2):
                op("act", lambda e, hh=hh: e.activation(out=sz_.ap[:, hh * 512:(hh + 1) * 512], in_=zb[hh].ap[:, :],
                                                        func=AF.Silu), r=[zb[hh]], w=[sz_])
            U, AB, DT, ADT = (sm_.ap[:, i, :] for i in range(4))
            op("dve", lambda e: e.tensor_tensor(out=U, in0=db.ap[:, 0:16], in1=dtb.ap[:, :], op=ALU.add),
               r=[db, dtb], w=[sm_])
            op("dve", lambda e: e.scalar_tensor_tensor(out=AB, in0=U, scalar=-1.0, in1=U, op0=ALU.mult, op1=ALU.max),
               r=[sm_], w=[sm_])
            op("act", lambda e: e.activation(out=AB, in_=AB, func=AF.Exp, scale=-1.0), r=[sm_], w=[sm_])
            op("act", lambda e: e.activation(out=AB, in_=AB, func=AF.Ln, bias=1.0), r=[sm_], w=[sm_])
            op("dve", lambda e: e.scalar_tensor_tensor(out=DT, in0=U, scalar=0.0, in1=AB, op0=ALU.max, op1=ALU.add),
               r=[sm_], w=[sm_])
            op("dve", lambda e: e.tensor_tensor(out=ADT, in0=DT, in1=arow.ap[:, :], op=ALU.mult),
               r=[sm_, arow], w=[sm_])
            yield
            ACU, TOT, EA, CD, DS = (sm_.ap[:, i, :] for i in range(4, 8)) + (None,) if False else \
                (sm_.ap[:, 4, :], sm_.ap[:, 5, :], sm_.ap[:, 6, :], sm_.ap[:, 7, :], sm_.ap[:, 1, :])
            for q in range(3):
                bk = self.nbank()
                for i in range(4):
                    cc = q * 4 + i
                    for k in range(4):
                        op("pe", lambda e, bk=bk, i=i, cc=cc, k=k: e.matmul(
                            bk.ap[:, i * 128:(i + 1) * 128], lhsT=cdiag.ap[:, cc * 4 + k, :],
                            rhs=cin_.ap[:, cc, k:k + 128], start=(k == 0), stop=False), r=[cdiag, cin_], w=[bk])
                    op("pe", lambda e, bk=bk, i=i, cc=cc: e.matmul(
                        bk.ap[:, i * 128:(i + 1) * 128], lhsT=cb.ap[0:1, cc * 128:(cc + 1) * 128],
                        rhs=self.onesb.ap[0:1, :], start=False, stop=True), r=[cb, self.onesb], w=[bk])
                op("act", lambda e, bk=bk, q=q: e.activation(
                    out=xc.ap[:, q * 4:(q + 1) * 4, :], in_=v3(bk.ap[:, :], 128), func=AF.Silu), r=[bk], w=[xc])
            ab_ = self.nbank()
            op("pe", lambda e: e.matmul(ab_.ap[:, 0:16], lhsT=tri.ap[:, :], rhs=ADT, start=True, stop=True),
               r=[tri, sm_], w=[ab_])
            op("pe", lambda e: e.matmul(ab_.ap[:, 16:32], lhsT=ones.ap[:, :], rhs=ADT, start=True, stop=True),
               r=[ones, sm_], w=[ab_])
            op("dve", lambda e: e.tensor_copy(out=sm_.ap[:, 4:6, :], in_=v3(ab_.ap[:, 0:32], 16)), r=[ab_], w=[sm_])
            op("act", lambda e: e.activation(out=sm_.ap[:, 6:8, :], in_=sm_.ap[:, 4:6, :], func=AF.Exp),
               r=[sm_], w=[sm_])
            op("dve", lambda e: e.tensor_tensor(out=DS, in0=TOT, in1=ACU, op=ALU.subtract), r=[sm_], w=[sm_])
            op("act", lambda e: e.activation(out=DS, in_=DS, func=AF.Exp), r=[sm_], w=[sm_])
            op("dve", lambda e: e.tensor_tensor(out=DS, in0=DS, in1=DT, op=ALU.mult), r=[sm_], w=[sm_])
            op("dve", lambda e: e.tensor_tensor(out=Dl.ap[:, :, :], in0=bc(ADT, 128),
                                                in1=ustr.ap[:, :].unsqueeze(1).broadcast_to([128, 16, 128]),
                                                op=ALU.mult), r=[sm_, ustr], w=[Dl])
            op("pool", lambda e: e.tensor_copy(out=bcT_.ap[:, :, :], in_=xc.ap[:, 8:12, :]), r=[xc], w=[bcT_])
            for g, wdw in enumerate(POOL_W):
                prev, pidx = uin_, g
                nlev = g + 1
                for m in range(1, nlev + 1):
                    sh = 1 << (m - 1)
                    lo = 15 - (wdw - (1 << m))
                    if m == nlev:
                        dstt, dap = prr, prr.ap[:, g, :]
                    else:
                        dstt = pta if (m % 2 == 1) else ptb
                        gi = (g - 1) if (m % 2 == 1) else (g - 2)
                        dap = dstt.ap[:, gi, lo:143]
                    op("pool", lambda e, dap=dap, prev=prev, pidx=pidx, lo=lo, sh=sh: e.tensor_tensor(
                        out=dap, in0=prev.ap[:, pidx, lo:143], in1=prev.ap[:, pidx, lo - sh:143 - sh], op=ALU.add),
                        r=[prev], w=[dstt])
                    prev = dstt
                    pidx = gi if m < nlev else 0
            if c == 0:
                op("pool", lambda e: e.tensor_tensor(out=prr.ap[:, :, :], in0=prr.ap[:, :, :], in1=invf.ap[:, :, :],
                                                     op=ALU.mult), r=[prr, invf], w=[prr])
            else:
                for g, wdw in enumerate(POOL_W):
                    op("pool", lambda e, g=g, wdw=wdw: e.tensor_scalar(
                        out=prr.ap[:, g, :], in0=prr.ap[:, g, :], scalar1=1.0 / wdw, scalar2=None, op0=ALU.mult),
                        r=[prr], w=[prr])
            op("pool", lambda e: e.tensor_tensor(out=pl.ap[:, :, :], in0=prr.ap[:, :, :], in1=uin_.ap[:, :, 15:143],
                                                 op=ALU.subtract), r=[prr, uin_], w=[pl])
            yield
            xb = [self.nbank(), self.nbank()]
            for cc in range(8):
                op("pe", lambda e, cc=cc: e.transpose(out=xb[cc // 4].ap[:, (cc % 4) * 128:(cc % 4 + 1) * 128],
                                                      in_=xc.ap[:, cc, :], identity=ident.ap[:, :]),
                   r=[xc, ident], w=[xb[cc // 4]])
            bb_ = self.nbank()
            for g in range(2):
                op("pe", lambda e, g=g: e.transpose(out=bb_.ap[:, g * 128:(g + 1) * 128], in_=xc.ap[:, 8 + g, :],
                                                    identity=ident.ap[:, :]), r=[xc, ident], w=[bb_])
            for hh in range(2):
                op("act", lambda e, hh=hh: e.activation(out=xtok.ap[:, hh * 512:(hh + 1) * 512], in_=xb[hh].ap[:, :],
                                                        func=AF.Identity), r=[xb[hh]], w=[xtok])
            op("dve", lambda e: e.tensor_copy(out=btok.ap[:, :], in_=bb_.ap[:, 0:256]), r=[bb_], w=[btok])
            cbk = self.nbank()
            for g in range(2):
                op("pe", lambda e, g=g: e.matmul(cbk.ap[:, g * 128:(g + 1) * 128], lhsT=bcT_.ap[:, g, :],
                                                 rhs=bcT_.ap[:, 2 + g, :], start=True, stop=True), r=[bcT_], w=[cbk])
            op("dve", lambda e: e.tensor_tensor(out=cbm.ap[:, :, :], in0=v3(cbk.ap[:, 0:256], 128),
                                                in1=tri.ap[:, :].unsqueeze(1).broadcast_to([128, 2, 128]), op=ALU.mult),
               r=[cbk, tri], w=[cbm])
            for q in range(4):
                bk = self.nbank()
                for i in range(4):
                    h = q * 4 + i
                    op("pe", lambda e, bk=bk, i=i, h=h: e.matmul(bk.ap[:, i * 128:(i + 1) * 128], lhsT=Dl.ap[:, h, :],
                                                                 rhs=tri.ap[:, :], start=True, stop=True),
                       r=[Dl, tri], w=[bk])
                op("act", lambda e, bk=bk, q=q: e.activation(out=LT.ap[:, q * 4:(q + 1) * 4, :], in_=v3(bk.ap[:, :], 128),
                                                             func=AF.Exp), r=[bk], w=[LT])
            pb = self.nbank()
            for g in range(4):
                op("pe", lambda e, g=g: e.matmul(pb.ap[:, g * 128:(g + 1) * 128], lhsT=pw.ap[:, g, :],
                                                 rhs=pl.ap[:, g, :], start=True, stop=True), r=[pw, pl], w=[pb])
            op("act", lambda e: e.activation(out=ybT_.ap[:, :, :], in_=v3(pb.ap[:, :], 128), func=AF.Identity),
               r=[pb], w=[ybT_])
            for g in range(2):
                op("dve", lambda e, g=g: e.tensor_tensor(
                    out=MT.ap[:, g * 8:(g + 1) * 8, :], in0=LT.ap[:, g * 8:(g + 1) * 8, :],
                    in1=cbm.ap[:, g:g + 1, :].broadcast_to([128, 8, 128]), op=ALU.mult), r=[LT, cbm], w=[MT])
            op("dve", lambda e: e.tensor_tensor(out=v3(xdt.ap[:, :], 64), in0=v3(xtok.ap[:, :], 64), in1=bc(DT, 64),
                                                op=ALU.mult), r=[xtok, sm_], w=[xdt])
            op("dve", lambda e: e.tensor_tensor(out=v3(xdd.ap[:, :], 64), in0=v3(xtok.ap[:, :], 64), in1=bc(DS, 64),
                                                op=ALU.mult), r=[xtok, sm_], w=[xdd])
            yield
            if c == 0:
                op("pool", lambda e: e.memset(hst.ap[:, :], 0.0), w=[hst])
                op("pool", lambda e: e.memset(hb.ap[:, :], 0.0), w=[hb])
            yd = [self.nbank(), self.nbank()]
            for h in range(16):
                op("pe", lambda e, h=h: e.matmul(yd[h // 8].ap[:, (h % 8) * 64:(h % 8 + 1) * 64], lhsT=MT.ap[:, h, :],
                                                 rhs=xdt.ap[:, h * 64:(h + 1) * 64], start=True, stop=True),
                   r=[MT, xdt], w=[yd[h // 8]])
            yo = [self.nbank(), self.nbank()]
            for g in range(2):
                op("pe", lambda e, g=g: e.matmul(yo[g].ap[:, :], lhsT=bcT_.ap[:, 2 + g, :],
                                                 rhs=hb.ap[:, g * 512:(g + 1) * 512], start=True, stop=True),
                   r=[bcT_, hb], w=[yo[g]])
            stb = [self.nbank(), self.nbank()]
            for g in range(2):
                op("pe", lambda e, g=g: e.matmul(stb[g].ap[:, :], lhsT=btok.ap[:, g * 128:(g + 1) * 128],
                                                 rhs=xdd.ap[:, g * 512:(g + 1) * 512], start=True, stop=True),
                   r=[btok, xdd], w=[stb[g]])
            for g in range(2):
                op("dve", lambda e, g=g: e.tensor_tensor(
                    out=v3(t1.ap[:, g * 512:(g + 1) * 512], 64), in0=v3(yo[g].ap[:, :], 64),
                    in1=bc(sm_.ap[:, 6, g * 8:(g + 1) * 8], 64), op=ALU.mult), r=[yo[g], sm_], w=[t1])
                op("dve", lambda e, g=g: e.tensor_tensor(
                    out=t1.ap[:, g * 512:(g + 1) * 512], in0=yd[g].ap[:, :], in1=t1.ap[:, g * 512:(g + 1) * 512],
                    op=ALU.add), r=[yd[g], t1], w=[t1])
            op("pool", lambda e: e.tensor_tensor(out=v3(yn.ap[:, :], 64), in0=v3(xtok.ap[:, :], 64),
                                                 in1=bc(dvec.ap[:, :], 64), op=ALU.mult), r=[xtok, dvec], w=[yn])
            op("pool", lambda e: e.tensor_tensor(out=yn.ap[:, :], in0=yn.ap[:, :], in1=t1.ap[:, :], op=ALU.add),
               r=[yn, t1], w=[yn])
            op("pool", lambda e: e.tensor_tensor(out=yn.ap[:, :], in0=yn.ap[:, :], in1=sz_.ap[:, :], op=ALU.mult),
               r=[yn, sz_], w=[yn])
            op("act", lambda e: e.activation(out=t1.ap[:, :], in_=yn.ap[:, :], func=AF.Square,
                                             accum_out=ssq.ap[:, 0:1]), r=[yn], w=[t1, ssq])
            op("pool", lambda e: e.tensor_scalar(out=ssq.ap[:, 1:2], in0=ssq.ap[:, 0:1], scalar1=1.0 / D, scalar2=EPS,
                                                 op0=ALU.mult, op1=ALU.add), r=[ssq], w=[ssq])
            op("pool", lambda e: e.tensor_tensor(out=ssq.ap[:, 1:2], in0=ssq.ap[:, 1:2], in1=self.m05.ap[:, 0:1],
                                                 op=ALU.pow), r=[ssq, self.m05], w=[ssq])
            op("dve", lambda e: e.tensor_scalar(out=yn.ap[:, :], in0=yn.ap[:, :], scalar1=ssq.ap[:, 1:2], scalar2=None,
                                                op0=ALU.mult), r=[yn, ssq], w=[yn])
            op("dve", lambda e: e.tensor_tensor(out=v3(hst.ap[:, :], 64), in0=v3(hst.ap[:, :], 64),
                                                in1=bc(sm_.ap[:, 7, :], 64), op=ALU.mult), r=[hst, sm_], w=[hst])
            for g in range(2):
                op("dve", lambda e, g=g: e.tensor_tensor(
                    out=hst.ap[:, g * 512:(g + 1) * 512], in0=stb[g].ap[:, :], in1=hst.ap[:, g * 512:(g + 1) * 512],
                    op=ALU.add), r=[stb[g], hst], w=[hst])
            op("pool", lambda e: e.tensor_copy(out=hb.ap[:, :], in_=hst.ap[:, :]), r=[hst], w=[hb])
            yield
            tb = [self.nbank(), self.nbank()]
            for cc in range(8):
                op("pe", lambda e, cc=cc: e.transpose(out=tb[cc // 4].ap[:, (cc % 4) * 128:(cc % 4 + 1) * 128],
                                                      in_=yn.ap[:, cc * 128:(cc + 1) * 128], identity=ident.ap[:, :]),
                   r=[yn, ident], w=[tb[cc // 4]])
            for cc in range(8):
                op("act", lambda e, cc=cc: e.activation(
                    out=yaT_.ap[:, cc, :], in_=tb[cc // 4].ap[:, (cc % 4) * 128:(cc % 4 + 1) * 128], func=AF.Identity,
                    scale=normgT.ap[:, cc:cc + 1]), r=[tb[cc // 4], normgT], w=[yaT_])
            yg = Tl(None, [self.ybuf_g[t]])
            op("sp", lambda e: e.dma_start(out=self.dr["yT"][t, :, 0:1024], in_=yaT_.ap[:, :, :].rearrange("p a b -> p (a b)")),
               r=[yaT_], w=[yg], dma=True)
            op("sp", lambda e: e.dma_start(out=self.dr["yT"][t, :, 1024:1536], in_=ybT_.ap[:, :, :].rearrange("p a b -> p (a b)")),
               r=[ybT_], w=[yg], dma=True)
            yield

        self.pipeline([tile(t) for t in range(NT)])
        self.mixer_out(l, src, dst, dr["ev_w_out"][e_])

    def mixer_out(self, l, src, dst, w_dram):
        op = self.op
        al = self.Alloc(self)
        w_out = al([128, 12, D], BF16); al.align()
        ybuf = [al([128, 12, 128], BF16) for _ in range(3)]
        self.load_w_cast(w_out, D, lambda kc: w_dram[kc * 128:(kc + 1) * 128, :], 12, [(0, 512), (512, 1024)])
        al.align()
        li = self.layers.index(l)
        extra = self.mods_gen(self.layers[li + 1], al) if li + 1 < len(self.layers) else None

        def tile(t):
            yb_ = ybuf[t % 3]
            op("sp", lambda e: e.dma_start(out=yb_.ap[:, :, :].rearrange("p a b -> p (a b)"), in_=self.dr["yT"][t, :, :]),
               r=[Tl(None, [self.ybuf_g[t]])], w=[yb_], dma=True)
            yield
            chunks = [(yb_, yb_.ap[:, i, :]) for i in range(12)]
            self.out_proj_epilogue(t, l, src, dst, w_out, chunks)
            yield

        self.pipeline([tile(t) for t in range(self.NT)], extra=extra)

    def odd_mixer(self, l, src, dst):
        op = self.op
        o_ = l // 2
        dr = self.dr
        NT, TPS = self.NT, self.TPS
        ident, ones, onesb = self.ident, self.ones, self.onesb
        al = self.Alloc(self)
        w_in = al([128, 8, OD_IN], BF16); al.align()
        fdiag = al([128, 124, 128], BF16); al.align()
        ldiag = al([128, 32, 128], BF16); al.align()
        wa = al([128, 8, 128], BF16)
        wx = al([128, 8, 128], BF16); al.align()
        brow = al([128, 1536], BF16)
        cl = al([128, 8], F32)
        lng = al([128, 4], F32)
        lnb = al([128, 4], F32)
        hprev = al([128, 8], F32)
        al.align()
        work0 = al.off
        fw = al([128, 124], F32)
        lw = al([128, 32], F32)
        waf = al([128, 8, 128], F32)
        wxf = al([128, 8, 128], F32); al.align()
        browf = al([128, 1536], F32); al.align()
        zt_ = al([128, 8], F32)
        self.load_w_cast(w_in, OD_IN, lambda kc: dr["od_w_in"][o_, kc * 128:(kc + 1) * 128, :], 8,
                         [(0, 1024), (1024, 2048), (2048, OD_IN)])
        op("sp", lambda e: e.dma_start(out=fw.ap[:, :], in_=dr["conf_dw_wT"][o_].rearrange("p c k -> p (c k)")),
           w=[fw], dma=True)
        op("sp", lambda e: e.dma_start(out=lw.ap[:, :], in_=dr["lru_conv_wT"][o_].rearrange("p c k -> p (c k)")),
           w=[lw], dma=True)
        op("sp", lambda e: e.dma_start(out=waf.ap[:, :, :], in_=dr["lru_waT"][o_]), w=[waf], dma=True)
        op("sp", lambda e: e.dma_start(out=wxf.ap[:, :, :], in_=dr["lru_wxT"][o_]), w=[wxf], dma=True)
        op("sp", lambda e: e.dma_start(out=browf.ap[0:1, 0:1024], in_=dr["lru_conv_b"][o_:o_ + 1, :]), w=[browf], dma=True)
        op("sp", lambda e: e.dma_start(out=browf.ap[0:1, 1024:1536], in_=dr["conf_dw_b"][o_:o_ + 1, :]), w=[browf], dma=True)
        op("sp", lambda e: e.dma_start(out=browf.ap[32:33, 0:1024], in_=dr["lru_ba"][o_:o_ + 1, :]), w=[browf], dma=True)
        op("sp", lambda e: e.dma_start(out=browf.ap[64:65, 0:1024], in_=dr["lru_bx"][o_:o_ + 1, :]), w=[browf], dma=True)
        op("sp", lambda e: e.dma_start(out=cl.ap[:, :], in_=dr["lru_lamT"][o_]), w=[cl], dma=True)
        op("sp", lambda e: e.dma_start(out=lng.ap[:, :], in_=dr["conf_ln_gT"][o_]), w=[lng], dma=True)
        op("sp", lambda e: e.dma_start(out=lnb.ap[:, :], in_=dr["conf_ln_bT"][o_]), w=[lnb], dma=True)
        op("dve", lambda e: e.tensor_tensor(
            out=fdiag.ap[:, :, :], in0=ident.ap[:, :].unsqueeze(1).broadcast_to([128, 124, 128]),
            in1=fw.ap[:, :].unsqueeze(2).broadcast_to([128, 124, 128]), op=ALU.mult), r=[ident, fw], w=[fdiag])
        op("dve", lambda e: e.tensor_tensor(
            out=ldiag.ap[:, :, :], in0=ident.ap[:, :].unsqueeze(1).broadcast_to([128, 32, 128]),
            in1=lw.ap[:, :].unsqueeze(2).broadcast_to([128, 32, 128]), op=ALU.mult), r=[ident, lw], w=[ldiag])
        op("dve", lambda e: e.tensor_copy(out=wa.ap[:, :, :], in_=waf.ap[:, :, :]), r=[waf], w=[wa])
        op("dve", lambda e: e.tensor_copy(out=wx.ap[:, :, :], in_=wxf.ap[:, :, :]), r=[wxf], w=[wx])
        op("dve", lambda e: e.tensor_copy(out=brow.ap[0:1, :], in_=browf.ap[0:1, :]), r=[browf], w=[brow])
        op("dve", lambda e: e.tensor_copy(out=brow.ap[32:33, 0:1024], in_=browf.ap[32:33, 0:1024]), r=[browf], w=[brow])
        op("dve", lambda e: e.tensor_copy(out=brow.ap[64:65, 0:1024], in_=browf.ap[64:65, 0:1024]), r=[browf], w=[brow])
        op("act", lambda e: e.activation(out=zt_.ap[:, :], in_=cl.ap[:, :], func=AF.Exp, scale=-1.0), r=[cl], w=[zt_])
        op("dve", lambda e: e.tensor_scalar(out=cl.ap[:, :], in0=zt_.ap[:, :], scalar1=1.0 / 5, scalar2=-1.0 / 4,
                                            op0=ALU.mult, op1=ALU.add), r=[zt_], w=[cl])
        for cst in (1.0 / 3, -1.0 / 2, 1.0):
            op("dve", lambda e: e.tensor_tensor(out=cl.ap[:, :], in0=cl.ap[:, :], in1=zt_.ap[:, :], op=ALU.mult),
               r=[cl, zt_], w=[cl])
            op("dve", lambda e, cst=cst: e.tensor_scalar(out=cl.ap[:, :], in0=cl.ap[:, :], scalar1=cst, scalar2=None,
                                                         op0=ALU.add), r=[cl], w=[cl])
        op("dve", lambda e: e.tensor_tensor(out=cl.ap[:, :], in0=cl.ap[:, :], in1=zt_.ap[:, :], op=ALU.mult),
           r=[cl, zt_], w=[cl])
        op("dve", lambda e: e.tensor_scalar(out=cl.ap[:, :], in0=cl.ap[:, :], scalar1=-8.0, scalar2=None, op0=ALU.mult),
           r=[cl], w=[cl])
        al.off = work0
        hT = al([128, 8, 128], BF16)
        cfin = [al([128, 4, 160], BF16) for _ in range(2)]
        lrin = [al([128, 8, 132], BF16) for _ in range(2)]
        gs = [al([128, 8, 128], BF16) for _ in range(3)]
        ycT = [al([128, 4, 128], BF16) for _ in range(2)]
        hc = al([128, 4, 128], F32)
        sq = al([128, 4, 128], F32)
        mean = al([128, 128], F32)
        rstd = al([128, 128], F32)
        msq = rstd
        xrcs = [al([128, 8, 128], BF16) for _ in range(2)]
        ydT = al([128, 8, 128], BF16)
        R = al([128, 8, 128], F32)
        I = al([128, 8, 128], F32)
        A = al([128, 8, 128], F32)

        def v3(ap, inner):
            return ap.rearrange("p (a b) -> p a b", b=inner)

        def bc(ap2, n):
            return ap2.unsqueeze(2).broadcast_to([128, ap2.shape[1], n])

        def tile(t):
            c = t % TPS
            b = t // TPS
            par = t % 2
            cfin_, lrin_, gs_, ycT_, xrc = cfin[par], lrin[par], gs[t % 3], ycT[par], xrcs[par]
            cfinp, lrinp = cfin[1 - par], lrin[1 - par]
            xl = self.load_x(t, src, t % 4)
            yield
            self.transpose_mod(xl, l, 1, b, hT, 0)
            yield
            if c == 0:
                op("pool", lambda e: e.memset(cfin_.ap[:, :, 0:30], 0.0), w=[cfin_])
                op("pool", lambda e: e.memset(lrin_.ap[:, :, 0:3], 0.0), w=[lrin_])
            else:
                op("pool", lambda e: e.tensor_copy(out=cfin_.ap[:, :, 0:30], in_=cfinp.ap[:, :, 128:158]),
                   r=[cfinp], w=[cfin_])
                op("pool", lambda e: e.tensor_copy(out=lrin_.ap[:, :, 0:3], in_=lrinp.ap[:, :, 128:131]),
                   r=[lrinp], w=[lrin_])
            bks = []
            for q in range(6):
                bk = self.nbank()
                bks.append(bk)
                for i in range(4):
                    col = (q * 4 + i) * 128
                    for kc in range(8):
                        ws = self.nar(w_in, kc * OD_IN + col, 128, w_in.ap[:, kc, col:col + 128])
                        op("pe", lambda e, bk=bk, i=i, ws=ws, kc=kc: e.matmul(
                            bk.ap[:, i * 128:(i + 1) * 128], lhsT=ws.ap, rhs=hT.ap[:, kc, :],
                            start=(kc == 0), stop=(kc == 7)), r=[ws, hT], w=[bk])
                if q == 1:
                    op("act", lambda e, bk=bk: e.activation(out=cfin_.ap[:, :, 30:158], in_=v3(bk.ap[:, :], 128),
                                                            func=AF.Sigmoid), r=[bk], w=[cfin_])
                    op("dve", lambda e: e.tensor_tensor(out=cfin_.ap[:, :, 30:158], in0=v3(bks[0].ap[:, :], 128),
                                                        in1=cfin_.ap[:, :, 30:158], op=ALU.mult),
                       r=[bks[0], cfin_], w=[cfin_])
                elif q in (2, 3):
                    op("act", lambda e, bk=bk, q=q: e.activation(
                        out=lrin_.ap[:, (q - 2) * 4:(q - 1) * 4, 3:131], in_=v3(bk.ap[:, :], 128), func=AF.Identity),
                        r=[bk], w=[lrin_])
                elif q in (4, 5):
                    op("act", lambda e, bk=bk, q=q: e.activation(
                        out=gs_.ap[:, (q - 4) * 4:(q - 3) * 4, :], in_=v3(bk.ap[:, :], 128), func=AF.Gelu_apprx_tanh),
                        r=[bk], w=[gs_])
            yield
            fb = self.nbank()
            for cc in range(4):
                for k in range(31):
                    op("pe", lambda e, cc=cc, k=k: e.matmul(
                        fb.ap[:, cc * 128:(cc + 1) * 128], lhsT=fdiag.ap[:, cc * 31 + k, :],
                        rhs=cfin_.ap[:, cc, k:k + 128], start=(k == 0), stop=False), r=[fdiag, cfin_], w=[fb])
                op("pe", lambda e, cc=cc: e.matmul(
                    fb.ap[:, cc * 128:(cc + 1) * 128], lhsT=brow.ap[0:1, 1024 + cc * 128:1024 + (cc + 1) * 128],
                    rhs=onesb.ap[0:1, :], start=False, stop=True), r=[brow, onesb], w=[fb])
            op("act", lambda e: e.activation(out=hc.ap[:, :, :], in_=v3(fb.ap[:, :], 128), func=AF.Identity),
               r=[fb], w=[hc])
            op("act", lambda e: e.activation(out=sq.ap[:, :, :], in_=v3(fb.ap[:, :], 128), func=AF.Square),
               r=[fb], w=[sq])
            lb = [self.nbank(), self.nbank()]
            for cc in range(8):
                bk = lb[cc // 4]
                i = cc % 4
                for k in range(4):
                    op("pe", lambda e, bk=bk, i=i, cc=cc, k=k: e.matmul(
                        bk.ap[:, i * 128:(i + 1) * 128], lhsT=ldiag.ap[:, cc * 4 + k, :],
                        rhs=lrin_.ap[:, cc, k:k + 128], start=(k == 0), stop=False), r=[ldiag, lrin_], w=[bk])
                op("pe", lambda e, bk=bk, i=i, cc=cc: e.matmul(
                    bk.ap[:, i * 128:(i + 1) * 128], lhsT=brow.ap[0:1, cc * 128:(cc + 1) * 128],
                    rhs=onesb.ap[0:1, :], start=False, stop=True), r=[brow, onesb], w=[bk])
            for hh in range(2):
                op("act", lambda e, hh=hh: e.activation(out=xrc.ap[:, hh * 4:(hh + 1) * 4, :],
                                                        in_=v3(lb[hh].ap[:, :], 128), func=AF.Identity),
                   r=[lb[hh]], w=[xrc])
            yield
            sb_ = self.nbank()
            for cc in range(4):
                op("pe", lambda e, cc=cc: e.matmul(sb_.ap[:, 0:128], lhsT=ones.ap[:, :], rhs=hc.ap[:, cc, :],
                                                   start=(cc == 0), stop=(cc == 3)), r=[ones, hc], w=[sb_])
            for cc in range(4):
                op("pe", lambda e, cc=cc: e.matmul(sb_.ap[:, 128:256], lhsT=ones.ap[:, :], rhs=sq.ap[:, cc, :],
                                                   start=(cc == 0), stop=(cc == 3)), r=[ones, sq], w=[sb_])
            for (wg, prow, dstg) in ((wa, 32, R), (wx, 64, I)):
                gb = [self.nbank(), self.nbank()]
                for h in range(8):
                    bk = gb[h // 4]
                    i = h % 4
                    op("pe", lambda e, bk=bk, i=i, h=h, wg=wg: e.matmul(
                        bk.ap[:, i * 128:(i + 1) * 128], lhsT=wg.ap[:, h, :], rhs=xrc.ap[:, h, :],
                        start=True, stop=False), r=[wg, xrc], w=[bk])
                    op("pe", lambda e, bk=bk, i=i, h=h, prow=prow: e.matmul(
                        bk.ap[:, i * 128:(i + 1) * 128], lhsT=brow.ap[prow:prow + 1, h * 128:(h + 1) * 128],
                        rhs=onesb.ap[prow:prow + 1, :], start=False, stop=True), r=[brow, onesb], w=[bk])
                for hh in range(2):
                    op("act", lambda e, hh=hh, gb=gb, dstg=dstg: e.activation(
                        out=dstg.ap[:, hh * 4:(hh + 1) * 4, :], in_=v3(gb[hh].ap[:, :], 128), func=AF.Sigmoid),
                        r=[gb[hh]], w=[dstg])
            op("dve", lambda e: e.tensor_scalar(out=mean.ap[:, :], in0=sb_.ap[:, 0:128], scalar1=1.0 / 512, scalar2=None,
                                                op0=ALU.mult), r=[sb_], w=[mean])
            op("dve", lambda e: e.tensor_tensor(out=msq.ap[:, :], in0=mean.ap[:, :], in1=mean.ap[:, :], op=ALU.mult),
               r=[mean], w=[msq])
            op("dve", lambda e: e.scalar_tensor_tensor(out=rstd.ap[:, :], in0=sb_.ap[:, 128:256], scalar=1.0 / 512,
                                                       in1=msq.ap[:, :], op0=ALU.mult, op1=ALU.subtract),
               r=[sb_, msq], w=[rstd])
            op("pool", lambda e: e.tensor_scalar(out=rstd.ap[:, :], in0=rstd.ap[:, :], scalar1=EPS, scalar2=None,
                                                 op0=ALU.add), r=[rstd], w=[rstd])
            op("pool", lambda e: e.tensor_tensor(out=rstd.ap[:, :], in0=rstd.ap[:, :], in1=self.m05.ap[:, :],
                                                 op=ALU.pow), r=[rstd, self.m05], w=[rstd])
            op("dve", lambda e: e.tensor_tensor(out=hc.ap[:, :, :], in0=hc.ap[:, :, :],
                                                in1=mean.ap[:, :].unsqueeze(1).broadcast_to([128, 4, 128]),
                                                op=ALU.subtract), r=[hc, mean], w=[hc])
            op("dve", lambda e: e.tensor_tensor(out=hc.ap[:, :, :], in0=hc.ap[:, :, :],
                                                in1=rstd.ap[:, :].unsqueeze(1).broadcast_to([128, 4, 128]),
                                                op=ALU.mult), r=[hc, rstd], w=[hc])
            for cc in range(4):
                op("act", lambda e, cc=cc: e.activation(out=ycT_.ap[:, cc, :], in_=hc.ap[:, cc, :], func=AF.Silu,
                                                        bias=lnb.ap[:, cc:cc + 1], scale=lng.ap[:, cc:cc + 1]),
                   r=[hc, lng, lnb], w=[ycT_])
            yield
            if c == 0:
                op("pool", lambda e: e.memset(hprev.ap[:, :], 0.0), w=[hprev])
            op("dve", lambda e: e.tensor_tensor(out=R.ap[:, :, :], in0=R.ap[:, :, :], in1=bc(cl.ap[:, :], 128),
                                                op=ALU.mult), r=[R, cl], w=[R])
            op("act", lambda e: e.activation(out=A.ap[:, :, :], in_=R.ap[:, :, :], func=AF.Exp), r=[R], w=[A])
            op("act", lambda e: e.activation(out=R.ap[:, :, :], in_=R.ap[:, :, :], func=AF.Exp, scale=2.0),
               r=[R], w=[R])
            op("act", lambda e: e.activation(out=R.ap[:, :, :], in_=R.ap[:, :, :], func=AF.Ln, scale=-1.0, bias=1.0),
               r=[R], w=[R])
            op("act", lambda e: e.activation(out=R.ap[:, :, :], in_=R.ap[:, :, :], func=AF.Exp, scale=0.5),
               r=[R], w=[R])
            op("dve", lambda e: e.tensor_tensor(out=I.ap[:, :, :], in0=I.ap[:, :, :], in1=xrc.ap[:, :, :], op=ALU.mult),
               r=[I, xrc], w=[I])
            op("dve", lambda e: e.tensor_tensor(out=I.ap[:, :, :], in0=I.ap[:, :, :], in1=R.ap[:, :, :], op=ALU.mult),
               r=[I, R], w=[I])
            for cc in range(8):
                op("dve", lambda e, cc=cc: e.tensor_tensor_scan(
                    out=R.ap[:, cc, :], data0=A.ap[:, cc, :], data1=I.ap[:, cc, :], initial=hprev.ap[:, cc:cc + 1],
                    op0=ALU.mult, op1=ALU.add), r=[A, I, hprev], w=[R])
            op("dve", lambda e: e.tensor_copy(out=hprev.ap[:, :], in_=R.ap[:, :, 127]), r=[R], w=[hprev])
            op("dve", lambda e: e.tensor_tensor(out=ydT.ap[:, :, :], in0=R.ap[:, :, :], in1=gs_.ap[:, :, :], op=ALU.mult),
               r=[R, gs_], w=[ydT])
            yg = Tl(None, [self.ybuf_g[t]])
            op("sp", lambda e: e.dma_start(out=self.dr["yT"][t, :, 0:512], in_=ycT_.ap[:, :, :].rearrange("p a b -> p (a b)")),
               r=[ycT_], w=[yg], dma=True)
            op("sp", lambda e: e.dma_start(out=self.dr["yT"][t, :, 512:1536], in_=ydT.ap[:, :, :].rearrange("p a b -> p (a b)")),
               r=[ydT], w=[yg], dma=True)
            yield

        self.pipeline([tile(t) for t in range(NT)])
        self.mixer_out(l, src, dst, dr["od_w_out"][o_])


def host_layout(inp, n_cores, seq):
    f = lambda a: np.ascontiguousarray(np.asarray(a, dtype=np.float32))
    shared = {}
    shared["ada_w"] = f(inp["ada_w"])
    shared["ada_b"] = f(inp["ada_b"])
    shared["ada_bT"] = f(np.asarray(inp["ada_b"]).reshape(L_FULL, 72, 128).transpose(0, 2, 1))
    for k in ("ln_g", "ln_b", "ffn_w_in", "ffn_w_out", "ev_w_in", "ssd_conv_b", "ssd_dt_bias", "ssd_a_log",
              "ssd_d", "pool_scale", "ev_w_out", "od_w_in", "conf_dw_b", "lru_conv_b",
              "lru_ba", "lru_bx", "od_w_out"):
        shared[k] = f(inp[k])
    shared["ssd_conv_wT"] = f(np.asarray(inp["ssd_conv_w"]).reshape(-1, 4, 12, 128).transpose(0, 3, 2, 1))
    shared["ssd_norm_gT"] = f(np.asarray(inp["ssd_norm_g"]).reshape(-1, 8, 128).transpose(0, 2, 1))
    shared["pool_wT"] = f(np.asarray(inp["pool_w"]).transpose(0, 2, 1, 3))
    shared["conf_dw_wT"] = f(np.asarray(inp["conf_dw_w"]).reshape(-1, 31, 4, 128).transpose(0, 3, 2, 1))
    shared["conf_ln_gT"] = f(np.asarray(inp["conf_ln_g"]).reshape(-1, 4, 128).transpose(0, 2, 1))
    shared["conf_ln_bT"] = f(np.asarray(inp["conf_ln_b"]).reshape(-1, 4, 128).transpose(0, 2, 1))
    shared["lru_conv_wT"] = f(np.asarray(inp["lru_conv_w"]).reshape(-1, 4, 8, 128).transpose(0, 3, 2, 1))
    shared["lru_waT"] = f(np.asarray(inp["lru_wa"]).transpose(0, 2, 1, 3))
    shared["lru_wxT"] = f(np.asarray(inp["lru_wx"]).transpose(0, 2, 1, 3))
    shared["lru_lamT"] = f(np.asarray(inp["lru_lambda"]).reshape(-1, 8, 128).transpose(0, 2, 1))
    x = np.asarray(inp["x"], dtype=np.float32)
    c = np.asarray(inp["c"], dtype=np.float32)
    maps = []
    for i in range(n_cores):
        m = dict(shared)
        m["x"] = np.ascontiguousarray(x[2 * i:2 * i + 2].reshape(2 * seq, D))
        m["cT"] = np.ascontiguousarray(c[2 * i:2 * i + 2].reshape(2, 8, 128).transpose(2, 1, 0))
        maps.append(m)
    return maps


_NC_CACHE = {}


def kernel(**inputs):
    x = np.asarray(inputs["x"])
    bsz, seq, _ = x.shape
    n_cores = bsz // 2
    maps = host_layout(inputs, n_cores, seq)
    key = (seq,)
    if key not in _NC_CACHE:
        _NC_CACHE[key] = Builder(seq, list(range(L_FULL))).build()
    nc = _NC_CACHE[key]
    res = run_bass_kernel_spmd(nc, maps, core_ids=list(range(n_cores)))
    outs = [np.asarray(r["out"]).reshape(2, seq, D) for r in res.results]
    return np.concatenate(outs, axis=0).astype(np.float32)
```

```python
import numpy as np
from contextlib import ExitStack
import concourse.bass as bass
import concourse.mybir as mybir
from concourse.bass_utils import run_bass_kernel_spmd

F32 = mybir.dt.float32
BF16 = mybir.dt.bfloat16
AF = mybir.ActivationFunctionType
ALU = mybir.AluOpType

D = 1024
DFF = 2816
NJ = 22
L_FULL = 4
ALPHA = float((2.0 * 4) ** 0.25)
EPS = 1e-5
EV_IN = 3088
OD_IN = 3072
POOL_W = (2, 4, 8, 16)

ENGS = ("pe", "act", "dve", "pool", "sp")
N_DMA_SEMS = 8


class Buf:
    __slots__ = ("name", "last_w", "readers")

    def __init__(self, name=""):
        self.name = name
        self.last_w = None
        self.readers = []


class Op:
    __slots__ = ("eng", "fn", "dma", "deps", "signal", "ordinal", "waits", "dsem", "dval", "prewait")

    def __init__(self, eng, fn, dma):
        self.eng = eng
        self.fn = fn
        self.dma = dma
        self.deps = []
        self.signal = False
        self.ordinal = 0
        self.waits = []
        self.dsem = None
        self.dval = 0
        self.prewait = None


class Prog:
    def __init__(self, nc):
        self.nc = nc
        self.ops = {e: [] for e in ENGS}
        self.all = []

    def add(self, eng, fn, reads=(), writes=(), dma=False):
        op = Op(eng, fn, dma)
        deps = op.deps
        for b in reads:
            if b.last_w is not None:
                deps.append((b.last_w, True))
        for b in writes:
            if b.last_w is not None:
                deps.append((b.last_w, False))
            for r in b.readers:
                deps.append((r, False))
        for b in reads:
            b.readers.append(op)
        for b in writes:
            b.last_w = op
            b.readers = []
        self.ops[eng].append(op)
        self.all.append(op)
        return op

    def finalize(self):
        for op in self.all:
            need = []
            for (p, raw) in op.deps:
                if p is op:
                    continue
                if p.dma or p.eng != op.eng:
                    need.append(p)
                elif op.dma or op.eng != "pe":
                    need.append(p)
            op.deps = need
            for p in need:
                if not p.dma:
                    p.signal = True
        for e in ENGS:
            n = 0
            for op in self.ops[e]:
                if (not op.dma) and op.signal:
                    n += 1
                    op.ordinal = n
        for e in ENGS:
            k = 0
            cnt = [0] * N_DMA_SEMS
            for op in self.ops[e]:
                if not op.dma:
                    continue
                s = k % N_DMA_SEMS
                k += 1
                if cnt[s] > 0:
                    op.prewait = (("dma", e, s), cnt[s] * 16)
                cnt[s] += 1
                op.dsem = ("dma", e, s)
                op.dval = cnt[s] * 16
        for e in ENGS:
            sn = {}
            for op in self.ops[e]:
                w = {}
                if op.prewait is not None:
                    k, v = op.prewait
                    w[k] = v
                for p in op.deps:
                    if p.dma:
                        k, v = p.dsem, p.dval
                    else:
                        k, v = ("eng", p.eng), p.ordinal
                    if w.get(k, 0) < v:
                        w[k] = v
                for k, v in w.items():
                    if sn.get(k, 0) >= v:
                        continue
                    sn[k] = v
                    op.waits.append((k, v))

    def emit(self):
        nc = self.nc
        self.finalize()
        with ExitStack() as st:
            sems = {}
            for e in ENGS:
                if any(op.signal for op in self.ops[e]):
                    sems[("eng", e)] = st.enter_context(nc.semaphore("s_" + e))
                if any(op.dma for op in self.ops[e]):
                    for s in range(N_DMA_SEMS):
                        sems[("dma", e, s)] = st.enter_context(nc.semaphore("d_%s%d" % (e, s)))
            block = st.enter_context(nc.Block())

            def run(engname):
                def body(eng):
                    for op in self.ops[engname]:
                        for (k, v) in op.waits:
                            eng.wait_ge(sems[k], v)
                        ins = op.fn(eng)
                        if op.dma:
                            ins.then_inc(sems[op.dsem], 16)
                        elif op.signal:
                            ins.then_inc(sems[("eng", engname)], 1)
                    last = {}
                    for op in self.ops[engname]:
                        if op.dma:
                            last[op.dsem] = op.dval
                    for k, v in last.items():
                        eng.wait_ge(sems[k], v)
                return body

            block.sync(run("sp"))
            block.scalar(run("act"))
            block.vector(run("dve"))
            block.gpsimd(run("pool"))
            block.tensor(run("pe"))


class Tl:
    __slots__ = ("ap", "bufs", "off", "esz")

    def __init__(self, ap, bufs, off=0, esz=4):
        self.ap = ap
        self.bufs = bufs
        self.off = off
        self.esz = esz

    def __getitem__(self, k):
        return self.ap[k]


PAGE = 2048


class Builder:
    def __init__(self, seq, layers, n_sub=3):
        self.SEQ = seq
        self.layers = layers
        self.n_sub = n_sub
        self.NT = 2 * seq // 128
        self.TPS = seq // 128
        self.nc = bass.Bass("TRN2", target_bir_lowering=False)
        self.P = Prog(self.nc)
        self.st = ExitStack()
        self.rr = 0

    def op(self, eng, fn, r=(), w=(), dma=False):
        rb = []
        for t in r:
            rb.extend(t.bufs)
        wb = []
        for t in w:
            wb.extend(t.bufs)
        return self.P.add(eng, fn, rb, wb, dma)

    def sb(self, name, shape, dt):
        t = self.st.enter_context(self.nc.sbuf_tensor(name, shape, dt))
        return Tl(t, [Buf(name)])

    def ps(self, name):
        t = self.st.enter_context(self.nc.psum_tensor(name, [128, 1024], F32))
        return (Tl(t[:, 0:512], [Buf(name + "a")]), Tl(t[:, 512:1024], [Buf(name + "b")]),
                Tl(t[:, :], None))

    def dram(self, name, shape, dt, kind="Internal"):
        return self.nc.dram_tensor(name, shape, dt, kind=kind).ap()

    def av(self, off, shape, dt):
        n = 1
        for s in shape[1:]:
            n *= s
        esz = 4 if dt == F32 else 2
        nbytes = n * esz
        assert off % 4 == 0 and nbytes % 4 == 0
        assert off + nbytes <= self.arena_bytes, (off, nbytes, self.arena_bytes)
        ap = self.arena[0:shape[0], off // 4:(off + nbytes) // 4]
        if dt != F32:
            ap = ap.bitcast(dt)
        if len(shape) == 3:
            ap = ap.rearrange("p (a b) -> p a b", b=shape[2])
        elif len(shape) == 4:
            ap = ap.rearrange("p (a b c) -> p a b c", b=shape[2], c=shape[3])
        bufs = self.pages[off // PAGE:(off + nbytes - 1) // PAGE + 1]
        return Tl(ap, bufs, off, esz)

    def nar(self, tl, e0, n, ap):
        o0 = tl.off + e0 * tl.esz
        o1 = tl.off + (e0 + n) * tl.esz - 1
        return Tl(ap, self.pages[o0 // PAGE:o1 // PAGE + 1], o0, tl.esz)

    class Alloc:
        def __init__(self, b):
            self.b = b
            self.off = 0

        def __call__(self, shape, dt):
            n = 1
            for s in shape[1:]:
                n *= s
            nbytes = n * (4 if dt == F32 else 2)
            nbytes = (nbytes + 3) // 4 * 4
            t = self.b.av(self.off, shape, dt)
            self.off += nbytes
            return t

        def align(self):
            self.off = (self.off + PAGE - 1) // PAGE * PAGE

    def sub(self, tl, ap):
        return Tl(ap, tl.bufs)

    def build(self):
        nc = self.nc
        SEQ, NT = self.SEQ, self.NT
        L = len(self.layers)
        NE = (L_FULL + 1) // 2
        NO = L_FULL // 2
        dr = {}
        dr["x"] = self.dram("x", [NT * 128, D], F32, "ExternalInput")
        dr["cT"] = self.dram("cT", [128, 8, 2], F32, "ExternalInput")
        dr["ada_w"] = self.dram("ada_w", [L_FULL, D, 9 * D], F32, "ExternalInput")
        dr["ada_bT"] = self.dram("ada_bT", [L_FULL, 128, 72], F32, "ExternalInput")
        dr["ada_b"] = self.dram("ada_b", [L_FULL, 9 * D], F32, "ExternalInput")
        dr["ln_g"] = self.dram("ln_g", [L_FULL, 3, D], F32, "ExternalInput")
        dr["ln_b"] = self.dram("ln_b", [L_FULL, 3, D], F32, "ExternalInput")
        dr["ffn_w_in"] = self.dram("ffn_w_in", [L_FULL, 2, D, 2 * DFF], F32, "ExternalInput")
        dr["ffn_w_out"] = self.dram("ffn_w_out", [L_FULL, 2, DFF, D], F32, "ExternalInput")
        dr["ev_w_in"] = self.dram("ev_w_in", [NE, D, EV_IN], F32, "ExternalInput")
        dr["ssd_conv_wT"] = self.dram("ssd_conv_wT", [NE, 128, 12, 4], F32, "ExternalInput")
        dr["ssd_conv_b"] = self.dram("ssd_conv_b", [NE, 1536], F32, "ExternalInput")
        dr["ssd_dt_bias"] = self.dram("ssd_dt_bias", [NE, 16], F32, "ExternalInput")
        dr["ssd_a_log"] = self.dram("ssd_a_log", [NE, 16], F32, "ExternalInput")
        dr["ssd_d"] = self.dram("ssd_d", [NE, 16], F32, "ExternalInput")
        dr["ssd_norm_gT"] = self.dram("ssd_norm_gT", [NE, 128, 8], F32, "ExternalInput")
        dr["pool_wT"] = self.dram("pool_wT", [NE, 128, 4, 128], F32, "ExternalInput")
        dr["pool_scale"] = self.dram("pool_scale", [NE, 512], F32, "ExternalInput")
        dr["ev_w_out"] = self.dram("ev_w_out", [NE, 1536, D], F32, "ExternalInput")
        dr["od_w_in"] = self.dram("od_w_in", [NO, D, OD_IN], F32, "ExternalInput")
        dr["od_w_out"] = self.dram("od_w_out", [NO, 1536, D], F32, "ExternalInput")
        dr["conf_dw_wT"] = self.dram("conf_dw_wT", [NO, 128, 4, 31], F32, "ExternalInput")
        dr["conf_dw_b"] = self.dram("conf_dw_b", [NO, 512], F32, "ExternalInput")
        dr["conf_ln_gT"] = self.dram("conf_ln_gT", [NO, 128, 4], F32, "ExternalInput")
        dr["conf_ln_bT"] = self.dram("conf_ln_bT", [NO, 128, 4], F32, "ExternalInput")
        dr["lru_conv_wT"] = self.dram("lru_conv_wT", [NO, 128, 8, 4], F32, "ExternalInput")
        dr["lru_conv_b"] = self.dram("lru_conv_b", [NO, D], F32, "ExternalInput")
        dr["lru_waT"] = self.dram("lru_waT", [NO, 128, 8, 128], F32, "ExternalInput")
        dr["lru_ba"] = self.dram("lru_ba", [NO, D], F32, "ExternalInput")
        dr["lru_wxT"] = self.dram("lru_wxT", [NO, 128, 8, 128], F32, "ExternalInput")
        dr["lru_bx"] = self.dram("lru_bx", [NO, D], F32, "ExternalInput")
        dr["lru_lamT"] = self.dram("lru_lamT", [NO, 128, 8], F32, "ExternalInput")
        dr["out"] = self.dram("out", [NT * 128, D], F32, "ExternalOutput")
        dr["xs"] = self.dram("xs", [NT * 128, D], F32)
        dr["gates"] = self.dram("gates", [L_FULL, 3, 2, D], F32)
        dr["yT"] = self.dram("yT", [NT, 128, 1536], BF16)
        self.ybuf_g = [Buf("yT%d" % t) for t in range(NT)]
        self.dr = dr
        self.xbuf = [Buf("xd%d" % t) for t in range(NT)]
        self.gates_buf = Buf("gates")

        self.ident = self.sb("ident", [128, 128], F32)
        self.ones = self.sb("ones", [128, 128], F32)
        self.tri = self.sb("tri", [128, 128], F32)
        self.ustr = self.sb("ustr", [128, 128], F32)
        self.m05 = self.sb("m05", [128, 128], F32)
        self.onesb = self.sb("onesb", [128, 128], BF16)
        self.ones1 = self.sb("ones1", [1, 128], F32)
        self.condT = self.sb("condT", [128, 8, 2], F32)
        self.modT = self.sb("modT", [128, L_FULL, 6, 8, 2], F32)
        self.adabT = self.sb("adabT", [128, L_FULL, 72], F32)
        self.xl = [self.sb("xl%d" % i, [128, D], F32) for i in range(4)]
        self.xe = [self.sb("xe%d" % i, [128, D], F32) for i in range(2)]
        self.zt = [self.sb("zt%d" % i, [128, D], F32) for i in range(2)]
        self.gate_row = self.sb("gate_row", [128, D], F32)
        self.lng_row = self.sb("lng_row", [128, D], F32)
        self.lnb_row = self.sb("lnb_row", [128, D], F32)
        self.bst = [self.sb("bst%d" % i, [128, 2, 6], F32) for i in range(2)]
        self.mv = [self.sb("mv%d" % i, [128, 2], F32) for i in range(2)]
        self.rstd = [self.sb("rstd%d" % i, [128, 2], F32) for i in range(2)]
        self.pst = [self.ps("ps%d" % i) for i in range(4)]
        self.banks = []
        for p in self.pst:
            self.banks.append(p[0])
            self.banks.append(p[1])
        rem = int(nc.sbuf_bytes_remaining) - 256
        self.arena_bytes = rem // PAGE * PAGE
        t = self.st.enter_context(nc.sbuf_tensor("arena", [128, self.arena_bytes // 4], F32))
        self.arena = t
        self.pages = [Buf("pg%d" % i) for i in range(self.arena_bytes // PAGE)]

        self.setup_consts()
        for _ in self.mods_gen(self.layers[0], self.Alloc(self)):
            pass
        src = "x"
        nsub_total = L * self.n_sub
        k = 0
        for li, l in enumerate(self.layers):
            for s in range(self.n_sub):
                k += 1
                dst = "out" if k == nsub_total else "xs"
                if s == 0:
                    self.ffn(l, 0, src, dst)
                elif s == 1:
                    if l % 2 == 0:
                        self.even_mixer(l, src, dst)
                    else:
                        self.odd_mixer(l, src, dst)
                else:
                    self.ffn(l, 1, src, dst)
                src = dst
        self.P.emit()
        self.st.close()
        return nc

    def setup_consts(self):
        op = self.op
        ident, ones, tri, ustr, m05 = self.ident, self.ones, self.tri, self.ustr, self.m05
        op("pool", lambda e: e.memset(ones.ap[:], 1.0), w=[ones])
        op("pool", lambda e: e.memset(m05.ap[:], -0.5), w=[m05])
        op("pool", lambda e: e.memset(self.onesb.ap[:], 1.0), w=[self.onesb])
        op("pool", lambda e: e.memset(self.ones1.ap[:], 1.0), w=[self.ones1])
        op("pool", lambda e: e.affine_select(out=ident.ap[:], in_=ones.ap[:], pattern=[[-1, 128]],
                                             compare_op=ALU.is_equal, fill=0.0, base=0, channel_multiplier=1),
           r=[ones], w=[ident])
        op("pool", lambda e: e.affine_select(out=tri.ap[:], in_=ones.ap[:], pattern=[[1, 128]],
                                             compare_op=ALU.is_ge, fill=0.0, base=0, channel_multiplier=-1),
           r=[ones], w=[tri])
        op("pool", lambda e: e.affine_select(out=ustr.ap[:], in_=ones.ap[:], pattern=[[-1, 128]],
                                             compare_op=ALU.is_gt, fill=0.0, base=0, channel_multiplier=1),
           r=[ones], w=[ustr])
        op("sp", lambda e: e.dma_start(out=self.condT.ap[:], in_=self.dr["cT"][:, :, :]), w=[self.condT], dma=True)
        op("act", lambda e: e.activation(out=self.condT.ap[:], in_=self.condT.ap[:], func=AF.Silu),
           r=[self.condT], w=[self.condT])
        op("sp", lambda e: e.dma_start(out=self.adabT.ap[:], in_=self.dr["ada_bT"].rearrange("l p c -> p l c")),
           w=[self.adabT], dma=True)

    def nbank(self):
        b = self.banks[self.rr % 8]
        self.rr += 1
        return b

    def mods_gen(self, l, al):
        op = self.op
        stage = [al([128, 8, 512], F32) for _ in range(2)]
        al.align()
        grow = [al([2, 512], F32) for _ in range(2)]
        brow = [al([2, 512], F32) for _ in range(2)]
        condT = self.condT
        k = 0
        if True:
            for cg in range(18):
                stg = stage[k % 2]
                k += 1
                v, hf = cg // 2, cg % 2
                for kc in range(8):
                    op("sp", lambda e, stg=stg, kc=kc, l=l, cg=cg: e.dma_start(
                        out=stg.ap[:, kc, :], in_=self.dr["ada_w"][l, kc * 128:(kc + 1) * 128, cg * 512:(cg + 1) * 512]),
                        w=[stg], dma=True)
                if v % 3 == 2:
                    j = v // 3
                    bk = self.nbank()
                    for kc in range(8):
                        op("pe", lambda e, bk=bk, stg=stg, kc=kc: e.matmul(
                            bk.ap[0:2, :], lhsT=condT.ap[:, kc, :], rhs=stg.ap[:, kc, :],
                            start=(kc == 0), stop=(kc == 7)), r=[condT, stg], w=[bk])
                    g_, b_ = grow[hf], brow[hf]
                    for bb in range(2):
                        op("sp", lambda e, b_=b_, l=l, cg=cg, bb=bb: e.dma_start(
                            out=b_.ap[bb:bb + 1, :], in_=self.dr["ada_b"][l:l + 1, cg * 512:(cg + 1) * 512]),
                            w=[b_], dma=True)
                    op("dve", lambda e, g_=g_, bk=bk, b_=b_: e.tensor_tensor(
                        out=g_.ap[:, :], in0=bk.ap[0:2, :], in1=b_.ap[:, :], op=ALU.add), r=[bk, b_], w=[g_])
                    mul = 1.0 if j == 1 else 0.5
                    op("dve", lambda e, g_=g_, mul=mul: e.tensor_scalar(
                        out=g_.ap[:, :], in0=g_.ap[:, :], scalar1=1.0, scalar2=mul, op0=ALU.add, op1=ALU.mult),
                        r=[g_], w=[g_])
                    op("pool", lambda e, g_=g_, l=l, j=j, hf=hf: e.dma_start(
                        out=self.dr["gates"][l, j, :, hf * 512:(hf + 1) * 512], in_=g_.ap[:, :]),
                        r=[g_], w=[Tl(None, [self.gates_buf])], dma=True)
                else:
                    j = v // 3
                    vi = j * 2 + (v % 3)
                    bk = self.nbank()
                    for fc in range(4):
                        for kc in range(8):
                            op("pe", lambda e, bk=bk, stg=stg, kc=kc, fc=fc: e.matmul(
                                bk.ap[:, fc * 2:fc * 2 + 2], lhsT=stg.ap[:, kc, fc * 128:(fc + 1) * 128],
                                rhs=condT.ap[:, kc, :], start=(kc == 0), stop=(kc == 7)),
                                r=[condT, stg], w=[bk])
                    for fc in range(4):
                        ch = hf * 4 + fc
                        addc = 1.0 if (v % 3) == 1 else 0.0
                        op("dve", lambda e, bk=bk, l=l, vi=vi, ch=ch, fc=fc, cg=cg, addc=addc: e.tensor_scalar(
                            out=self.modT.ap[:, l, vi, ch, :], in0=bk.ap[:, fc * 2:fc * 2 + 2],
                            scalar1=self.adabT.ap[:, l, cg * 4 + fc:cg * 4 + fc + 1], scalar2=addc,
                            op0=ALU.add, op1=ALU.add), r=[bk, self.adabT], w=[self.modT])
                yield

    def load_rows(self, l, j, b):
        op = self.op
        op("sp", lambda e: e.dma_start(out=self.gate_row.ap[:, :],
                                       in_=self.dr["gates"][l, j, b, :].partition_broadcast(128)),
           r=[Tl(None, [self.gates_buf])], w=[self.gate_row], dma=True)
        if b == 0:
            op("sp", lambda e: e.dma_start(out=self.lng_row.ap[:, :],
                                           in_=self.dr["ln_g"][l, j, :].partition_broadcast(128)),
               w=[self.lng_row], dma=True)
            op("sp", lambda e: e.dma_start(out=self.lnb_row.ap[:, :],
                                           in_=self.dr["ln_b"][l, j, :].partition_broadcast(128)),
               w=[self.lnb_row], dma=True)

    def load_x(self, t, src, slot):
        xl = self.xl[slot]
        self.op("sp", lambda e: e.dma_start(out=xl.ap[:, :], in_=self.dr[src][t * 128:(t + 1) * 128, :]),
                r=[Tl(None, [self.xbuf[t]])], w=[xl], dma=True)
        return xl

    def transpose_mod(self, xl, l, j, b, hT, col0):
        op = self.op
        for half in range(2):
            bk = self.nbank()
            for q in range(4):
                kc = half * 4 + q
                op("pe", lambda e, bk=bk, q=q, kc=kc: e.transpose(
                    out=bk.ap[:, q * 128:(q + 1) * 128], in_=xl.ap[:, kc * 128:(kc + 1) * 128],
                    identity=self.ident.ap[:, :]), r=[xl, self.ident], w=[bk])
            for q in range(4):
                kc = half * 4 + q
                op("act", lambda e, bk=bk, q=q, kc=kc: e.activation(
                    out=hT.ap[:, kc, col0:col0 + 128], in_=bk.ap[:, q * 128:(q + 1) * 128], func=AF.Identity,
                    bias=self.modT.ap[:, l, 2 * j, kc, b:b + 1], scale=self.modT.ap[:, l, 2 * j + 1, kc, b:b + 1]),
                    r=[bk, self.modT], w=[hT])

    def epilogue(self, t, l, j, src, dst, ybanks, par, ring=None):
        op = self.op
        if ring is None:
            xe, zt, bst, mv, rs = self.xe[par], self.zt[par], self.bst[par], self.mv[par], self.rstd[par]
        else:
            xe, zt, bst, mv, rs = ring[t % len(ring)]
        xg = Tl(None, [self.xbuf[t]])
        op("sp", lambda e: e.dma_start(out=xe.ap[:, :], in_=self.dr[src][t * 128:(t + 1) * 128, :]),
           r=[xg], w=[xe], dma=True)
        for h in range(2):
            op("dve", lambda e, h=h: e.tensor_tensor(out=zt.ap[:, h * 512:(h + 1) * 512], in0=ybanks[h].ap[:, :],
                                                     in1=self.gate_row.ap[:, h * 512:(h + 1) * 512], op=ALU.mult),
               r=[ybanks[h], self.gate_row], w=[zt])
        op("dve", lambda e: e.scalar_tensor_tensor(out=zt.ap[:, :], in0=xe.ap[:, :], scalar=ALPHA, in1=zt.ap[:, :],
                                                   op0=ALU.mult, op1=ALU.add), r=[xe, zt], w=[zt])
        for h in range(2):
            op("dve", lambda e, h=h: e.bn_stats(out=bst.ap[:, h, :], in_=zt.ap[:, h * 512:(h + 1) * 512]),
               r=[zt], w=[bst])
        op("dve", lambda e: e.bn_aggr(out=mv.ap[:, :], in_=bst.ap[:, :, :].rearrange("p a b -> p (a b)")),
           r=[bst], w=[mv])
        op("pool", lambda e: e.tensor_scalar(out=rs.ap[:, 0:1], in0=mv.ap[:, 1:2], scalar1=EPS, scalar2=None,
                                             op0=ALU.add), r=[mv], w=[rs])
        op("pool", lambda e: e.tensor_tensor(out=rs.ap[:, 0:1], in0=rs.ap[:, 0:1], in1=self.m05.ap[:, 0:1],
                                             op=ALU.pow), r=[rs, self.m05], w=[rs])
        op("dve", lambda e: e.scalar_tensor_tensor(out=zt.ap[:, :], in0=zt.ap[:, :], scalar=mv.ap[:, 0:1],
                                                   in1=self.lng_row.ap[:, :], op0=ALU.subtract, op1=ALU.mult),
           r=[zt, mv, self.lng_row], w=[zt])
        op("act", lambda e: e.activation(out=xe.ap[:, :], in_=zt.ap[:, :], func=AF.Identity, scale=rs.ap[:, 0:1]),
           r=[zt, rs], w=[xe])
        op("pool", lambda e: e.tensor_tensor(out=xe.ap[:, :], in0=xe.ap[:, :], in1=self.lnb_row.ap[:, :],
                                             op=ALU.add), r=[xe, self.lnb_row], w=[xe])
        op("pool", lambda e: e.dma_start(out=self.dr[dst][t * 128:(t + 1) * 128, :], in_=xe.ap[:, :]),
           r=[xe], w=[xg], dma=True)

    def pipeline(self, gens, extra=None, every=3):
        live = []
        n = len(gens)
        i = 0
        step = 0
        while i < n or live:
            step += 1
            if extra is not None and step % every == 0:
                try:
                    next(extra)
                except StopIteration:
                    extra = None
            if i < n:
                live.append(gens[i])
                i += 1
            nxt = []
            for g in live:
                try:
                    next(g)
                    nxt.append(g)
                except StopIteration:
                    pass
            live = nxt
        if extra is not None:
            for _ in extra:
                pass

    def ffn(self, l, k, src, dst):
        op = self.op
        j = 0 if k == 0 else 2
        G = 256
        NGT = G // 128
        NG = self.NT // NGT
        al = self.Alloc(self)
        w_in = al([128, 8, 2 * DFF], BF16)
        al.align()
        w_out = al([128, NJ, D], BF16)
        al.align()
        hT = al([128, 8, G], BF16)
        al.align()
        actT = al([128, NJ, G], BF16)
        al.align()
        sg = [al([128, G], BF16) for _ in range(2)]
        wi_d = self.dr["ffn_w_in"]
        wo_d = self.dr["ffn_w_out"]
        for q in range(6):
            wdt = 512 if q < 5 else 256
            for half in range(2):
                c0 = half * DFF + q * 512
                for kc in range(8):
                    off = (kc * 2 * DFF + c0) * 2
                    dst_t = Tl(w_in.ap[:, kc, c0:c0 + wdt], self.pages[off // PAGE:(off + wdt * 2 - 1) // PAGE + 1])
                    op("pool", lambda e, dst_t=dst_t, kc=kc, c0=c0, wdt=wdt: e.dma_start(
                        out=dst_t.ap, in_=wi_d[l, k, kc * 128:(kc + 1) * 128, c0:c0 + wdt]),
                        w=[dst_t], dma=True)
        wo_base = w_out.bufs
        for jj in range(NJ):
            for hh in range(2):
                off0 = (self.arena_off(w_out)) + (jj * D + hh * 512) * 2
                dst_t = Tl(w_out.ap[:, jj, hh * 512:(hh + 1) * 512],
                           self.pages[off0 // PAGE:(off0 + 1024 - 1) // PAGE + 1])
                op("pool", lambda e, dst_t=dst_t, jj=jj, hh=hh: e.dma_start(
                    out=dst_t.ap, in_=wo_d[l, k, jj * 128:(jj + 1) * 128, hh * 512:(hh + 1) * 512]),
                    w=[dst_t], dma=True)

        def w_in_slice(kc, c0):
            off = (kc * 2 * DFF + c0) * 2
            return Tl(w_in.ap[:, kc, c0:c0 + 128], self.pages[off // PAGE:(off + 255) // PAGE + 1])

        def w_out_slice(jj, hh):
            off0 = self.arena_off(w_out) + (jj * D + hh * 512) * 2
            return Tl(w_out.ap[:, jj, hh * 512:(hh + 1) * 512], self.pages[off0 // PAGE:(off0 + 1023) // PAGE + 1])

        def group(g):
            t0 = g * NGT
            b = (t0 * 128) // self.SEQ
            xls = []
            for i in range(NGT):
                xls.append(self.load_x(t0 + i, src, (g % 2) * NGT + i))
            yield
            for i in range(NGT):
                self.transpose_mod(xls[i], l, j, b, hT, i * 128)
            yield
            for jj in range(NJ):
                bg = self.nbank()
                bu = self.nbank()
                for kc in range(8):
                    ws = w_in_slice(kc, jj * 128)
                    op("pe", lambda e, bg=bg, ws=ws, kc=kc: e.matmul(
                        bg.ap[:, 0:G], lhsT=ws.ap, rhs=hT.ap[:, kc, :], start=(kc == 0), stop=(kc == 7)),
                        r=[ws, hT], w=[bg])
                for kc in range(8):
                    ws = w_in_slice(kc, DFF + jj * 128)
                    op("pe", lambda e, bu=bu, ws=ws, kc=kc: e.matmul(
                        bu.ap[:, 0:G], lhsT=ws.ap, rhs=hT.ap[:, kc, :], start=(kc == 0), stop=(kc == 7)),
                        r=[ws, hT], w=[bu])
                s_ = sg[jj % 2]
                op("act", lambda e, s_=s_, bg=bg: e.activation(out=s_.ap[:, :], in_=bg.ap[:, 0:G], func=AF.Silu),
                   r=[bg], w=[s_])
                at = self.sub(actT, actT.ap[:, jj, :])
                op("dve", lambda e, s_=s_, bu=bu, at=at: e.tensor_tensor(
                    out=at.ap, in0=bu.ap[:, 0:G], in1=s_.ap[:, :], op=ALU.mult), r=[bu, s_], w=[at])
            yield
            for i in range(NGT):
                t = t0 + i
                if (t * 128) % self.SEQ == 0:
                    self.load_rows(l, j, b)
                yb = [self.nbank(), self.nbank()]
                for hh in range(2):
                    for jj in range(NJ):
                        ws = w_out_slice(jj, hh)
                        op("pe", lambda e, hh=hh, jj=jj, ws=ws, yb=yb, i=i: e.matmul(
                            yb[hh].ap[:, :], lhsT=actT.ap[:, jj, i * 128:(i + 1) * 128], rhs=ws.ap,
                            start=(jj == 0), stop=(jj == NJ - 1)), r=[actT, ws], w=[yb[hh]])
                self.epilogue(t, l, j, src, dst, yb, t % 2)
            yield

        self.pipeline([group(g) for g in range(NG)])

    def arena_off(self, tl):
        return self.pages.index(tl.bufs[0]) * PAGE


    def load_w_cast(self, wt, ncols, dram_rows_fn, n_kc, splits):
        for kc in range(n_kc):
            for (c0, c1) in splits:
                d = self.nar(wt, kc * ncols + c0, c1 - c0, wt.ap[:, kc, c0:c1])
                self.op("pool", lambda e, d=d, kc=kc, c0=c0, c1=c1: e.dma_start(
                    out=d.ap, in_=dram_rows_fn(kc)[:, c0:c1]), w=[d], dma=True)

    def out_proj_epilogue(self, t, l, src, dst, w_out, chunks, ring=None):
        op = self.op
        b = (t * 128) // self.SEQ
        if (t * 128) % self.SEQ == 0:
            self.load_rows(l, 1, b)
        yb = [self.nbank(), self.nbank()]
        n = len(chunks)
        for hh in range(2):
            for ci, (ct, ap) in enumerate(chunks):
                ws = self.nar(w_out, ci * D + hh * 512, 512, w_out.ap[:, ci, hh * 512:(hh + 1) * 512])
                op("pe", lambda e, hh=hh, ci=ci, ws=ws, ap=ap: e.matmul(
                    yb[hh].ap[:, :], lhsT=ap, rhs=ws.ap, start=(ci == 0), stop=(ci == n - 1)),
                    r=[ct, ws], w=[yb[hh]])
        self.epilogue(t, l, 1, src, dst, yb, t % 2, ring)

    def even_mixer(self, l, src, dst):
        op = self.op
        e_ = l // 2
        dr = self.dr
        NT, TPS = self.NT, self.TPS
        ident, ones, tri, ustr = self.ident, self.ones, self.tri, self.ustr
        al = self.Alloc(self)
        w_in = al([128, 8, EV_IN], BF16); al.align()
        cdiag = al([128, 48, 128], BF16); al.align()
        pw = al([128, 4, 128], BF16)
        cb = al([1, 1536], BF16); al.align()
        dtb = al([128, 16], F32)
        arow = al([128, 16], F32)
        dvec = al([128, 16], F32)
        invf = al([128, 4, 128], F32)
        normgT = al([128, 8], F32); al.align()
        work0 = al.off
        cw = al([128, 48], F32)
        pwf = al([128, 4, 128], F32)
        psr = al([128, 512], F32); al.align()
        cbf = al([1, 1536], F32); al.align()
        iot = al([128, 128], F32)
        self.load_w_cast(w_in, EV_IN, lambda kc: dr["ev_w_in"][e_, kc * 128:(kc + 1) * 128, :], 8,
                         [(0, 1024), (1024, 2560), (2560, EV_IN)])
        op("sp", lambda e: e.dma_start(out=cw.ap[:, :], in_=dr["ssd_conv_wT"][e_].rearrange("p c k -> p (c k)")),
           w=[cw], dma=True)
        op("sp", lambda e: e.dma_start(out=pwf.ap[:, :, :], in_=dr["pool_wT"][e_]), w=[pwf], dma=True)
        op("sp", lambda e: e.dma_start(out=psr.ap[:, :], in_=dr["pool_scale"][e_, :].partition_broadcast(128)),
           w=[psr], dma=True)
        op("sp", lambda e: e.dma_start(out=cbf.ap[:, :], in_=dr["ssd_conv_b"][e_:e_ + 1, :]), w=[cbf], dma=True)
        op("sp", lambda e: e.dma_start(out=dtb.ap[:, :], in_=dr["ssd_dt_bias"][e_, :].partition_broadcast(128)),
           w=[dtb], dma=True)
        op("sp", lambda e: e.dma_start(out=arow.ap[:, :], in_=dr["ssd_a_log"][e_, :].partition_broadcast(128)),
           w=[arow], dma=True)
        op("sp", lambda e: e.dma_start(out=dvec.ap[:, :], in_=dr["ssd_d"][e_, :].partition_broadcast(128)),
           w=[dvec], dma=True)
        op("sp", lambda e: e.dma_start(out=normgT.ap[:, :], in_=dr["ssd_norm_gT"][e_]), w=[normgT], dma=True)
        op("dve", lambda e: e.tensor_tensor(
            out=cdiag.ap[:, :, :], in0=ident.ap[:, :].unsqueeze(1).broadcast_to([128, 48, 128]),
            in1=cw.ap[:, :].unsqueeze(2).broadcast_to([128, 48, 128]), op=ALU.mult), r=[ident, cw], w=[cdiag])
        op("dve", lambda e: e.tensor_tensor(
            out=pw.ap[:, :, :], in0=pwf.ap[:, :, :], in1=psr.ap[:, :].rearrange("p (g d) -> p g d", d=128),
            op=ALU.mult), r=[pwf, psr], w=[pw])
        op("dve", lambda e: e.tensor_copy(out=cb.ap[:, :], in_=cbf.ap[:, :]), r=[cbf], w=[cb])
        op("act", lambda e: e.activation(out=arow.ap[:, :], in_=arow.ap[:, :], func=AF.Exp), r=[arow], w=[arow])
        op("dve", lambda e: e.tensor_scalar(out=arow.ap[:, :], in0=arow.ap[:, :], scalar1=-1.0, scalar2=None,
                                            op0=ALU.mult), r=[arow], w=[arow])
        op("pool", lambda e: e.iota(iot.ap[:, :], pattern=[[1, 128]], base=1, channel_multiplier=0,
                                    allow_small_or_imprecise_dtypes=True), w=[iot])
        for g, wdw in enumerate(POOL_W):
            op("dve", lambda e, g=g, wdw=wdw: e.tensor_scalar(out=invf.ap[:, g, :], in0=iot.ap[:, :],
                                                              scalar1=float(wdw), scalar2=None, op0=ALU.min),
               r=[iot], w=[invf])
        op("dve", lambda e: e.reciprocal(out=invf.ap[:, :, :], in_=invf.ap[:, :, :]), r=[invf], w=[invf])
        al.off = work0
        hT = [al([128, 8, 128], BF16)] * 2
        cin = [al([128, 12, 132], BF16) for _ in range(2)]
        uin = [al([128, 4, 144], F32) for _ in range(2)]
        al.align()
        bcT = [al([128, 4, 128], BF16) for _ in range(2)]
        sz = [al([128, D], BF16) for _ in range(3)]
        sm = [al([128, 8, 16], F32) for _ in range(3)]
        al.align()
        xtok = al([128, D], F32)
        xdt = al([128, D], BF16)
        xdd = al([128, D], BF16)
        btok = al([128, 256], BF16)
        al.align()
        Dl = al([128, 16, 128], F32)
        xc = al([128, 12, 128], F32)
        t1 = al([128, D], F32)
        yn = al([128, D], F32)
        LT = al([128, 16, 128], BF16)
        MT = LT
        cbm = al([128, 2, 128], F32)
        al.align()
        yaT = [al([128, 8, 128], BF16)] * 2
        ybT = [al([128, 4, 128], BF16) for _ in range(2)]
        al.align()
        pta = al([128, 3, 144], F32)
        ptb = al([128, 2, 144], F32)
        prr = al([128, 4, 128], F32)
        pl = al([128, 4, 128], BF16)
        al.align()
        hst = al([128, D], F32)
        hb = al([128, D], BF16)
        ssq = al([128, 2], F32)

        def v3(ap, inner):
            return ap.rearrange("p (a b) -> p a b", b=inner)

        def bc(ap2, n):
            return ap2.unsqueeze(2).broadcast_to([128, ap2.shape[1], n])

        def tile(t):
            c = t % TPS
            b = t // TPS
            par = t % 2
            hT_, cin_, uin_, bcT_, sm_, yaT_, ybT_, sz_ = hT[par], cin[par], uin[par], bcT[par], sm[t % 3], yaT[par], ybT[par], sz[t % 3]
            cinp, uinp = cin[1 - par], uin[1 - par]
            xl = self.load_x(t, src, t % 4)
            yield
            self.transpose_mod(xl, l, 1, b, hT_, 0)
            yield
            if c == 0:
                op("pool", lambda e: e.memset(cin_.ap[:, :, 0:3], 0.0), w=[cin_])
                op("pool", lambda e: e.memset(uin_.ap[:, :, 0:15], 0.0), w=[uin_])
            else:
                op("pool", lambda e: e.tensor_copy(out=cin_.ap[:, :, 0:3], in_=cinp.ap[:, :, 128:131]),
                   r=[cinp], w=[cin_])
                op("pool", lambda e: e.tensor_copy(out=uin_.ap[:, :, 0:15], in_=uinp.ap[:, :, 128:143]),
                   r=[uinp], w=[uin_])
            for q in range(4):
                bk = self.nbank()
                for i in range(4):
                    cc = q * 4 + i
                    col = (1024 + cc * 128) if cc < 12 else (2576 + (cc - 12) * 128)
                    for kc in range(8):
                        ws = self.nar(w_in, kc * EV_IN + col, 128, w_in.ap[:, kc, col:col + 128])
                        op("pe", lambda e, bk=bk, i=i, ws=ws, kc=kc: e.matmul(
                            bk.ap[:, i * 128:(i + 1) * 128], lhsT=ws.ap, rhs=hT_.ap[:, kc, :],
                            start=(kc == 0), stop=(kc == 7)), r=[ws, hT_], w=[bk])
                if q < 3:
                    op("act", lambda e, bk=bk, q=q: e.activation(
                        out=cin_.ap[:, q * 4:(q + 1) * 4, 3:131], in_=v3(bk.ap[:, :], 128), func=AF.Identity),
                        r=[bk], w=[cin_])
                else:
                    op("dve", lambda e, bk=bk: e.tensor_copy(out=uin_.ap[:, :, 15:143], in_=v3(bk.ap[:, :], 128)),
                       r=[bk], w=[uin_])
            zb = [self.nbank(), self.nbank()]
            for hh in range(2):
                for kc in range(8):
                    ws = self.nar(w_in, kc * EV_IN + hh * 512, 512, w_in.ap[:, kc, hh * 512:(hh + 1) * 512])
                    op("pe", lambda e, hh=hh, ws=ws, kc=kc: e.matmul(
                        zb[hh].ap[:, :], lhsT=hT_.ap[:, kc, :], rhs=ws.ap, start=(kc == 0), stop=(kc == 7)),
                        r=[ws, hT_], w=[zb[hh]])
            db = self.nbank()
            for kc in range(8):
                ws = self.nar(w_in, kc * EV_IN + 2560, 16, w_in.ap[:, kc, 2560:2576])
                op("pe", lambda e, ws=ws, kc=kc: e.matmul(
                    db.ap[:, 0:16], lhsT=hT_.ap[:, kc, :], rhs=ws.ap, start=(kc == 0), stop=(kc == 7)),
                    r=[ws, hT_], w=[db])
            for hh in range(2):
                op("act", lambda e, hh=hh: e.activation(out=sz_.ap[:, hh * 512:(hh + 1) * 512], in_=zb[hh].ap[:, :],
                                                        func=AF.Silu), r=[zb[hh]], w=[sz_])
            U, AB, DT, ADT = (sm_.ap[:, i, :] for i in range(4))
            op("dve", lambda e: e.tensor_tensor(out=U, in0=db.ap[:, 0:16], in1=dtb.ap[:, :], op=ALU.add),
               r=[db, dtb], w=[sm_])
            op("dve", lambda e: e.scalar_tensor_tensor(out=AB, in0=U, scalar=-1.0, in1=U, op0=ALU.mult, op1=ALU.max),
               r=[sm_], w=[sm_])
            op("act", lambda e: e.activation(out=AB, in_=AB, func=AF.Exp, scale=-1.0), r=[sm_], w=[sm_])
            op("act", lambda e: e.activation(out=AB, in_=AB, func=AF.Ln, bias=1.0), r=[sm_], w=[sm_])
            op("dve", lambda e: e.scalar_tensor_tensor(out=DT, in0=U, scalar=0.0, in1=AB, op0=ALU.max, op1=ALU.add),
               r=[sm_], w=[sm_])
            op("dve", lambda e: e.tensor_tensor(out=ADT, in0=DT, in1=arow.ap[:, :], op=ALU.mult),
               r=[sm_, arow], w=[sm_])
            yield
            ACU, TOT, EA, CD, DS = (sm_.ap[:, i, :] for i in range(4, 8)) + (None,) if False else \
                (sm_.ap[:, 4, :], sm_.ap[:, 5, :], sm_.ap[:, 6, :], sm_.ap[:, 7, :], sm_.ap[:, 1, :])
            for q in range(3):
                bk = self.nbank()
                for i in range(4):
                    cc = q * 4 + i
                    for k in range(4):
                        op("pe", lambda e, bk=bk, i=i, cc=cc, k=k: e.matmul(
                            bk.ap[:, i * 128:(i + 1) * 128], lhsT=cdiag.ap[:, cc * 4 + k, :],
                            rhs=cin_.ap[:, cc, k:k + 128], start=(k == 0), stop=False), r=[cdiag, cin_], w=[bk])
                    op("pe", lambda e, bk=bk, i=i, cc=cc: e.matmul(
                        bk.ap[:, i * 128:(i + 1) * 128], lhsT=cb.ap[0:1, cc * 128:(cc + 1) * 128],
                        rhs=self.onesb.ap[0:1, :], start=False, stop=True), r=[cb, self.onesb], w=[bk])
                op("act", lambda e, bk=bk, q=q: e.activation(
                    out=xc.ap[:, q * 4:(q + 1) * 4, :], in_=v3(bk.ap[:, :], 128), func=AF.Silu), r=[bk], w=[xc])
            ab_ = self.nbank()
            op("pe", lambda e: e.matmul(ab_.ap[:, 0:16], lhsT=tri.ap[:, :], rhs=ADT, start=True, stop=True),
               r=[tri, sm_], w=[ab_])
            op("pe", lambda e: e.matmul(ab_.ap[:, 16:32], lhsT=ones.ap[:, :], rhs=ADT, start=True, stop=True),
               r=[ones, sm_], w=[ab_])
            op("dve", lambda e: e.tensor_copy(out=sm_.ap[:, 4:6, :], in_=v3(ab_.ap[:, 0:32], 16)), r=[ab_], w=[sm_])
            op("act", lambda e: e.activation(out=sm_.ap[:, 6:8, :], in_=sm_.ap[:, 4:6, :], func=AF.Exp),
               r=[sm_], w=[sm_])
            op("dve", lambda e: e.tensor_tensor(out=DS, in0=TOT, in1=ACU, op=ALU.subtract), r=[sm_], w=[sm_])
            op("act", lambda e: e.activation(out=DS, in_=DS, func=AF.Exp), r=[sm_], w=[sm_])
            op("dve", lambda e: e.tensor_tensor(out=DS, in0=DS, in1=DT, op=ALU.mult), r=[sm_], w=[sm_])
            op("dve", lambda e: e.tensor_tensor(out=Dl.ap[:, :, :], in0=bc(ADT, 128),
                                                in1=ustr.ap[:, :].unsqueeze(1).broadcast_to([128, 16, 128]),
                                                op=ALU.mult), r=[sm_, ustr], w=[Dl])
            op("pool", lambda e: e.tensor_copy(out=bcT_.ap[:, :, :], in_=xc.ap[:, 8:12, :]), r=[xc], w=[bcT_])
            for g, wdw in enumerate(POOL_W):
                prev, pidx = uin_, g
                nlev = g + 1
                for m in range(1, nlev + 1):
                    sh = 1 << (m - 1)
                    lo = 15 - (wdw - (1 << m))
                    if m == nlev:
                        dstt, dap = prr, prr.ap[:, g, :]
                    else:
                        dstt = pta if (m % 2 == 1) else ptb
                        gi = (g - 1) if (m % 2 == 1) else (g - 2)
                        dap = dstt.ap[:, gi, lo:143]
                    op("pool", lambda e, dap=dap, prev=prev, pidx=pidx, lo=lo, sh=sh: e.tensor_tensor(
                        out=dap, in0=prev.ap[:, pidx, lo:143], in1=prev.ap[:, pidx, lo - sh:143 - sh], op=ALU.add),
                        r=[prev], w=[dstt])
                    prev = dstt
                    pidx = gi if m < nlev else 0
            if c == 0:
                op("pool", lambda e: e.tensor_tensor(out=prr.ap[:, :, :], in0=prr.ap[:, :, :], in1=invf.ap[:, :, :],
                                                     op=ALU.mult), r=[prr, invf], w=[prr])
            else:
                for g, wdw in enumerate(POOL_W):
                    op("pool", lambda e, g=g, wdw=wdw: e.tensor_scalar(
                        out=prr.ap[:, g, :], in0=prr.ap[:, g, :], scalar1=1.0 / wdw, scalar2=None, op0=ALU.mult),
                        r=[prr], w=[prr])
            op("pool", lambda e: e.tensor_tensor(out=pl.ap[:, :, :], in0=prr.ap[:, :, :], in1=uin_.ap[:, :, 15:143],
                                                 op=ALU.subtract), r=[prr, uin_], w=[pl])
            yield
            xb = [self.nbank(), self.nbank()]
            for cc in range(8):
                op("pe", lambda e, cc=cc: e.transpose(out=xb[cc // 4].ap[:, (cc % 4) * 128:(cc % 4 + 1) * 128],
                                                      in_=xc.ap[:, cc, :], identity=ident.ap[:, :]),
                   r=[xc, ident], w=[xb[cc // 4]])
            bb_ = self.nbank()
            for g in range(2):
                op("pe", lambda e, g=g: e.transpose(out=bb_.ap[:, g * 128:(g + 1) * 128], in_=xc.ap[:, 8 + g, :],
                                                    identity=ident.ap[:, :]), r=[xc, ident], w=[bb_])
            for hh in range(2):
                op("act", lambda e, hh=hh: e.activation(out=xtok.ap[:, hh * 512:(hh + 1) * 512], in_=xb[hh].ap[:, :],
                                                        func=AF.Identity), r=[xb[hh]], w=[xtok])
            op("dve", lambda e: e.tensor_copy(out=btok.ap[:, :], in_=bb_.ap[:, 0:256]), r=[bb_], w=[btok])
            cbk = self.nbank()
            for g in range(2):
                op("pe", lambda e, g=g: e.matmul(cbk.ap[:, g * 128:(g + 1) * 128], lhsT=bcT_.ap[:, g, :],
                                                 rhs=bcT_.ap[:, 2 + g, :], start=True, stop=True), r=[bcT_], w=[cbk])
            op("dve", lambda e: e.tensor_tensor(out=cbm.ap[:, :, :], in0=v3(cbk.ap[:, 0:256], 128),
                                                in1=tri.ap[:, :].unsqueeze(1).broadcast_to([128, 2, 128]), op=ALU.mult),
               r=[cbk, tri], w=[cbm])
            for q in range(4):
                bk = self.nbank()
                for i in range(4):
                    h = q * 4 + i
                    op("pe", lambda e, bk=bk, i=i, h=h: e.matmul(bk.ap[:, i * 128:(i + 1) * 128], lhsT=Dl.ap[:, h, :],
                                                                 rhs=tri.ap[:, :], start=True, stop=True),
                       r=[Dl, tri], w=[bk])
                op("act", lambda e, bk=bk, q=q: e.activation(out=LT.ap[:, q * 4:(q + 1) * 4, :], in_=v3(bk.ap[:, :], 128),
                                                             func=AF.Exp), r=[bk], w=[LT])
            pb = self.nbank()
            for g in range(4):
                op("pe", lambda e, g=g: e.matmul(pb.ap[:, g * 128:(g + 1) * 128], lhsT=pw.ap[:, g, :],
                                                 rhs=pl.ap[:, g, :], start=True, stop=True), r=[pw, pl], w=[pb])
            op("act", lambda e: e.activation(out=ybT_.ap[:, :, :], in_=v3(pb.ap[:, :], 128), func=AF.Identity),
               r=[pb], w=[ybT_])
            for g in range(2):
                op("dve", lambda e, g=g: e.tensor_tensor(
                    out=MT.ap[:, g * 8:(g + 1) * 8, :], in0=LT.ap[:, g * 8:(g + 1) * 8, :],
                    in1=cbm.ap[:, g:g + 1, :].broadcast_to([128, 8, 128]), op=ALU.mult), r=[LT, cbm], w=[MT])
            op("dve", lambda e: e.tensor_tensor(out=v3(xdt.ap[:, :], 64), in0=v3(xtok.ap[:, :], 64), in1=bc(DT, 64),
                                                op=ALU.mult), r=[xtok, sm_], w=[xdt])
            op("dve", lambda e: e.tensor_tensor(out=v3(xdd.ap[:, :], 64), in0=v3(xtok.ap[:, :], 64), in1=bc(DS, 64),
                                                op=ALU.mult), r=[xtok, sm_], w=[xdd])
            yield
            if c == 0:
                op("pool", lambda e: e.memset(hst.ap[:, :], 0.0), w=[hst])
                op("pool", lambda e: e.memset(hb.ap[:, :], 0.0), w=[hb])
            yd = [self.nbank(), self.nbank()]
            for h in range(16):
                op("pe", lambda e, h=h: e.matmul(yd[h // 8].ap[:, (h % 8) * 64:(h % 8 + 1) * 64], lhsT=MT.ap[:, h, :],
                                                 rhs=xdt.ap[:, h * 64:(h + 1) * 64], start=True, stop=True),
                   r=[MT, xdt], w=[yd[h // 8]])
            yo = [self.nbank(), self.nbank()]
            for g in range(2):
                op("pe", lambda e, g=g: e.matmul(yo[g].ap[:, :], lhsT=bcT_.ap[:, 2 + g, :],
                                                 rhs=hb.ap[:, g * 512:(g + 1) * 512], start=True, stop=True),
                   r=[bcT_, hb], w=[yo[g]])
            stb = [self.nbank(), self.nbank()]
            for g in range(2):
                op("pe", lambda e, g=g: e.matmul(stb[g].ap[:, :], lhsT=btok.ap[:, g * 128:(g + 1) * 128],
                                                 rhs=xdd.ap[:, g * 512:(g + 1) * 512], start=True, stop=True),
                   r=[btok, xdd], w=[stb[g]])
            for g in range(2):
                op("dve", lambda e, g=g: e.tensor_tensor(
                    out=v3(t1.ap[:, g * 512:(g + 1) * 512], 64), in0=v3(yo[g].ap[:, :], 64),
                    in1=bc(sm_.ap[:, 6, g * 8:(g + 1) * 8], 64), op=ALU.mult), r=[yo[g], sm_], w=[t1])
                op("dve", lambda e, g=g: e.tensor_tensor(
                    out=t1.ap[:, g * 512:(g + 1) * 512], in0=yd[g].ap[:, :], in1=t1.ap[:, g * 512:(g + 1) * 512],
                    op=ALU.add), r=[yd[g], t1], w=[t1])
            op("dve", lambda e: e.tensor_tensor(out=v3(hst.ap[:, :], 64), in0=v3(hst.ap[:, :], 64),
                                                in1=bc(sm_.ap[:, 7, :], 64), op=ALU.mult), r=[hst, sm_], w=[hst])
            for g in range(2):
                op("dve", lambda e, g=g: e.tensor_tensor(
                    out=hst.ap[:, g * 512:(g + 1) * 512], in0=stb[g].ap[:, :], in1=hst.ap[:, g * 512:(g + 1) * 512],
                    op=ALU.add), r=[stb[g], hst], w=[hst])
            op("pool", lambda e: e.tensor_copy(out=hb.ap[:, :], in_=hst.ap[:, :]), r=[hst], w=[hb])
            op("pool", lambda e: e.tensor_tensor(out=v3(yn.ap[:, :], 64), in0=v3(xtok.ap[:, :], 64),
                                                 in1=bc(dvec.ap[:, :], 64), op=ALU.mult), r=[xtok, dvec], w=[yn])
            op("pool", lambda e: e.tensor_tensor(out=yn.ap[:, :], in0=yn.ap[:, :], in1=t1.ap[:, :], op=ALU.add),
               r=[yn, t1], w=[yn])
            op("pool", lambda e: e.tensor_tensor(out=yn.ap[:, :], in0=yn.ap[:, :], in1=sz_.ap[:, :], op=ALU.mult),
               r=[yn, sz_], w=[yn])
            op("act", lambda e: e.activation(out=t1.ap[:, :], in_=yn.ap[:, :], func=AF.Square,
                                             accum_out=ssq.ap[:, 0:1]), r=[yn], w=[t1, ssq])
            op("pool", lambda e: e.tensor_scalar(out=ssq.ap[:, 1:2], in0=ssq.ap[:, 0:1], scalar1=1.0 / D, scalar2=EPS,
                                                 op0=ALU.mult, op1=ALU.add), r=[ssq], w=[ssq])
            op("pool", lambda e: e.tensor_tensor(out=ssq.ap[:, 1:2], in0=ssq.ap[:, 1:2], in1=self.m05.ap[:, 0:1],
                                                 op=ALU.pow), r=[ssq, self.m05], w=[ssq])
            op("dve", lambda e: e.tensor_scalar(out=yn.ap[:, :], in0=yn.ap[:, :], scalar1=ssq.ap[:, 1:2], scalar2=None,
                                                op0=ALU.mult), r=[yn, ssq], w=[yn])
            yield
            tb = [self.nbank(), self.nbank()]
            for cc in range(8):
                op("pe", lambda e, cc=cc: e.transpose(out=tb[cc // 4].ap[:, (cc % 4) * 128:(cc % 4 + 1) * 128],
                                                      in_=yn.ap[:, cc * 128:(cc + 1) * 128], identity=ident.ap[:, :]),
                   r=[yn, ident], w=[tb[cc // 4]])
            for cc in range(8):
                op("act", lambda e, cc=cc: e.activation(
                    out=yaT_.ap[:, cc, :], in_=tb[cc // 4].ap[:, (cc % 4) * 128:(cc % 4 + 1) * 128], func=AF.Identity,
                    scale=normgT.ap[:, cc:cc + 1]), r=[tb[cc // 4], normgT], w=[yaT_])
            yg = Tl(None, [self.ybuf_g[t]])
            op("act", lambda e: e.dma_start(out=self.dr["yT"][t, :, 0:1024], in_=yaT_.ap[:, :, :].rearrange("p a b -> p (a b)")),
               r=[yaT_], w=[yg], dma=True)
            op("act", lambda e: e.dma_start(out=self.dr["yT"][t, :, 1024:1536], in_=ybT_.ap[:, :, :].rearrange("p a b -> p (a b)")),
               r=[ybT_], w=[yg], dma=True)
            yield

        self.pipeline([tile(t) for t in range(NT)])
        self.mixer_out(l, src, dst, dr["ev_w_out"][e_])

    def mixer_out(self, l, src, dst, w_dram):
        op = self.op
        al = self.Alloc(self)
        w_out = al([128, 12, D], BF16); al.align()
        ybuf = [al([128, 12, 128], BF16) for _ in range(4)]
        al.align()
        ring = []
        for _ in range(4):
            ring.append((al([128, D], F32), al([128, D], F32), al([128, 2, 6], F32), al([128, 2], F32), al([128, 2], F32)))
            al.align()
        self.load_w_cast(w_out, D, lambda kc: w_dram[kc * 128:(kc + 1) * 128, :], 12, [(0, 512), (512, 1024)])
        al.align()
        li = self.layers.index(l)
        extra = self.mods_gen(self.layers[li + 1], al) if li + 1 < len(self.layers) else None

        def tile(t):
            yb_ = ybuf[t % 4]
            op("sp", lambda e: e.dma_start(out=yb_.ap[:, :, :].rearrange("p a b -> p (a b)"), in_=self.dr["yT"][t, :, :]),
               r=[Tl(None, [self.ybuf_g[t]])], w=[yb_], dma=True)
            yield
            chunks = [(yb_, yb_.ap[:, i, :]) for i in range(12)]
            self.out_proj_epilogue(t, l, src, dst, w_out, chunks, ring)
            yield

        self.pipeline([tile(t) for t in range(self.NT)], extra=extra)

    def odd_mixer(self, l, src, dst):
        op = self.op
        o_ = l // 2
        dr = self.dr
        NT, TPS = self.NT, self.TPS
        ident, ones, onesb = self.ident, self.ones, self.onesb
        al = self.Alloc(self)
        w_in = al([128, 8, OD_IN], BF16); al.align()
        fdiag = al([128, 124, 128], BF16); al.align()
        ldiag = al([128, 32, 128], BF16); al.align()
        wa = al([128, 8, 128], BF16)
        wx = al([128, 8, 128], BF16); al.align()
        brow = al([128, 1536], BF16)
        cl = al([128, 8], F32)
        lng = al([128, 4], F32)
        lnb = al([128, 4], F32)
        hprev = al([128, 8], F32)
        al.align()
        work0 = al.off
        fw = al([128, 124], F32)
        lw = al([128, 32], F32)
        waf = al([128, 8, 128], F32)
        wxf = al([128, 8, 128], F32); al.align()
        browf = al([128, 1536], F32); al.align()
        zt_ = al([128, 8], F32)
        self.load_w_cast(w_in, OD_IN, lambda kc: dr["od_w_in"][o_, kc * 128:(kc + 1) * 128, :], 8,
                         [(0, 1024), (1024, 2048), (2048, OD_IN)])
        op("sp", lambda e: e.dma_start(out=fw.ap[:, :], in_=dr["conf_dw_wT"][o_].rearrange("p c k -> p (c k)")),
           w=[fw], dma=True)
        op("sp", lambda e: e.dma_start(out=lw.ap[:, :], in_=dr["lru_conv_wT"][o_].rearrange("p c k -> p (c k)")),
           w=[lw], dma=True)
        op("sp", lambda e: e.dma_start(out=waf.ap[:, :, :], in_=dr["lru_waT"][o_]), w=[waf], dma=True)
        op("sp", lambda e: e.dma_start(out=wxf.ap[:, :, :], in_=dr["lru_wxT"][o_]), w=[wxf], dma=True)
        op("sp", lambda e: e.dma_start(out=browf.ap[0:1, 0:1024], in_=dr["lru_conv_b"][o_:o_ + 1, :]), w=[browf], dma=True)
        op("sp", lambda e: e.dma_start(out=browf.ap[0:1, 1024:1536], in_=dr["conf_dw_b"][o_:o_ + 1, :]), w=[browf], dma=True)
        op("sp", lambda e: e.dma_start(out=browf.ap[32:33, 0:1024], in_=dr["lru_ba"][o_:o_ + 1, :]), w=[browf], dma=True)
        op("sp", lambda e: e.dma_start(out=browf.ap[64:65, 0:1024], in_=dr["lru_bx"][o_:o_ + 1, :]), w=[browf], dma=True)
        op("sp", lambda e: e.dma_start(out=cl.ap[:, :], in_=dr["lru_lamT"][o_]), w=[cl], dma=True)
        op("sp", lambda e: e.dma_start(out=lng.ap[:, :], in_=dr["conf_ln_gT"][o_]), w=[lng], dma=True)
        op("sp", lambda e: e.dma_start(out=lnb.ap[:, :], in_=dr["conf_ln_bT"][o_]), w=[lnb], dma=True)
        op("dve", lambda e: e.tensor_tensor(
            out=fdiag.ap[:, :, :], in0=ident.ap[:, :].unsqueeze(1).broadcast_to([128, 124, 128]),
            in1=fw.ap[:, :].unsqueeze(2).broadcast_to([128, 124, 128]), op=ALU.mult), r=[ident, fw], w=[fdiag])
        op("dve", lambda e: e.tensor_tensor(
            out=ldiag.ap[:, :, :], in0=ident.ap[:, :].unsqueeze(1).broadcast_to([128, 32, 128]),
            in1=lw.ap[:, :].unsqueeze(2).broadcast_to([128, 32, 128]), op=ALU.mult), r=[ident, lw], w=[ldiag])
        op("dve", lambda e: e.tensor_copy(out=wa.ap[:, :, :], in_=waf.ap[:, :, :]), r=[waf], w=[wa])
        op("dve", lambda e: e.tensor_copy(out=wx.ap[:, :, :], in_=wxf.ap[:, :, :]), r=[wxf], w=[wx])
        op("dve", lambda e: e.tensor_copy(out=brow.ap[0:1, :], in_=browf.ap[0:1, :]), r=[browf], w=[brow])
        op("dve", lambda e: e.tensor_copy(out=brow.ap[32:33, 0:1024], in_=browf.ap[32:33, 0:1024]), r=[browf], w=[brow])
        op("dve", lambda e: e.tensor_copy(out=brow.ap[64:65, 0:1024], in_=browf.ap[64:65, 0:1024]), r=[browf], w=[brow])
        op("act", lambda e: e.activation(out=zt_.ap[:, :], in_=cl.ap[:, :], func=AF.Exp, scale=-1.0), r=[cl], w=[zt_])
        op("dve", lambda e: e.tensor_scalar(out=cl.ap[:, :], in0=zt_.ap[:, :], scalar1=1.0 / 5, scalar2=-1.0 / 4,
                                            op0=ALU.mult, op1=ALU.add), r=[zt_], w=[cl])
        for cst in (1.0 / 3, -1.0 / 2, 1.0):
            op("dve", lambda e: e.tensor_tensor(out=cl.ap[:, :], in0=cl.ap[:, :], in1=zt_.ap[:, :], op=ALU.mult),
               r=[cl, zt_], w=[cl])
            op("dve", lambda e, cst=cst: e.tensor_scalar(out=cl.ap[:, :], in0=cl.ap[:, :], scalar1=cst, scalar2=None,
                                                         op0=ALU.add), r=[cl], w=[cl])
        op("dve", lambda e: e.tensor_tensor(out=cl.ap[:, :], in0=cl.ap[:, :], in1=zt_.ap[:, :], op=ALU.mult),
           r=[cl, zt_], w=[cl])
        op("dve", lambda e: e.tensor_scalar(out=cl.ap[:, :], in0=cl.ap[:, :], scalar1=-8.0, scalar2=None, op0=ALU.mult),
           r=[cl], w=[cl])
        al.off = work0
        hT = al([128, 8, 128], BF16)
        cfin = [al([128, 4, 160], BF16) for _ in range(2)]
        lrin = [al([128, 8, 132], BF16) for _ in range(2)]
        gs = [al([128, 8, 128], BF16) for _ in range(3)]
        ycT = [al([128, 4, 128], BF16) for _ in range(2)]
        hc = al([128, 4, 128], F32)
        sq = al([128, 4, 128], F32)
        mean = al([128, 128], F32)
        rstd = al([128, 128], F32)
        msq = rstd
        xrcs = [al([128, 8, 128], BF16) for _ in range(2)]
        ydT = al([128, 8, 128], BF16)
        R = al([128, 8, 128], F32)
        I = al([128, 8, 128], F32)
        A = al([128, 8, 128], F32)

        def v3(ap, inner):
            return ap.rearrange("p (a b) -> p a b", b=inner)

        def bc(ap2, n):
            return ap2.unsqueeze(2).broadcast_to([128, ap2.shape[1], n])

        def tile(t):
            c = t % TPS
            b = t // TPS
            par = t % 2
            cfin_, lrin_, gs_, ycT_, xrc = cfin[par], lrin[par], gs[t % 3], ycT[par], xrcs[par]
            cfinp, lrinp = cfin[1 - par], lrin[1 - par]
            xl = self.load_x(t, src, t % 4)
            yield
            self.transpose_mod(xl, l, 1, b, hT, 0)
            yield
            if c == 0:
                op("pool", lambda e: e.memset(cfin_.ap[:, :, 0:30], 0.0), w=[cfin_])
                op("pool", lambda e: e.memset(lrin_.ap[:, :, 0:3], 0.0), w=[lrin_])
            else:
                op("pool", lambda e: e.tensor_copy(out=cfin_.ap[:, :, 0:30], in_=cfinp.ap[:, :, 128:158]),
                   r=[cfinp], w=[cfin_])
                op("pool", lambda e: e.tensor_copy(out=lrin_.ap[:, :, 0:3], in_=lrinp.ap[:, :, 128:131]),
                   r=[lrinp], w=[lrin_])
            bks = []
            for q in range(6):
                bk = self.nbank()
                bks.append(bk)
                for i in range(4):
                    col = (q * 4 + i) * 128
                    for kc in range(8):
                        ws = self.nar(w_in, kc * OD_IN + col, 128, w_in.ap[:, kc, col:col + 128])
                        op("pe", lambda e, bk=bk, i=i, ws=ws, kc=kc: e.matmul(
                            bk.ap[:, i * 128:(i + 1) * 128], lhsT=ws.ap, rhs=hT.ap[:, kc, :],
                            start=(kc == 0), stop=(kc == 7)), r=[ws, hT], w=[bk])
                if q == 1:
                    op("act", lambda e, bk=bk: e.activation(out=cfin_.ap[:, :, 30:158], in_=v3(bk.ap[:, :], 128),
                                                            func=AF.Sigmoid), r=[bk], w=[cfin_])
                    op("dve", lambda e: e.tensor_tensor(out=cfin_.ap[:, :, 30:158], in0=v3(bks[0].ap[:, :], 128),
                                                        in1=cfin_.ap[:, :, 30:158], op=ALU.mult),
                       r=[bks[0], cfin_], w=[cfin_])
                elif q in (2, 3):
                    op("act", lambda e, bk=bk, q=q: e.activation(
                        out=lrin_.ap[:, (q - 2) * 4:(q - 1) * 4, 3:131], in_=v3(bk.ap[:, :], 128), func=AF.Identity),
                        r=[bk], w=[lrin_])
                elif q in (4, 5):
                    op("act", lambda e, bk=bk, q=q: e.activation(
                        out=gs_.ap[:, (q - 4) * 4:(q - 3) * 4, :], in_=v3(bk.ap[:, :], 128), func=AF.Gelu_apprx_tanh),
                        r=[bk], w=[gs_])
            yield
            fb = self.nbank()
            for cc in range(4):
                for k in range(31):
                    op("pe", lambda e, cc=cc, k=k: e.matmul(
                        fb.ap[:, cc * 128:(cc + 1) * 128], lhsT=fdiag.ap[:, cc * 31 + k, :],
                        rhs=cfin_.ap[:, cc, k:k + 128], start=(k == 0), stop=False), r=[fdiag, cfin_], w=[fb])
                op("pe", lambda e, cc=cc: e.matmul(
                    fb.ap[:, cc * 128:(cc + 1) * 128], lhsT=brow.ap[0:1, 1024 + cc * 128:1024 + (cc + 1) * 128],
                    rhs=onesb.ap[0:1, :], start=False, stop=True), r=[brow, onesb], w=[fb])
            op("act", lambda e: e.activation(out=hc.ap[:, :, :], in_=v3(fb.ap[:, :], 128), func=AF.Identity),
               r=[fb], w=[hc])
            op("act", lambda e: e.activation(out=sq.ap[:, :, :], in_=v3(fb.ap[:, :], 128), func=AF.Square),
               r=[fb], w=[sq])
            lb = [self.nbank(), self.nbank()]
            for cc in range(8):
                bk = lb[cc // 4]
                i = cc % 4
                for k in range(4):
                    op("pe", lambda e, bk=bk, i=i, cc=cc, k=k: e.matmul(
                        bk.ap[:, i * 128:(i + 1) * 128], lhsT=ldiag.ap[:, cc * 4 + k, :],
                        rhs=lrin_.ap[:, cc, k:k + 128], start=(k == 0), stop=False), r=[ldiag, lrin_], w=[bk])
                op("pe", lambda e, bk=bk, i=i, cc=cc: e.matmul(
                    bk.ap[:, i * 128:(i + 1) * 128], lhsT=brow.ap[0:1, cc * 128:(cc + 1) * 128],
                    rhs=onesb.ap[0:1, :], start=False, stop=True), r=[brow, onesb], w=[bk])
            for hh in range(2):
                op("act", lambda e, hh=hh: e.activation(out=xrc.ap[:, hh * 4:(hh + 1) * 4, :],
                                                        in_=v3(lb[hh].ap[:, :], 128), func=AF.Identity),
                   r=[lb[hh]], w=[xrc])
            yield
            sb_ = self.nbank()
            for cc in range(4):
                op("pe", lambda e, cc=cc: e.matmul(sb_.ap[:, 0:128], lhsT=ones.ap[:, :], rhs=hc.ap[:, cc, :],
                                                   start=(cc == 0), stop=(cc == 3)), r=[ones, hc], w=[sb_])
            for cc in range(4):
                op("pe", lambda e, cc=cc: e.matmul(sb_.ap[:, 128:256], lhsT=ones.ap[:, :], rhs=sq.ap[:, cc, :],
                                                   start=(cc == 0), stop=(cc == 3)), r=[ones, sq], w=[sb_])
            for (wg, prow, dstg) in ((wa, 32, R), (wx, 64, I)):
                gb = [self.nbank(), self.nbank()]
                for h in range(8):
                    bk = gb[h // 4]
                    i = h % 4
                    op("pe", lambda e, bk=bk, i=i, h=h, wg=wg: e.matmul(
                        bk.ap[:, i * 128:(i + 1) * 128], lhsT=wg.ap[:, h, :], rhs=xrc.ap[:, h, :],
                        start=True, stop=False), r=[wg, xrc], w=[bk])
                    op("pe", lambda e, bk=bk, i=i, h=h, prow=prow: e.matmul(
                        bk.ap[:, i * 128:(i + 1) * 128], lhsT=brow.ap[prow:prow + 1, h * 128:(h + 1) * 128],
                        rhs=onesb.ap[prow:prow + 1, :], start=False, stop=True), r=[brow, onesb], w=[bk])
                for hh in range(2):
                    op("act", lambda e, hh=hh, gb=gb, dstg=dstg: e.activation(
                        out=dstg.ap[:, hh * 4:(hh + 1) * 4, :], in_=v3(gb[hh].ap[:, :], 128), func=AF.Sigmoid),
                        r=[gb[hh]], w=[dstg])
            op("dve", lambda e: e.tensor_scalar(out=mean.ap[:, :], in0=sb_.ap[:, 0:128], scalar1=1.0 / 512, scalar2=None,
                                                op0=ALU.mult), r=[sb_], w=[mean])
            op("dve", lambda e: e.tensor_tensor(out=msq.ap[:, :], in0=mean.ap[:, :], in1=mean.ap[:, :], op=ALU.mult),
               r=[mean], w=[msq])
            op("dve", lambda e: e.scalar_tensor_tensor(out=rstd.ap[:, :], in0=sb_.ap[:, 128:256], scalar=1.0 / 512,
                                                       in1=msq.ap[:, :], op0=ALU.mult, op1=ALU.subtract),
               r=[sb_, msq], w=[rstd])
            op("pool", lambda e: e.tensor_scalar(out=rstd.ap[:, :], in0=rstd.ap[:, :], scalar1=EPS, scalar2=None,
                                                 op0=ALU.add), r=[rstd], w=[rstd])
            op("pool", lambda e: e.tensor_tensor(out=rstd.ap[:, :], in0=rstd.ap[:, :], in1=self.m05.ap[:, :],
                                                 op=ALU.pow), r=[rstd, self.m05], w=[rstd])
            op("dve", lambda e: e.tensor_tensor(out=hc.ap[:, :, :], in0=hc.ap[:, :, :],
                                                in1=mean.ap[:, :].unsqueeze(1).broadcast_to([128, 4, 128]),
                                                op=ALU.subtract), r=[hc, mean], w=[hc])
            op("dve", lambda e: e.tensor_tensor(out=hc.ap[:, :, :], in0=hc.ap[:, :, :],
                                                in1=rstd.ap[:, :].unsqueeze(1).broadcast_to([128, 4, 128]),
                                                op=ALU.mult), r=[hc, rstd], w=[hc])
            for cc in range(4):
                op("act", lambda e, cc=cc: e.activation(out=ycT_.ap[:, cc, :], in_=hc.ap[:, cc, :], func=AF.Silu,
                                                        bias=lnb.ap[:, cc:cc + 1], scale=lng.ap[:, cc:cc + 1]),
                   r=[hc, lng, lnb], w=[ycT_])
            yield
            if c == 0:
                op("pool", lambda e: e.memset(hprev.ap[:, :], 0.0), w=[hprev])
            op("dve", lambda e: e.tensor_tensor(out=R.ap[:, :, :], in0=R.ap[:, :, :], in1=bc(cl.ap[:, :], 128),
                                                op=ALU.mult), r=[R, cl], w=[R])
            op("act", lambda e: e.activation(out=A.ap[:, :, :], in_=R.ap[:, :, :], func=AF.Exp), r=[R], w=[A])
            op("act", lambda e: e.activation(out=R.ap[:, :, :], in_=R.ap[:, :, :], func=AF.Exp, scale=2.0),
               r=[R], w=[R])
            op("act", lambda e: e.activation(out=R.ap[:, :, :], in_=R.ap[:, :, :], func=AF.Ln, scale=-1.0, bias=1.0),
               r=[R], w=[R])
            op("act", lambda e: e.activation(out=R.ap[:, :, :], in_=R.ap[:, :, :], func=AF.Exp, scale=0.5),
               r=[R], w=[R])
            op("dve", lambda e: e.tensor_tensor(out=I.ap[:, :, :], in0=I.ap[:, :, :], in1=xrc.ap[:, :, :], op=ALU.mult),
               r=[I, xrc], w=[I])
            op("dve", lambda e: e.tensor_tensor(out=I.ap[:, :, :], in0=I.ap[:, :, :], in1=R.ap[:, :, :], op=ALU.mult),
               r=[I, R], w=[I])
            for cc in range(8):
                op("dve", lambda e, cc=cc: e.tensor_tensor_scan(
                    out=R.ap[:, cc, :], data0=A.ap[:, cc, :], data1=I.ap[:, cc, :], initial=hprev.ap[:, cc:cc + 1],
                    op0=ALU.mult, op1=ALU.add), r=[A, I, hprev], w=[R])
            op("dve", lambda e: e.tensor_copy(out=hprev.ap[:, :], in_=R.ap[:, :, 127]), r=[R], w=[hprev])
            op("pool", lambda e: e.tensor_tensor(out=ydT.ap[:, :, :], in0=R.ap[:, :, :], in1=gs_.ap[:, :, :], op=ALU.mult),
               r=[R, gs_], w=[ydT])
            yg = Tl(None, [self.ybuf_g[t]])
            op("act", lambda e: e.dma_start(out=self.dr["yT"][t, :, 0:512], in_=ycT_.ap[:, :, :].rearrange("p a b -> p (a b)")),
               r=[ycT_], w=[yg], dma=True)
            op("pool", lambda e: e.dma_start(out=self.dr["yT"][t, :, 512:1536], in_=ydT.ap[:, :, :].rearrange("p a b -> p (a b)")),
               r=[ydT], w=[yg], dma=True)
            yield

        self.pipeline([tile(t) for t in range(NT)])
        self.mixer_out(l, src, dst, dr["od_w_out"][o_])


def host_layout(inp, n_cores, seq):
    f = lambda a: np.ascontiguousarray(np.asarray(a, dtype=np.float32))
    shared = {}
    shared["ada_w"] = f(inp["ada_w"])
    shared["ada_b"] = f(inp["ada_b"])
    shared["ada_bT"] = f(np.asarray(inp["ada_b"]).reshape(L_FULL, 72, 128).transpose(0, 2, 1))
    for k in ("ln_g", "ln_b", "ffn_w_in", "ffn_w_out", "ev_w_in", "ssd_conv_b", "ssd_dt_bias", "ssd_a_log",
              "ssd_d", "pool_scale", "ev_w_out", "od_w_in", "conf_dw_b", "lru_conv_b",
              "lru_ba", "lru_bx", "od_w_out"):
        shared[k] = f(inp[k])
    shared["ssd_conv_wT"] = f(np.asarray(inp["ssd_conv_w"]).reshape(-1, 4, 12, 128).transpose(0, 3, 2, 1))
    shared["ssd_norm_gT"] = f(np.asarray(inp["ssd_norm_g"]).reshape(-1, 8, 128).transpose(0, 2, 1))
    shared["pool_wT"] = f(np.asarray(inp["pool_w"]).transpose(0, 2, 1, 3))
    shared["conf_dw_wT"] = f(np.asarray(inp["conf_dw_w"]).reshape(-1, 31, 4, 128).transpose(0, 3, 2, 1))
    shared["conf_ln_gT"] = f(np.asarray(inp["conf_ln_g"]).reshape(-1, 4, 128).transpose(0, 2, 1))
    shared["conf_ln_bT"] = f(np.asarray(inp["conf_ln_b"]).reshape(-1, 4, 128).transpose(0, 2, 1))
    shared["lru_conv_wT"] = f(np.asarray(inp["lru_conv_w"]).reshape(-1, 4, 8, 128).transpose(0, 3, 2, 1))
    shared["lru_waT"] = f(np.asarray(inp["lru_wa"]).transpose(0, 2, 1, 3))
    shared["lru_wxT"] = f(np.asarray(inp["lru_wx"]).transpose(0, 2, 1, 3))
    shared["lru_lamT"] = f(np.asarray(inp["lru_lambda"]).reshape(-1, 8, 128).transpose(0, 2, 1))
    x = np.asarray(inp["x"], dtype=np.float32)
    c = np.asarray(inp["c"], dtype=np.float32)
    maps = []
    for i in range(n_cores):
        m = dict(shared)
        m["x"] = np.ascontiguousarray(x[2 * i:2 * i + 2].reshape(2 * seq, D))
        m["cT"] = np.ascontiguousarray(c[2 * i:2 * i + 2].reshape(2, 8, 128).transpose(2, 1, 0))
        maps.append(m)
    return maps


_NC_CACHE = {}


def kernel(**inputs):
    x = np.asarray(inputs["x"])
    bsz, seq, _ = x.shape
    n_cores = bsz // 2
    maps = host_layout(inputs, n_cores, seq)
    key = (seq,)
    if key not in _NC_CACHE:
        _NC_CACHE[key] = Builder(seq, list(range(L_FULL))).build()
    nc = _NC_CACHE[key]
    res = run_bass_kernel_spmd(nc, maps, core_ids=list(range(n_cores)))
    outs = [np.asarray(r["out"]).reshape(2, seq, D) for r in res.results]
    return np.concatenate(outs, axis=0).astype(np.float32)
```

```python
import numpy as np
from contextlib import ExitStack
import concourse.bass as bass
import concourse.mybir as mybir
from concourse.bass_utils import run_bass_kernel_spmd

F32 = mybir.dt.float32
BF16 = mybir.dt.bfloat16
AF = mybir.ActivationFunctionType
ALU = mybir.AluOpType

D = 1024
DFF = 2816
NJ = 22
L_FULL = 4
ALPHA = float((2.0 * 4) ** 0.25)
EPS = 1e-5
EV_IN = 3088
OD_IN = 3072
POOL_W = (2, 4, 8, 16)

ENGS = ("pe", "act", "dve", "pool", "sp")
N_DMA_SEMS = 8


class Buf:
    __slots__ = ("name", "last_w", "readers")

    def __init__(self, name=""):
        self.name = name
        self.last_w = None
        self.readers = []


class Op:
    __slots__ = ("eng", "fn", "dma", "deps", "signal", "ordinal", "waits", "dsem", "dval", "prewait")

    def __init__(self, eng, fn, dma):
        self.eng = eng
        self.fn = fn
        self.dma = dma
        self.deps = []
        self.signal = False
        self.ordinal = 0
        self.waits = []
        self.dsem = None
        self.dval = 0
        self.prewait = None


class Prog:
    def __init__(self, nc):
        self.nc = nc
        self.ops = {e: [] for e in ENGS}
        self.all = []

    def add(self, eng, fn, reads=(), writes=(), dma=False):
        op = Op(eng, fn, dma)
        deps = op.deps
        for b in reads:
            if b.last_w is not None:
                deps.append((b.last_w, True))
        for b in writes:
            if b.last_w is not None:
                deps.append((b.last_w, False))
            for r in b.readers:
                deps.append((r, False))
        for b in reads:
            b.readers.append(op)
        for b in writes:
            b.last_w = op
            b.readers = []
        self.ops[eng].append(op)
        self.all.append(op)
        return op

    def finalize(self):
        for op in self.all:
            need = []
            for (p, raw) in op.deps:
                if p is op:
                    continue
                if p.dma or p.eng != op.eng:
                    need.append(p)
                elif op.dma or op.eng != "pe":
                    need.append(p)
            op.deps = need
            for p in need:
                if not p.dma:
                    p.signal = True
        for e in ENGS:
            n = 0
            for op in self.ops[e]:
                if (not op.dma) and op.signal:
                    n += 1
                    op.ordinal = n
        for e in ENGS:
            k = 0
            cnt = [0] * N_DMA_SEMS
            for op in self.ops[e]:
                if not op.dma:
                    continue
                s = k % N_DMA_SEMS
                k += 1
                if cnt[s] > 0:
                    op.prewait = (("dma", e, s), cnt[s] * 16)
                cnt[s] += 1
                op.dsem = ("dma", e, s)
                op.dval = cnt[s] * 16
        for e in ENGS:
            sn = {}
            for op in self.ops[e]:
                w = {}
                if op.prewait is not None:
                    k, v = op.prewait
                    w[k] = v
                for p in op.deps:
                    if p.dma:
                        k, v = p.dsem, p.dval
                    else:
                        k, v = ("eng", p.eng), p.ordinal
                    if w.get(k, 0) < v:
                        w[k] = v
                for k, v in w.items():
                    if sn.get(k, 0) >= v:
                        continue
                    sn[k] = v
                    op.waits.append((k, v))

    def emit(self):
        nc = self.nc
        self.finalize()
        with ExitStack() as st:
            sems = {}
            for e in ENGS:
                if any(op.signal for op in self.ops[e]):
                    sems[("eng", e)] = st.enter_context(nc.semaphore("s_" + e))
                if any(op.dma for op in self.ops[e]):
                    for s in range(N_DMA_SEMS):
                        sems[("dma", e, s)] = st.enter_context(nc.semaphore("d_%s%d" % (e, s)))
            block = st.enter_context(nc.Block())

            def run(engname):
                def body(eng):
                    for op in self.ops[engname]:
                        for (k, v) in op.waits:
                            eng.wait_ge(sems[k], v)
                        ins = op.fn(eng)
                        if op.dma:
                            ins.then_inc(sems[op.dsem], 16)
                        elif op.signal:
                            ins.then_inc(sems[("eng", engname)], 1)
                    last = {}
                    for op in self.ops[engname]:
                        if op.dma:
                            last[op.dsem] = op.dval
                    for k, v in last.items():
                        eng.wait_ge(sems[k], v)
                return body

            block.sync(run("sp"))
            block.scalar(run("act"))
            block.vector(run("dve"))
            block.gpsimd(run("pool"))
            block.tensor(run("pe"))


class Tl:
    __slots__ = ("ap", "bufs", "off", "esz")

    def __init__(self, ap, bufs, off=0, esz=4):
        self.ap = ap
        self.bufs = bufs
        self.off = off
        self.esz = esz

    def __getitem__(self, k):
        return self.ap[k]


PAGE = 2048


class Builder:
    def __init__(self, seq, layers, n_sub=3):
        self.SEQ = seq
        self.layers = layers
        self.n_sub = n_sub
        self.NT = 2 * seq // 128
        self.TPS = seq // 128
        self.nc = bass.Bass("TRN2", target_bir_lowering=False)
        self.P = Prog(self.nc)
        self.st = ExitStack()
        self.rr = 0

    def op(self, eng, fn, r=(), w=(), dma=False):
        rb = []
        for t in r:
            rb.extend(t.bufs)
        wb = []
        for t in w:
            wb.extend(t.bufs)
        return self.P.add(eng, fn, rb, wb, dma)

    def sb(self, name, shape, dt):
        t = self.st.enter_context(self.nc.sbuf_tensor(name, shape, dt))
        return Tl(t, [Buf(name)])

    def ps(self, name):
        t = self.st.enter_context(self.nc.psum_tensor(name, [128, 1024], F32))
        return (Tl(t[:, 0:512], [Buf(name + "a")]), Tl(t[:, 512:1024], [Buf(name + "b")]),
                Tl(t[:, :], None))

    def dram(self, name, shape, dt, kind="Internal"):
        return self.nc.dram_tensor(name, shape, dt, kind=kind).ap()

    def av(self, off, shape, dt):
        n = 1
        for s in shape[1:]:
            n *= s
        esz = 4 if dt == F32 else 2
        nbytes = n * esz
        assert off % 4 == 0 and nbytes % 4 == 0
        assert off + nbytes <= self.arena_bytes, (off, nbytes, self.arena_bytes)
        ap = self.arena[0:shape[0], off // 4:(off + nbytes) // 4]
        if dt != F32:
            ap = ap.bitcast(dt)
        if len(shape) == 3:
            ap = ap.rearrange("p (a b) -> p a b", b=shape[2])
        elif len(shape) == 4:
            ap = ap.rearrange("p (a b c) -> p a b c", b=shape[2], c=shape[3])
        bufs = self.pages[off // PAGE:(off + nbytes - 1) // PAGE + 1]
        return Tl(ap, bufs, off, esz)

    def nar(self, tl, e0, n, ap):
        o0 = tl.off + e0 * tl.esz
        o1 = tl.off + (e0 + n) * tl.esz - 1
        return Tl(ap, self.pages[o0 // PAGE:o1 // PAGE + 1], o0, tl.esz)

    class Alloc:
        def __init__(self, b):
            self.b = b
            self.off = 0

        def __call__(self, shape, dt):
            n = 1
            for s in shape[1:]:
                n *= s
            nbytes = n * (4 if dt == F32 else 2)
            nbytes = (nbytes + 3) // 4 * 4
            t = self.b.av(self.off, shape, dt)
            self.off += nbytes
            return t

        def align(self):
            self.off = (self.off + PAGE - 1) // PAGE * PAGE

    def sub(self, tl, ap):
        return Tl(ap, tl.bufs)

    def build(self):
        nc = self.nc
        SEQ, NT = self.SEQ, self.NT
        L = len(self.layers)
        NE = (L_FULL + 1) // 2
        NO = L_FULL // 2
        dr = {}
        dr["x"] = self.dram("x", [NT * 128, D], F32, "ExternalInput")
        dr["cT"] = self.dram("cT", [128, 8, 2], F32, "ExternalInput")
        dr["ada_w"] = self.dram("ada_w", [L_FULL, D, 9 * D], F32, "ExternalInput")
        dr["ada_bT"] = self.dram("ada_bT", [L_FULL, 128, 72], F32, "ExternalInput")
        dr["ada_b"] = self.dram("ada_b", [L_FULL, 9 * D], F32, "ExternalInput")
        dr["ln_g"] = self.dram("ln_g", [L_FULL, 3, D], F32, "ExternalInput")
        dr["ln_b"] = self.dram("ln_b", [L_FULL, 3, D], F32, "ExternalInput")
        dr["ffn_w_in"] = self.dram("ffn_w_in", [L_FULL, 2, D, 2 * DFF], F32, "ExternalInput")
        dr["ffn_w_out"] = self.dram("ffn_w_out", [L_FULL, 2, DFF, D], F32, "ExternalInput")
        dr["ev_w_in"] = self.dram("ev_w_in", [NE, D, EV_IN], F32, "ExternalInput")
        dr["ssd_conv_wT"] = self.dram("ssd_conv_wT", [NE, 128, 12, 4], F32, "ExternalInput")
        dr["ssd_conv_b"] = self.dram("ssd_conv_b", [NE, 1536], F32, "ExternalInput")
        dr["ssd_dt_bias"] = self.dram("ssd_dt_bias", [NE, 16], F32, "ExternalInput")
        dr["ssd_a_log"] = self.dram("ssd_a_log", [NE, 16], F32, "ExternalInput")
        dr["ssd_d"] = self.dram("ssd_d", [NE, 16], F32, "ExternalInput")
        dr["ssd_norm_gT"] = self.dram("ssd_norm_gT", [NE, 128, 8], F32, "ExternalInput")
        dr["pool_wT"] = self.dram("pool_wT", [NE, 128, 4, 128], F32, "ExternalInput")
        dr["pool_scale"] = self.dram("pool_scale", [NE, 512], F32, "ExternalInput")
        dr["ev_w_out"] = self.dram("ev_w_out", [NE, 1536, D], F32, "ExternalInput")
        dr["od_w_in"] = self.dram("od_w_in", [NO, D, OD_IN], F32, "ExternalInput")
        dr["od_w_out"] = self.dram("od_w_out", [NO, 1536, D], F32, "ExternalInput")
        dr["conf_dw_wT"] = self.dram("conf_dw_wT", [NO, 128, 4, 31], F32, "ExternalInput")
        dr["conf_dw_b"] = self.dram("conf_dw_b", [NO, 512], F32, "ExternalInput")
        dr["conf_ln_gT"] = self.dram("conf_ln_gT", [NO, 128, 4], F32, "ExternalInput")
        dr["conf_ln_bT"] = self.dram("conf_ln_bT", [NO, 128, 4], F32, "ExternalInput")
        dr["lru_conv_wT"] = self.dram("lru_conv_wT", [NO, 128, 8, 4], F32, "ExternalInput")
        dr["lru_conv_b"] = self.dram("lru_conv_b", [NO, D], F32, "ExternalInput")
        dr["lru_waT"] = self.dram("lru_waT", [NO, 128, 8, 128], F32, "ExternalInput")
        dr["lru_ba"] = self.dram("lru_ba", [NO, D], F32, "ExternalInput")
        dr["lru_wxT"] = self.dram("lru_wxT", [NO, 128, 8, 128], F32, "ExternalInput")
        dr["lru_bx"] = self.dram("lru_bx", [NO, D], F32, "ExternalInput")
        dr["lru_lamT"] = self.dram("lru_lamT", [NO, 128, 8], F32, "ExternalInput")
        dr["out"] = self.dram("out", [NT * 128, D], F32, "ExternalOutput")
        dr["xs"] = self.dram("xs", [NT * 128, D], F32)
        dr["gates"] = self.dram("gates", [L_FULL, 3, 2, D], F32)
        dr["yT"] = self.dram("yT", [NT, 128, 1536], BF16)
        self.ybuf_g = [Buf("yT%d" % t) for t in range(NT)]
        self.dr = dr
        self.xbuf = [Buf("xd%d" % t) for t in range(NT)]
        self.gates_buf = Buf("gates")

        self.ident = self.sb("ident", [128, 128], F32)
        self.ones = self.sb("ones", [128, 128], F32)
        self.tri = self.sb("tri", [128, 128], F32)
        self.ustr = self.sb("ustr", [128, 128], F32)
        self.m05 = self.sb("m05", [128, 128], F32)
        self.onesb = self.sb("onesb", [128, 128], BF16)
        self.ones1 = self.sb("ones1", [1, 128], F32)
        self.condT = self.sb("condT", [128, 8, 2], F32)
        self.modT = self.sb("modT", [128, L_FULL, 6, 8, 2], F32)
        self.adabT = self.sb("adabT", [128, L_FULL, 72], F32)
        self.xl = [self.sb("xl%d" % i, [128, D], F32) for i in range(4)]
        self.xe = [self.sb("xe%d" % i, [128, D], F32) for i in range(2)]
        self.zt = [self.sb("zt%d" % i, [128, D], F32) for i in range(2)]
        self.gate_row = self.sb("gate_row", [128, D], F32)
        self.lng_row = self.sb("lng_row", [128, D], F32)
        self.lnb_row = self.sb("lnb_row", [128, D], F32)
        self.bst = [self.sb("bst%d" % i, [128, 2, 6], F32) for i in range(2)]
        self.mv = [self.sb("mv%d" % i, [128, 2], F32) for i in range(2)]
        self.rstd = [self.sb("rstd%d" % i, [128, 2], F32) for i in range(2)]
        self.pst = [self.ps("ps%d" % i) for i in range(4)]
        self.banks = []
        for p in self.pst:
            self.banks.append(p[0])
            self.banks.append(p[1])
        rem = int(nc.sbuf_bytes_remaining) - 256
        self.arena_bytes = rem // PAGE * PAGE
        t = self.st.enter_context(nc.sbuf_tensor("arena", [128, self.arena_bytes // 4], F32))
        self.arena = t
        self.pages = [Buf("pg%d" % i) for i in range(self.arena_bytes // PAGE)]

        self.setup_consts()
        for _ in self.mods_gen(self.layers[0], self.Alloc(self)):
            pass
        src = "x"
        nsub_total = L * self.n_sub
        k = 0
        for li, l in enumerate(self.layers):
            for s in range(self.n_sub):
                k += 1
                dst = "out" if k == nsub_total else "xs"
                if s == 0:
                    self.ffn(l, 0, src, dst)
                elif s == 1:
                    if l % 2 == 0:
                        self.even_mixer(l, src, dst)
                    else:
                        self.odd_mixer(l, src, dst)
                else:
                    self.ffn(l, 1, src, dst)
                src = dst
        self.P.emit()
        self.st.close()
        return nc

    def setup_consts(self):
        op = self.op
        ident, ones, tri, ustr, m05 = self.ident, self.ones, self.tri, self.ustr, self.m05
        op("pool", lambda e: e.memset(ones.ap[:], 1.0), w=[ones])
        op("pool", lambda e: e.memset(m05.ap[:], -0.5), w=[m05])
        op("pool", lambda e: e.memset(self.onesb.ap[:], 1.0), w=[self.onesb])
        op("pool", lambda e: e.memset(self.ones1.ap[:], 1.0), w=[self.ones1])
        op("pool", lambda e: e.affine_select(out=ident.ap[:], in_=ones.ap[:], pattern=[[-1, 128]],
                                             compare_op=ALU.is_equal, fill=0.0, base=0, channel_multiplier=1),
           r=[ones], w=[ident])
        op("pool", lambda e: e.affine_select(out=tri.ap[:], in_=ones.ap[:], pattern=[[1, 128]],
                                             compare_op=ALU.is_ge, fill=0.0, base=0, channel_multiplier=-1),
           r=[ones], w=[tri])
        op("pool", lambda e: e.affine_select(out=ustr.ap[:], in_=ones.ap[:], pattern=[[-1, 128]],
                                             compare_op=ALU.is_gt, fill=0.0, base=0, channel_multiplier=1),
           r=[ones], w=[ustr])
        op("sp", lambda e: e.dma_start(out=self.condT.ap[:], in_=self.dr["cT"][:, :, :]), w=[self.condT], dma=True)
        op("act", lambda e: e.activation(out=self.condT.ap[:], in_=self.condT.ap[:], func=AF.Silu),
           r=[self.condT], w=[self.condT])
        op("sp", lambda e: e.dma_start(out=self.adabT.ap[:], in_=self.dr["ada_bT"].rearrange("l p c -> p l c")),
           w=[self.adabT], dma=True)

    def nbank(self):
        b = self.banks[self.rr % 8]
        self.rr += 1
        return b

    def mods_gen(self, l, al):
        op = self.op
        stage = [al([128, 8, 512], F32) for _ in range(2)]
        al.align()
        grow = [al([2, 512], F32) for _ in range(2)]
        brow = [al([2, 512], F32) for _ in range(2)]
        condT = self.condT
        k = 0
        if True:
            def fetch(cg_):
                stg_ = stage[cg_ % 2]
                for kc in range(8):
                    op("sp", lambda e, stg_=stg_, kc=kc, cg_=cg_: e.dma_start(
                        out=stg_.ap[:, kc, :], in_=self.dr["ada_w"][l, kc * 128:(kc + 1) * 128, cg_ * 512:(cg_ + 1) * 512]),
                        w=[stg_], dma=True)

            fetch(0)
            for cg in range(18):
                stg = stage[cg % 2]
                k += 1
                v, hf = cg // 2, cg % 2
                if cg >= 1 and cg + 1 < 18:
                    pass
                if v % 3 == 2:
                    j = v // 3
                    bk = self.nbank()
                    for kc in range(8):
                        op("pe", lambda e, bk=bk, stg=stg, kc=kc: e.matmul(
                            bk.ap[0:2, :], lhsT=condT.ap[:, kc, :], rhs=stg.ap[:, kc, :],
                            start=(kc == 0), stop=(kc == 7)), r=[condT, stg], w=[bk])
                    g_, b_ = grow[hf], brow[hf]
                    for bb in range(2):
                        op("sp", lambda e, b_=b_, l=l, cg=cg, bb=bb: e.dma_start(
                            out=b_.ap[bb:bb + 1, :], in_=self.dr["ada_b"][l:l + 1, cg * 512:(cg + 1) * 512]),
                            w=[b_], dma=True)
                    op("dve", lambda e, g_=g_, bk=bk, b_=b_: e.tensor_tensor(
                        out=g_.ap[:, :], in0=bk.ap[0:2, :], in1=b_.ap[:, :], op=ALU.add), r=[bk, b_], w=[g_])
                    mul = 1.0 if j == 1 else 0.5
                    op("dve", lambda e, g_=g_, mul=mul: e.tensor_scalar(
                        out=g_.ap[:, :], in0=g_.ap[:, :], scalar1=1.0, scalar2=mul, op0=ALU.add, op1=ALU.mult),
                        r=[g_], w=[g_])
                    op("pool", lambda e, g_=g_, l=l, j=j, hf=hf: e.dma_start(
                        out=self.dr["gates"][l, j, :, hf * 512:(hf + 1) * 512], in_=g_.ap[:, :]),
                        r=[g_], w=[Tl(None, [self.gates_buf])], dma=True)
                else:
                    j = v // 3
                    vi = j * 2 + (v % 3)
                    bk = self.nbank()
                    for fc in range(4):
                        for kc in range(8):
                            op("pe", lambda e, bk=bk, stg=stg, kc=kc, fc=fc: e.matmul(
                                bk.ap[:, fc * 2:fc * 2 + 2], lhsT=stg.ap[:, kc, fc * 128:(fc + 1) * 128],
                                rhs=condT.ap[:, kc, :], start=(kc == 0), stop=(kc == 7)),
                                r=[condT, stg], w=[bk])
                    for fc in range(4):
                        ch = hf * 4 + fc
                        addc = 1.0 if (v % 3) == 1 else 0.0
                        op("dve", lambda e, bk=bk, l=l, vi=vi, ch=ch, fc=fc, cg=cg, addc=addc: e.tensor_scalar(
                            out=self.modT.ap[:, l, vi, ch, :], in0=bk.ap[:, fc * 2:fc * 2 + 2],
                            scalar1=self.adabT.ap[:, l, cg * 4 + fc:cg * 4 + fc + 1], scalar2=addc,
                            op0=ALU.add, op1=ALU.add), r=[bk, self.adabT], w=[self.modT])
                if cg + 1 < 18:
                    fetch(cg + 1)
                yield

    def load_rows(self, l, j, b):
        op = self.op
        op("sp", lambda e: e.dma_start(out=self.gate_row.ap[:, :],
                                       in_=self.dr["gates"][l, j, b, :].partition_broadcast(128)),
           r=[Tl(None, [self.gates_buf])], w=[self.gate_row], dma=True)
        if b == 0:
            op("sp", lambda e: e.dma_start(out=self.lng_row.ap[:, :],
                                           in_=self.dr["ln_g"][l, j, :].partition_broadcast(128)),
               w=[self.lng_row], dma=True)
            op("sp", lambda e: e.dma_start(out=self.lnb_row.ap[:, :],
                                           in_=self.dr["ln_b"][l, j, :].partition_broadcast(128)),
               w=[self.lnb_row], dma=True)

    def load_x(self, t, src, slot):
        xl = self.xl[slot]
        self.op("sp", lambda e: e.dma_start(out=xl.ap[:, :], in_=self.dr[src][t * 128:(t + 1) * 128, :]),
                r=[Tl(None, [self.xbuf[t]])], w=[xl], dma=True)
        return xl

    def transpose_mod(self, xl, l, j, b, hT, col0):
        op = self.op
        for half in range(2):
            bk = self.nbank()
            for q in range(4):
                kc = half * 4 + q
                op("pe", lambda e, bk=bk, q=q, kc=kc: e.transpose(
                    out=bk.ap[:, q * 128:(q + 1) * 128], in_=xl.ap[:, kc * 128:(kc + 1) * 128],
                    identity=self.ident.ap[:, :]), r=[xl, self.ident], w=[bk])
            for q in range(4):
                kc = half * 4 + q
                op("act", lambda e, bk=bk, q=q, kc=kc: e.activation(
                    out=hT.ap[:, kc, col0:col0 + 128], in_=bk.ap[:, q * 128:(q + 1) * 128], func=AF.Identity,
                    bias=self.modT.ap[:, l, 2 * j, kc, b:b + 1], scale=self.modT.ap[:, l, 2 * j + 1, kc, b:b + 1]),
                    r=[bk, self.modT], w=[hT])

    def epilogue(self, t, l, j, src, dst, ybanks, par):
        op = self.op
        xe, zt, bst, mv, rs = self.xe[par], self.zt[par], self.bst[par], self.mv[par], self.rstd[par]
        xg = Tl(None, [self.xbuf[t]])
        op("sp", lambda e: e.dma_start(out=xe.ap[:, :], in_=self.dr[src][t * 128:(t + 1) * 128, :]),
           r=[xg], w=[xe], dma=True)
        for h in range(2):
            op("dve", lambda e, h=h: e.tensor_tensor(out=zt.ap[:, h * 512:(h + 1) * 512], in0=ybanks[h].ap[:, :],
                                                     in1=self.gate_row.ap[:, h * 512:(h + 1) * 512], op=ALU.mult),
               r=[ybanks[h], self.gate_row], w=[zt])
        op("dve", lambda e: e.scalar_tensor_tensor(out=zt.ap[:, :], in0=xe.ap[:, :], scalar=ALPHA, in1=zt.ap[:, :],
                                                   op0=ALU.mult, op1=ALU.add), r=[xe, zt], w=[zt])
        for h in range(2):
            op("dve", lambda e, h=h: e.bn_stats(out=bst.ap[:, h, :], in_=zt.ap[:, h * 512:(h + 1) * 512]),
               r=[zt], w=[bst])
        op("dve", lambda e: e.bn_aggr(out=mv.ap[:, :], in_=bst.ap[:, :, :].rearrange("p a b -> p (a b)")),
           r=[bst], w=[mv])
        op("pool", lambda e: e.tensor_scalar(out=rs.ap[:, 0:1], in0=mv.ap[:, 1:2], scalar1=EPS, scalar2=None,
                                             op0=ALU.add), r=[mv], w=[rs])
        op("pool", lambda e: e.tensor_tensor(out=rs.ap[:, 0:1], in0=rs.ap[:, 0:1], in1=self.m05.ap[:, 0:1],
                                             op=ALU.pow), r=[rs, self.m05], w=[rs])
        op("dve", lambda e: e.scalar_tensor_tensor(out=zt.ap[:, :], in0=zt.ap[:, :], scalar=mv.ap[:, 0:1],
                                                   in1=self.lng_row.ap[:, :], op0=ALU.subtract, op1=ALU.mult),
           r=[zt, mv, self.lng_row], w=[zt])
        op("act", lambda e: e.activation(out=xe.ap[:, :], in_=zt.ap[:, :], func=AF.Identity, scale=rs.ap[:, 0:1]),
           r=[zt, rs], w=[xe])
        op("pool", lambda e: e.tensor_tensor(out=xe.ap[:, :], in0=xe.ap[:, :], in1=self.lnb_row.ap[:, :],
                                             op=ALU.add), r=[xe, self.lnb_row], w=[xe])
        op("pool", lambda e: e.dma_start(out=self.dr[dst][t * 128:(t + 1) * 128, :], in_=xe.ap[:, :]),
           r=[xe], w=[xg], dma=True)

    def pipeline(self, gens, extra=None, every=3):
        live = []
        n = len(gens)
        i = 0
        step = 0
        while i < n or live:
            step += 1
            if extra is not None and step % every == 0:
                try:
                    next(extra)
                except StopIteration:
                    extra = None
            if i < n:
                live.append(gens[i])
                i += 1
            nxt = []
            for g in live:
                try:
                    next(g)
                    nxt.append(g)
                except StopIteration:
                    pass
            live = nxt
        if extra is not None:
            for _ in extra:
                pass

    def ffn(self, l, k, src, dst):
        op = self.op
        j = 0 if k == 0 else 2
        G = 256
        NGT = G // 128
        NG = self.NT // NGT
        al = self.Alloc(self)
        w_in = al([128, 8, 2 * DFF], BF16)
        al.align()
        w_out = al([128, NJ, D], BF16)
        al.align()
        hT = al([128, 8, G], BF16)
        al.align()
        actT = al([128, NJ, G], BF16)
        al.align()
        sg = [al([128, G], BF16) for _ in range(2)]
        wi_d = self.dr["ffn_w_in"]
        wo_d = self.dr["ffn_w_out"]
        for q in range(6):
            wdt = 512 if q < 5 else 256
            for half in range(2):
                c0 = half * DFF + q * 512
                for kc in range(8):
                    off = (kc * 2 * DFF + c0) * 2
                    dst_t = Tl(w_in.ap[:, kc, c0:c0 + wdt], self.pages[off // PAGE:(off + wdt * 2 - 1) // PAGE + 1])
                    op("pool", lambda e, dst_t=dst_t, kc=kc, c0=c0, wdt=wdt: e.dma_start(
                        out=dst_t.ap, in_=wi_d[l, k, kc * 128:(kc + 1) * 128, c0:c0 + wdt]),
                        w=[dst_t], dma=True)
        wo_base = w_out.bufs
        for jj in range(NJ):
            for hh in range(2):
                off0 = (self.arena_off(w_out)) + (jj * D + hh * 512) * 2
                dst_t = Tl(w_out.ap[:, jj, hh * 512:(hh + 1) * 512],
                           self.pages[off0 // PAGE:(off0 + 1024 - 1) // PAGE + 1])
                op("pool", lambda e, dst_t=dst_t, jj=jj, hh=hh: e.dma_start(
                    out=dst_t.ap, in_=wo_d[l, k, jj * 128:(jj + 1) * 128, hh * 512:(hh + 1) * 512]),
                    w=[dst_t], dma=True)

        def w_in_slice(kc, c0):
            off = (kc * 2 * DFF + c0) * 2
            return Tl(w_in.ap[:, kc, c0:c0 + 128], self.pages[off // PAGE:(off + 255) // PAGE + 1])

        def w_out_slice(jj, hh):
            off0 = self.arena_off(w_out) + (jj * D + hh * 512) * 2
            return Tl(w_out.ap[:, jj, hh * 512:(hh + 1) * 512], self.pages[off0 // PAGE:(off0 + 1023) // PAGE + 1])

        def group(g):
            t0 = g * NGT
            b = (t0 * 128) // self.SEQ
            xls = []
            for i in range(NGT):
                xls.append(self.load_x(t0 + i, src, (g % 2) * NGT + i))
            yield
            for i in range(NGT):
                self.transpose_mod(xls[i], l, j, b, hT, i * 128)
            yield
            for jj in range(NJ):
                bg = self.nbank()
                bu = self.nbank()
                for kc in range(8):
                    ws = w_in_slice(kc, jj * 128)
                    op("pe", lambda e, bg=bg, ws=ws, kc=kc: e.matmul(
                        bg.ap[:, 0:G], lhsT=ws.ap, rhs=hT.ap[:, kc, :], start=(kc == 0), stop=(kc == 7)),
                        r=[ws, hT], w=[bg])
                for kc in range(8):
                    ws = w_in_slice(kc, DFF + jj * 128)
                    op("pe", lambda e, bu=bu, ws=ws, kc=kc: e.matmul(
                        bu.ap[:, 0:G], lhsT=ws.ap, rhs=hT.ap[:, kc, :], start=(kc == 0), stop=(kc == 7)),
                        r=[ws, hT], w=[bu])
                s_ = sg[jj % 2]
                op("act", lambda e, s_=s_, bg=bg: e.activation(out=s_.ap[:, :], in_=bg.ap[:, 0:G], func=AF.Silu),
                   r=[bg], w=[s_])
                at = self.sub(actT, actT.ap[:, jj, :])
                op("dve", lambda e, s_=s_, bu=bu, at=at: e.tensor_tensor(
                    out=at.ap, in0=bu.ap[:, 0:G], in1=s_.ap[:, :], op=ALU.mult), r=[bu, s_], w=[at])
            yield
            for i in range(NGT):
                t = t0 + i
                if (t * 128) % self.SEQ == 0:
                    self.load_rows(l, j, b)
                yb = [self.nbank(), self.nbank()]
                for hh in range(2):
                    for jj in range(NJ):
                        ws = w_out_slice(jj, hh)
                        op("pe", lambda e, hh=hh, jj=jj, ws=ws, yb=yb, i=i: e.matmul(
                            yb[hh].ap[:, :], lhsT=actT.ap[:, jj, i * 128:(i + 1) * 128], rhs=ws.ap,
                            start=(jj == 0), stop=(jj == NJ - 1)), r=[actT, ws], w=[yb[hh]])
                self.epilogue(t, l, j, src, dst, yb, t % 2)
            yield

        self.pipeline([group(g) for g in range(NG)])

    def arena_off(self, tl):
        return self.pages.index(tl.bufs[0]) * PAGE


    def load_w_cast(self, wt, ncols, dram_rows_fn, n_kc, splits):
        for kc in range(n_kc):
            for (c0, c1) in splits:
                d = self.nar(wt, kc * ncols + c0, c1 - c0, wt.ap[:, kc, c0:c1])
                self.op("pool", lambda e, d=d, kc=kc, c0=c0, c1=c1: e.dma_start(
                    out=d.ap, in_=dram_rows_fn(kc)[:, c0:c1]), w=[d], dma=True)

    def out_proj_epilogue(self, t, l, src, dst, w_out, chunks):
        op = self.op
        b = (t * 128) // self.SEQ
        if (t * 128) % self.SEQ == 0:
            self.load_rows(l, 1, b)
        yb = [self.nbank(), self.nbank()]
        n = len(chunks)
        for hh in range(2):
            for ci, (ct, ap) in enumerate(chunks):
                ws = self.nar(w_out, ci * D + hh * 512, 512, w_out.ap[:, ci, hh * 512:(hh + 1) * 512])
                op("pe", lambda e, hh=hh, ci=ci, ws=ws, ap=ap: e.matmul(
                    yb[hh].ap[:, :], lhsT=ap, rhs=ws.ap, start=(ci == 0), stop=(ci == n - 1)),
                    r=[ct, ws], w=[yb[hh]])
        self.epilogue(t, l, 1, src, dst, yb, t % 2)

    def even_mixer(self, l, src, dst):
        op = self.op
        e_ = l // 2
        dr = self.dr
        NT, TPS = self.NT, self.TPS
        ident, ones, tri, ustr = self.ident, self.ones, self.tri, self.ustr
        al = self.Alloc(self)
        w_in = al([128, 8, EV_IN], BF16); al.align()
        cdiag = al([128, 48, 128], BF16); al.align()
        pw = al([128, 4, 128], BF16)
        cb = al([1, 1536], BF16); al.align()
        dtb = al([128, 16], F32)
        arow = al([128, 16], F32)
        dvec = al([128, 16], F32)
        invf = al([128, 4, 128], F32)
        normgT = al([128, 8], F32); al.align()
        work0 = al.off
        cw = al([128, 48], F32)
        pwf = al([128, 4, 128], F32)
        psr = al([128, 512], F32); al.align()
        cbf = al([1, 1536], F32); al.align()
        iot = al([128, 128], F32)
        self.load_w_cast(w_in, EV_IN, lambda kc: dr["ev_w_in"][e_, kc * 128:(kc + 1) * 128, :], 8,
                         [(0, 1024), (1024, 2560), (2560, EV_IN)])
        op("sp", lambda e: e.dma_start(out=cw.ap[:, :], in_=dr["ssd_conv_wT"][e_].rearrange("p c k -> p (c k)")),
           w=[cw], dma=True)
        op("sp", lambda e: e.dma_start(out=pwf.ap[:, :, :], in_=dr["pool_wT"][e_]), w=[pwf], dma=True)
        op("sp", lambda e: e.dma_start(out=psr.ap[:, :], in_=dr["pool_scale"][e_, :].partition_broadcast(128)),
           w=[psr], dma=True)
        op("sp", lambda e: e.dma_start(out=cbf.ap[:, :], in_=dr["ssd_conv_b"][e_:e_ + 1, :]), w=[cbf], dma=True)
        op("sp", lambda e: e.dma_start(out=dtb.ap[:, :], in_=dr["ssd_dt_bias"][e_, :].partition_broadcast(128)),
           w=[dtb], dma=True)
        op("sp", lambda e: e.dma_start(out=arow.ap[:, :], in_=dr["ssd_a_log"][e_, :].partition_broadcast(128)),
           w=[arow], dma=True)
        op("sp", lambda e: e.dma_start(out=dvec.ap[:, :], in_=dr["ssd_d"][e_, :].partition_broadcast(128)),
           w=[dvec], dma=True)
        op("sp", lambda e: e.dma_start(out=normgT.ap[:, :], in_=dr["ssd_norm_gT"][e_]), w=[normgT], dma=True)
        op("dve", lambda e: e.tensor_tensor(
            out=cdiag.ap[:, :, :], in0=ident.ap[:, :].unsqueeze(1).broadcast_to([128, 48, 128]),
            in1=cw.ap[:, :].unsqueeze(2).broadcast_to([128, 48, 128]), op=ALU.mult), r=[ident, cw], w=[cdiag])
        op("dve", lambda e: e.tensor_tensor(
            out=pw.ap[:, :, :], in0=pwf.ap[:, :, :], in1=psr.ap[:, :].rearrange("p (g d) -> p g d", d=128),
            op=ALU.mult), r=[pwf, psr], w=[pw])
        op("dve", lambda e: e.tensor_copy(out=cb.ap[:, :], in_=cbf.ap[:, :]), r=[cbf], w=[cb])
        op("act", lambda e: e.activation(out=arow.ap[:, :], in_=arow.ap[:, :], func=AF.Exp), r=[arow], w=[arow])
        op("dve", lambda e: e.tensor_scalar(out=arow.ap[:, :], in0=arow.ap[:, :], scalar1=-1.0, scalar2=None,
                                            op0=ALU.mult), r=[arow], w=[arow])
        op("pool", lambda e: e.iota(iot.ap[:, :], pattern=[[1, 128]], base=1, channel_multiplier=0,
                                    allow_small_or_imprecise_dtypes=True), w=[iot])
        for g, wdw in enumerate(POOL_W):
            op("dve", lambda e, g=g, wdw=wdw: e.tensor_scalar(out=invf.ap[:, g, :], in0=iot.ap[:, :],
                                                              scalar1=float(wdw), scalar2=None, op0=ALU.min),
               r=[iot], w=[invf])
        op("dve", lambda e: e.reciprocal(out=invf.ap[:, :, :], in_=invf.ap[:, :, :]), r=[invf], w=[invf])
        al.off = work0
        hT = [al([128, 8, 128], BF16)] * 2
        cin = [al([128, 12, 132], BF16) for _ in range(2)]
        uin = [al([128, 4, 144], F32) for _ in range(2)]
        al.align()
        bcT = [al([128, 4, 128], BF16) for _ in range(2)]
        sz = [al([128, D], BF16) for _ in range(3)]
        sm = [al([128, 8, 16], F32) for _ in range(3)]
        al.align()
        xtok = al([128, D], F32)
        xdt = al([128, D], BF16)
        xdd = al([128, D], BF16)
        btok = al([128, 256], BF16)
        al.align()
        Dl = al([128, 16, 128], F32)
        xc = al([128, 12, 128], F32)
        t1 = al([128, D], F32)
        yn = al([128, D], F32)
        LT = al([128, 16, 128], BF16)
        MT = LT
        cbm = al([128, 2, 128], F32)
        al.align()
        yaT = [al([128, 8, 128], BF16)] * 2
        ybT = [al([128, 4, 128], BF16) for _ in range(2)]
        al.align()
        pta = al([128, 3, 144], F32)
        ptb = al([128, 2, 144], F32)
        prr = al([128, 4, 128], F32)
        pl = al([128, 4, 128], BF16)
        al.align()
        hst = al([128, D], F32)
        hb = al([128, D], BF16)
        ssq = al([128, 2], F32)

        def v3(ap, inner):
            return ap.rearrange("p (a b) -> p a b", b=inner)

        def bc(ap2, n):
            return ap2.unsqueeze(2).broadcast_to([128, ap2.shape[1], n])

        def tile(t):
            c = t % TPS
            b = t // TPS
            par = t % 2
            hT_, cin_, uin_, bcT_, sm_, yaT_, ybT_, sz_ = hT[par], cin[par], uin[par], bcT[par], sm[t % 3], yaT[par], ybT[par], sz[t % 3]
            cinp, uinp = cin[1 - par], uin[1 - par]
            xl = self.load_x(t, src, t % 4)
            yield
            self.transpose_mod(xl, l, 1, b, hT_, 0)
            yield
            if c == 0:
                op("pool", lambda e: e.memset(cin_.ap[:, :, 0:3], 0.0), w=[cin_])
                op("pool", lambda e: e.memset(uin_.ap[:, :, 0:15], 0.0), w=[uin_])
            else:
                op("pool", lambda e: e.tensor_copy(out=cin_.ap[:, :, 0:3], in_=cinp.ap[:, :, 128:131]),
                   r=[cinp], w=[cin_])
                op("pool", lambda e: e.tensor_copy(out=uin_.ap[:, :, 0:15], in_=uinp.ap[:, :, 128:143]),
                   r=[uinp], w=[uin_])
            for q in range(4):
                bk = self.nbank()
                for i in range(4):
                    cc = q * 4 + i
                    col = (1024 + cc * 128) if cc < 12 else (2576 + (cc - 12) * 128)
                    for kc in range(8):
                        ws = self.nar(w_in, kc * EV_IN + col, 128, w_in.ap[:, kc, col:col + 128])
                        op("pe", lambda e, bk=bk, i=i, ws=ws, kc=kc: e.matmul(
                            bk.ap[:, i * 128:(i + 1) * 128], lhsT=ws.ap, rhs=hT_.ap[:, kc, :],
                            start=(kc == 0), stop=(kc == 7)), r=[ws, hT_], w=[bk])
                if q < 3:
                    op("act", lambda e, bk=bk, q=q: e.activation(
                        out=cin_.ap[:, q * 4:(q + 1) * 4, 3:131], in_=v3(bk.ap[:, :], 128), func=AF.Identity),
                        r=[bk], w=[cin_])
                else:
                    op("dve", lambda e, bk=bk: e.tensor_copy(out=uin_.ap[:, :, 15:143], in_=v3(bk.ap[:, :], 128)),
                       r=[bk], w=[uin_])
            zb = [self.nbank(), self.nbank()]
            for hh in range(2):
                for kc in range(8):
                    ws = self.nar(w_in, kc * EV_IN + hh * 512, 512, w_in.ap[:, kc, hh * 512:(hh + 1) * 512])
                    op("pe", lambda e, hh=hh, ws=ws, kc=kc: e.matmul(
                        zb[hh].ap[:, :], lhsT=hT_.ap[:, kc, :], rhs=ws.ap, start=(kc == 0), stop=(kc == 7)),
                        r=[ws, hT_], w=[zb[hh]])
            db = self.nbank()
            for kc in range(8):
                ws = self.nar(w_in, kc * EV_IN + 2560, 16, w_in.ap[:, kc, 2560:2576])
                op("pe", lambda e, ws=ws, kc=kc: e.matmul(
                    db.ap[:, 0:16], lhsT=hT_.ap[:, kc, :], rhs=ws.ap, start=(kc == 0), stop=(kc == 7)),
                    r=[ws, hT_], w=[db])
            for hh in range(2):
                op("act", lambda e, hh=hh: e.activation(out=sz_.ap[:, hh * 512:(hh + 1) * 512], in_=zb[hh].ap[:, :],
                                                        func=AF.Silu), r=[zb[hh]], w=[sz_])
            U, AB, DT, ADT = (sm_.ap[:, i, :] for i in range(4))
            op("dve", lambda e: e.tensor_tensor(out=U, in0=db.ap[:, 0:16], in1=dtb.ap[:, :], op=ALU.add),
               r=[db, dtb], w=[sm_])
            op("dve", lambda e: e.scalar_tensor_tensor(out=AB, in0=U, scalar=-1.0, in1=U, op0=ALU.mult, op1=ALU.max),
               r=[sm_], w=[sm_])
            op("act", lambda e: e.activation(out=AB, in_=AB, func=AF.Exp, scale=-1.0), r=[sm_], w=[sm_])
            op("act", lambda e: e.activation(out=AB, in_=AB, func=AF.Ln, bias=1.0), r=[sm_], w=[sm_])
            op("dve", lambda e: e.scalar_tensor_tensor(out=DT, in0=U, scalar=0.0, in1=AB, op0=ALU.max, op1=ALU.add),
               r=[sm_], w=[sm_])
            op("dve", lambda e: e.tensor_tensor(out=ADT, in0=DT, in1=arow.ap[:, :], op=ALU.mult),
               r=[sm_, arow], w=[sm_])
            yield
            ACU, TOT, EA, CD, DS = (sm_.ap[:, i, :] for i in range(4, 8)) + (None,) if False else \
                (sm_.ap[:, 4, :], sm_.ap[:, 5, :], sm_.ap[:, 6, :], sm_.ap[:, 7, :], sm_.ap[:, 1, :])
            for q in range(3):
                bk = self.nbank()
                for i in range(4):
                    cc = q * 4 + i
                    for k in range(4):
                        op("pe", lambda e, bk=bk, i=i, cc=cc, k=k: e.matmul(
                            bk.ap[:, i * 128:(i + 1) * 128], lhsT=cdiag.ap[:, cc * 4 + k, :],
                            rhs=cin_.ap[:, cc, k:k + 128], start=(k == 0), stop=False), r=[cdiag, cin_], w=[bk])
                    op("pe", lambda e, bk=bk, i=i, cc=cc: e.matmul(
                        bk.ap[:, i * 128:(i + 1) * 128], lhsT=cb.ap[0:1, cc * 128:(cc + 1) * 128],
                        rhs=self.onesb.ap[0:1, :], start=False, stop=True), r=[cb, self.onesb], w=[bk])
                op("act", lambda e, bk=bk, q=q: e.activation(
                    out=xc.ap[:, q * 4:(q + 1) * 4, :], in_=v3(bk.ap[:, :], 128), func=AF.Silu), r=[bk], w=[xc])
            ab_ = self.nbank()
            op("pe", lambda e: e.matmul(ab_.ap[:, 0:16], lhsT=tri.ap[:, :], rhs=ADT, start=True, stop=True),
               r=[tri, sm_], w=[ab_])
            op("pe", lambda e: e.matmul(ab_.ap[:, 16:32], lhsT=ones.ap[:, :], rhs=ADT, start=True, stop=True),
               r=[ones, sm_], w=[ab_])
            op("dve", lambda e: e.tensor_copy(out=sm_.ap[:, 4:6, :], in_=v3(ab_.ap[:, 0:32], 16)), r=[ab_], w=[sm_])
            op("act", lambda e: e.activation(out=sm_.ap[:, 6:8, :], in_=sm_.ap[:, 4:6, :], func=AF.Exp),
               r=[sm_], w=[sm_])
            op("dve", lambda e: e.tensor_tensor(out=DS, in0=TOT, in1=ACU, op=ALU.subtract), r=[sm_], w=[sm_])
            op("act", lambda e: e.activation(out=DS, in_=DS, func=AF.Exp), r=[sm_], w=[sm_])
            op("dve", lambda e: e.tensor_tensor(out=DS, in0=DS, in1=DT, op=ALU.mult), r=[sm_], w=[sm_])
            op("dve", lambda e: e.tensor_tensor(out=Dl.ap[:, :, :], in0=bc(ADT, 128),
                                                in1=ustr.ap[:, :].unsqueeze(1).broadcast_to([128, 16, 128]),
                                                op=ALU.mult), r=[sm_, ustr], w=[Dl])
            op("pool", lambda e: e.tensor_copy(out=bcT_.ap[:, :, :], in_=xc.ap[:, 8:12, :]), r=[xc], w=[bcT_])
            for g, wdw in enumerate(POOL_W):
                prev, pidx = uin_, g
                nlev = g + 1
                for m in range(1, nlev + 1):
                    sh = 1 << (m - 1)
                    lo = 15 - (wdw - (1 << m))
                    if m == nlev:
                        dstt, dap = prr, prr.ap[:, g, :]
                    else:
                        dstt = pta if (m % 2 == 1) else ptb
                        gi = (g - 1) if (m % 2 == 1) else (g - 2)
                        dap = dstt.ap[:, gi, lo:143]
                    op("pool", lambda e, dap=dap, prev=prev, pidx=pidx, lo=lo, sh=sh: e.tensor_tensor(
                        out=dap, in0=prev.ap[:, pidx, lo:143], in1=prev.ap[:, pidx, lo - sh:143 - sh], op=ALU.add),
                        r=[prev], w=[dstt])
                    prev = dstt
                    pidx = gi if m < nlev else 0
            if c == 0:
                op("pool", lambda e: e.tensor_tensor(out=prr.ap[:, :, :], in0=prr.ap[:, :, :], in1=invf.ap[:, :, :],
                                                     op=ALU.mult), r=[prr, invf], w=[prr])
            else:
                for g, wdw in enumerate(POOL_W):
                    op("pool", lambda e, g=g, wdw=wdw: e.tensor_scalar(
                        out=prr.ap[:, g, :], in0=prr.ap[:, g, :], scalar1=1.0 / wdw, scalar2=None, op0=ALU.mult),
                        r=[prr], w=[prr])
            op("pool", lambda e: e.tensor_tensor(out=pl.ap[:, :, :], in0=prr.ap[:, :, :], in1=uin_.ap[:, :, 15:143],
                                                 op=ALU.subtract), r=[prr, uin_], w=[pl])
            yield
            xb = [self.nbank(), self.nbank()]
            for cc in range(8):
                op("pe", lambda e, cc=cc: e.transpose(out=xb[cc // 4].ap[:, (cc % 4) * 128:(cc % 4 + 1) * 128],
                                                      in_=xc.ap[:, cc, :], identity=ident.ap[:, :]),
                   r=[xc, ident], w=[xb[cc // 4]])
            bb_ = self.nbank()
            for g in range(2):
                op("pe", lambda e, g=g: e.transpose(out=bb_.ap[:, g * 128:(g + 1) * 128], in_=xc.ap[:, 8 + g, :],
                                                    identity=ident.ap[:, :]), r=[xc, ident], w=[bb_])
            for hh in range(2):
                op("act", lambda e, hh=hh: e.activation(out=xtok.ap[:, hh * 512:(hh + 1) * 512], in_=xb[hh].ap[:, :],
                                                        func=AF.Identity), r=[xb[hh]], w=[xtok])
            op("dve", lambda e: e.tensor_copy(out=btok.ap[:, :], in_=bb_.ap[:, 0:256]), r=[bb_], w=[btok])
            cbk = self.nbank()
            for g in range(2):
                op("pe", lambda e, g=g: e.matmul(cbk.ap[:, g * 128:(g + 1) * 128], lhsT=bcT_.ap[:, g, :],
                                                 rhs=bcT_.ap[:, 2 + g, :], start=True, stop=True), r=[bcT_], w=[cbk])
            op("dve", lambda e: e.tensor_tensor(out=cbm.ap[:, :, :], in0=v3(cbk.ap[:, 0:256], 128),
                                                in1=tri.ap[:, :].unsqueeze(1).broadcast_to([128, 2, 128]), op=ALU.mult),
               r=[cbk, tri], w=[cbm])
            for q in range(4):
                bk = self.nbank()
                for i in range(4):
                    h = q * 4 + i
                    op("pe", lambda e, bk=bk, i=i, h=h: e.matmul(bk.ap[:, i * 128:(i + 1) * 128], lhsT=Dl.ap[:, h, :],
                                                                 rhs=tri.ap[:, :], start=True, stop=True),
                       r=[Dl, tri], w=[bk])
                op("act", lambda e, bk=bk, q=q: e.activation(out=LT.ap[:, q * 4:(q + 1) * 4, :], in_=v3(bk.ap[:, :], 128),
                                                             func=AF.Exp), r=[bk], w=[LT])
            pb = self.nbank()
            for g in range(4):
                op("pe", lambda e, g=g: e.matmul(pb.ap[:, g * 128:(g + 1) * 128], lhsT=pw.ap[:, g, :],
                                                 rhs=pl.ap[:, g, :], start=True, stop=True), r=[pw, pl], w=[pb])
            op("act", lambda e: e.activation(out=ybT_.ap[:, :, :], in_=v3(pb.ap[:, :], 128), func=AF.Identity),
               r=[pb], w=[ybT_])
            for g in range(2):
                op("dve", lambda e, g=g: e.tensor_tensor(
                    out=MT.ap[:, g * 8:(g + 1) * 8, :], in0=LT.ap[:, g * 8:(g + 1) * 8, :],
                    in1=cbm.ap[:, g:g + 1, :].broadcast_to([128, 8, 128]), op=ALU.mult), r=[LT, cbm], w=[MT])
            op("dve", lambda e: e.tensor_tensor(out=v3(xdt.ap[:, :], 64), in0=v3(xtok.ap[:, :], 64), in1=bc(DT, 64),
                                                op=ALU.mult), r=[xtok, sm_], w=[xdt])
            op("dve", lambda e: e.tensor_tensor(out=v3(xdd.ap[:, :], 64), in0=v3(xtok.ap[:, :], 64), in1=bc(DS, 64),
                                                op=ALU.mult), r=[xtok, sm_], w=[xdd])
            yield
            if c == 0:
                op("pool", lambda e: e.memset(hst.ap[:, :], 0.0), w=[hst])
                op("pool", lambda e: e.memset(hb.ap[:, :], 0.0), w=[hb])
            yd = [self.nbank(), self.nbank()]
            for h in range(16):
                op("pe", lambda e, h=h: e.matmul(yd[h // 8].ap[:, (h % 8) * 64:(h % 8 + 1) * 64], lhsT=MT.ap[:, h, :],
                                                 rhs=xdt.ap[:, h * 64:(h + 1) * 64], start=True, stop=True),
                   r=[MT, xdt], w=[yd[h // 8]])
            yo = [self.nbank(), self.nbank()]
            for g in range(2):
                op("pe", lambda e, g=g: e.matmul(yo[g].ap[:, :], lhsT=bcT_.ap[:, 2 + g, :],
                                                 rhs=hb.ap[:, g * 512:(g + 1) * 512], start=True, stop=True),
                   r=[bcT_, hb], w=[yo[g]])
            stb = [self.nbank(), self.nbank()]
            for g in range(2):
                op("pe", lambda e, g=g: e.matmul(stb[g].ap[:, :], lhsT=btok.ap[:, g * 128:(g + 1) * 128],
                                                 rhs=xdd.ap[:, g * 512:(g + 1) * 512], start=True, stop=True),
                   r=[btok, xdd], w=[stb[g]])
            for g in range(2):
                op("dve", lambda e, g=g: e.tensor_tensor(
                    out=v3(t1.ap[:, g * 512:(g + 1) * 512], 64), in0=v3(yo[g].ap[:, :], 64),
                    in1=bc(sm_.ap[:, 6, g * 8:(g + 1) * 8], 64), op=ALU.mult), r=[yo[g], sm_], w=[t1])
                op("dve", lambda e, g=g: e.tensor_tensor(
                    out=t1.ap[:, g * 512:(g + 1) * 512], in0=yd[g].ap[:, :], in1=t1.ap[:, g * 512:(g + 1) * 512],
                    op=ALU.add), r=[yd[g], t1], w=[t1])
            op("pool", lambda e: e.tensor_tensor(out=v3(yn.ap[:, :], 64), in0=v3(xtok.ap[:, :], 64),
                                                 in1=bc(dvec.ap[:, :], 64), op=ALU.mult), r=[xtok, dvec], w=[yn])
            op("pool", lambda e: e.tensor_tensor(out=yn.ap[:, :], in0=yn.ap[:, :], in1=t1.ap[:, :], op=ALU.add),
               r=[yn, t1], w=[yn])
            op("pool", lambda e: e.tensor_tensor(out=yn.ap[:, :], in0=yn.ap[:, :], in1=sz_.ap[:, :], op=ALU.mult),
               r=[yn, sz_], w=[yn])
            op("act", lambda e: e.activation(out=t1.ap[:, :], in_=yn.ap[:, :], func=AF.Square,
                                             accum_out=ssq.ap[:, 0:1]), r=[yn], w=[t1, ssq])
            op("pool", lambda e: e.tensor_scalar(out=ssq.ap[:, 1:2], in0=ssq.ap[:, 0:1], scalar1=1.0 / D, scalar2=EPS,
                                                 op0=ALU.mult, op1=ALU.add), r=[ssq], w=[ssq])
            op("pool", lambda e: e.tensor_tensor(out=ssq.ap[:, 1:2], in0=ssq.ap[:, 1:2], in1=self.m05.ap[:, 0:1],
                                                 op=ALU.pow), r=[ssq, self.m05], w=[ssq])
            op("dve", lambda e: e.tensor_scalar(out=yn.ap[:, :], in0=yn.ap[:, :], scalar1=ssq.ap[:, 1:2], scalar2=None,
                                                op0=ALU.mult), r=[yn, ssq], w=[yn])
            op("dve", lambda e: e.tensor_tensor(out=v3(hst.ap[:, :], 64), in0=v3(hst.ap[:, :], 64),
                                                in1=bc(sm_.ap[:, 7, :], 64), op=ALU.mult), r=[hst, sm_], w=[hst])
            for g in range(2):
                op("dve", lambda e, g=g: e.tensor_tensor(
                    out=hst.ap[:, g * 512:(g + 1) * 512], in0=stb[g].ap[:, :], in1=hst.ap[:, g * 512:(g + 1) * 512],
                    op=ALU.add), r=[stb[g], hst], w=[hst])
            op("pool", lambda e: e.tensor_copy(out=hb.ap[:, :], in_=hst.ap[:, :]), r=[hst], w=[hb])
            yield
            tb = [self.nbank(), self.nbank()]
            for cc in range(8):
                op("pe", lambda e, cc=cc: e.transpose(out=tb[cc // 4].ap[:, (cc % 4) * 128:(cc % 4 + 1) * 128],
                                                      in_=yn.ap[:, cc * 128:(cc + 1) * 128], identity=ident.ap[:, :]),
                   r=[yn, ident], w=[tb[cc // 4]])
            for cc in range(8):
                op("act", lambda e, cc=cc: e.activation(
                    out=yaT_.ap[:, cc, :], in_=tb[cc // 4].ap[:, (cc % 4) * 128:(cc % 4 + 1) * 128], func=AF.Identity,
                    scale=normgT.ap[:, cc:cc + 1]), r=[tb[cc // 4], normgT], w=[yaT_])
            yg = Tl(None, [self.ybuf_g[t]])
            op("act", lambda e: e.dma_start(out=self.dr["yT"][t, :, 0:1024], in_=yaT_.ap[:, :, :].rearrange("p a b -> p (a b)")),
               r=[yaT_], w=[yg], dma=True)
            op("act", lambda e: e.dma_start(out=self.dr["yT"][t, :, 1024:1536], in_=ybT_.ap[:, :, :].rearrange("p a b -> p (a b)")),
               r=[ybT_], w=[yg], dma=True)
            yield

        self.pipeline([tile(t) for t in range(NT)])
        self.mixer_out(l, src, dst, dr["ev_w_out"][e_])

    def mixer_out(self, l, src, dst, w_dram):
        op = self.op
        al = self.Alloc(self)
        w_out = al([128, 12, D], BF16); al.align()
        ybuf = [al([128, 12, 128], BF16) for _ in range(3)]
        self.load_w_cast(w_out, D, lambda kc: w_dram[kc * 128:(kc + 1) * 128, :], 12, [(0, 512), (512, 1024)])
        al.align()
        li = self.layers.index(l)
        extra = self.mods_gen(self.layers[li + 1], al) if li + 1 < len(self.layers) else None

        def tile(t):
            yb_ = ybuf[t % 3]
            op("sp", lambda e: e.dma_start(out=yb_.ap[:, :, :].rearrange("p a b -> p (a b)"), in_=self.dr["yT"][t, :, :]),
               r=[Tl(None, [self.ybuf_g[t]])], w=[yb_], dma=True)
            yield
            chunks = [(yb_, yb_.ap[:, i, :]) for i in range(12)]
            self.out_proj_epilogue(t, l, src, dst, w_out, chunks)
            yield

        self.pipeline([tile(t) for t in range(self.NT)], extra=extra)

    def odd_mixer(self, l, src, dst):
        op = self.op
        o_ = l // 2
        dr = self.dr
        NT, TPS = self.NT, self.TPS
        ident, ones, onesb = self.ident, self.ones, self.onesb
        al = self.Alloc(self)
        w_in = al([128, 8, OD_IN], BF16); al.align()
        fdiag = al([128, 124, 128], BF16); al.align()
        ldiag = al([128, 32, 128], BF16); al.align()
        wa = al([128, 8, 128], BF16)
        wx = al([128, 8, 128], BF16); al.align()
        brow = al([128, 1536], BF16)
        cl = al([128, 8], F32)
        lng = al([128, 4], F32)
        lnb = al([128, 4], F32)
        hprev = al([128, 8], F32)
        al.align()
        work0 = al.off
        fw = al([128, 124], F32)
        lw = al([128, 32], F32)
        waf = al([128, 8, 128], F32)
        wxf = al([128, 8, 128], F32); al.align()
        browf = al([128, 1536], F32); al.align()
        zt_ = al([128, 8], F32)
        self.load_w_cast(w_in, OD_IN, lambda kc: dr["od_w_in"][o_, kc * 128:(kc + 1) * 128, :], 8,
                         [(0, 1024), (1024, 2048), (2048, OD_IN)])
        op("sp", lambda e: e.dma_start(out=fw.ap[:, :], in_=dr["conf_dw_wT"][o_].rearrange("p c k -> p (c k)")),
           w=[fw], dma=True)
        op("sp", lambda e: e.dma_start(out=lw.ap[:, :], in_=dr["lru_conv_wT"][o_].rearrange("p c k -> p (c k)")),
           w=[lw], dma=True)
        op("sp", lambda e: e.dma_start(out=waf.ap[:, :, :], in_=dr["lru_waT"][o_]), w=[waf], dma=True)
        op("sp", lambda e: e.dma_start(out=wxf.ap[:, :, :], in_=dr["lru_wxT"][o_]), w=[wxf], dma=True)
        op("sp", lambda e: e.dma_start(out=browf.ap[0:1, 0:1024], in_=dr["lru_conv_b"][o_:o_ + 1, :]), w=[browf], dma=True)
        op("sp", lambda e: e.dma_start(out=browf.ap[0:1, 1024:1536], in_=dr["conf_dw_b"][o_:o_ + 1, :]), w=[browf], dma=True)
        op("sp", lambda e: e.dma_start(out=browf.ap[32:33, 0:1024], in_=dr["lru_ba"][o_:o_ + 1, :]), w=[browf], dma=True)
        op("sp", lambda e: e.dma_start(out=browf.ap[64:65, 0:1024], in_=dr["lru_bx"][o_:o_ + 1, :]), w=[browf], dma=True)
        op("sp", lambda e: e.dma_start(out=cl.ap[:, :], in_=dr["lru_lamT"][o_]), w=[cl], dma=True)
        op("sp", lambda e: e.dma_start(out=lng.ap[:, :], in_=dr["conf_ln_gT"][o_]), w=[lng], dma=True)
        op("sp", lambda e: e.dma_start(out=lnb.ap[:, :], in_=dr["conf_ln_bT"][o_]), w=[lnb], dma=True)
        op("dve", lambda e: e.tensor_tensor(
            out=fdiag.ap[:, :, :], in0=ident.ap[:, :].unsqueeze(1).broadcast_to([128, 124, 128]),
            in1=fw.ap[:, :].unsqueeze(2).broadcast_to([128, 124, 128]), op=ALU.mult), r=[ident, fw], w=[fdiag])
        op("dve", lambda e: e.tensor_tensor(
            out=ldiag.ap[:, :, :], in0=ident.ap[:, :].unsqueeze(1).broadcast_to([128, 32, 128]),
            in1=lw.ap[:, :].unsqueeze(2).broadcast_to([128, 32, 128]), op=ALU.mult), r=[ident, lw], w=[ldiag])
        op("dve", lambda e: e.tensor_copy(out=wa.ap[:, :, :], in_=waf.ap[:, :, :]), r=[waf], w=[wa])
        op("dve", lambda e: e.tensor_copy(out=wx.ap[:, :, :], in_=wxf.ap[:, :, :]), r=[wxf], w=[wx])
        op("dve", lambda e: e.tensor_copy(out=brow.ap[0:1, :], in_=browf.ap[0:1, :]), r=[browf], w=[brow])
        op("dve", lambda e: e.tensor_copy(out=brow.ap[32:33, 0:1024], in_=browf.ap[32:33, 0:1024]), r=[browf], w=[brow])
        op("dve", lambda e: e.tensor_copy(out=brow.ap[64:65, 0:1024], in_=browf.ap[64:65, 0:1024]), r=[browf], w=[brow])
        op("act", lambda e: e.activation(out=zt_.ap[:, :], in_=cl.ap[:, :], func=AF.Exp, scale=-1.0), r=[cl], w=[zt_])
        op("dve", lambda e: e.tensor_scalar(out=cl.ap[:, :], in0=zt_.ap[:, :], scalar1=1.0 / 5, scalar2=-1.0 / 4,
                                            op0=ALU.mult, op1=ALU.add), r=[zt_], w=[cl])
        for cst in (1.0 / 3, -1.0 / 2, 1.0):
            op("dve", lambda e: e.tensor_tensor(out=cl.ap[:, :], in0=cl.ap[:, :], in1=zt_.ap[:, :], op=ALU.mult),
               r=[cl, zt_], w=[cl])
            op("dve", lambda e, cst=cst: e.tensor_scalar(out=cl.ap[:, :], in0=cl.ap[:, :], scalar1=cst, scalar2=None,
                                                         op0=ALU.add), r=[cl], w=[cl])
        op("dve", lambda e: e.tensor_tensor(out=cl.ap[:, :], in0=cl.ap[:, :], in1=zt_.ap[:, :], op=ALU.mult),
           r=[cl, zt_], w=[cl])
        op("dve", lambda e: e.tensor_scalar(out=cl.ap[:, :], in0=cl.ap[:, :], scalar1=-8.0, scalar2=None, op0=ALU.mult),
           r=[cl], w=[cl])
        al.off = work0
        hT = al([128, 8, 128], BF16)
        cfin = [al([128, 4, 160], BF16) for _ in range(2)]
        lrin = [al([128, 8, 132], BF16) for _ in range(2)]
        gs = [al([128, 8, 128], BF16) for _ in range(3)]
        ycT = [al([128, 4, 128], BF16) for _ in range(2)]
        hc = al([128, 4, 128], F32)
        sq = al([128, 4, 128], F32)
        mean = al([128, 128], F32)
        rstd = al([128, 128], F32)
        msq = rstd
        xrcs = [al([128, 8, 128], BF16) for _ in range(2)]
        ydT = al([128, 8, 128], BF16)
        R = al([128, 8, 128], F32)
        I = al([128, 8, 128], F32)
        A = al([128, 8, 128], F32)

        def v3(ap, inner):
            return ap.rearrange("p (a b) -> p a b", b=inner)

        def bc(ap2, n):
            return ap2.unsqueeze(2).broadcast_to([128, ap2.shape[1], n])

        def tile(t):
            c = t % TPS
            b = t // TPS
            par = t % 2
            cfin_, lrin_, gs_, ycT_, xrc = cfin[par], lrin[par], gs[t % 3], ycT[par], xrcs[par]
            cfinp, lrinp = cfin[1 - par], lrin[1 - par]
            xl = self.load_x(t, src, t % 4)
            yield
            self.transpose_mod(xl, l, 1, b, hT, 0)
            yield
            if c == 0:
                op("pool", lambda e: e.memset(cfin_.ap[:, :, 0:30], 0.0), w=[cfin_])
                op("pool", lambda e: e.memset(lrin_.ap[:, :, 0:3], 0.0), w=[lrin_])
            else:
                op("pool", lambda e: e.tensor_copy(out=cfin_.ap[:, :, 0:30], in_=cfinp.ap[:, :, 128:158]),
                   r=[cfinp], w=[cfin_])
                op("pool", lambda e: e.tensor_copy(out=lrin_.ap[:, :, 0:3], in_=lrinp.ap[:, :, 128:131]),
                   r=[lrinp], w=[lrin_])
            bks = []
            for q in range(6):
                bk = self.nbank()
                bks.append(bk)
                for i in range(4):
                    col = (q * 4 + i) * 128
                    for kc in range(8):
                        ws = self.nar(w_in, kc * OD_IN + col, 128, w_in.ap[:, kc, col:col + 128])
                        op("pe", lambda e, bk=bk, i=i, ws=ws, kc=kc: e.matmul(
                            bk.ap[:, i * 128:(i + 1) * 128], lhsT=ws.ap, rhs=hT.ap[:, kc, :],
                            start=(kc == 0), stop=(kc == 7)), r=[ws, hT], w=[bk])
                if q == 1:
                    op("act", lambda e, bk=bk: e.activation(out=cfin_.ap[:, :, 30:158], in_=v3(bk.ap[:, :], 128),
                                                            func=AF.Sigmoid), r=[bk], w=[cfin_])
                    op("dve", lambda e: e.tensor_tensor(out=cfin_.ap[:, :, 30:158], in0=v3(bks[0].ap[:, :], 128),
                                                        in1=cfin_.ap[:, :, 30:158], op=ALU.mult),
                       r=[bks[0], cfin_], w=[cfin_])
                elif q in (2, 3):
                    op("act", lambda e, bk=bk, q=q: e.activation(
                        out=lrin_.ap[:, (q - 2) * 4:(q - 1) * 4, 3:131], in_=v3(bk.ap[:, :], 128), func=AF.Identity),
                        r=[bk], w=[lrin_])
                elif q in (4, 5):
                    op("act", lambda e, bk=bk, q=q: e.activation(
                        out=gs_.ap[:, (q - 4) * 4:(q - 3) * 4, :], in_=v3(bk.ap[:, :], 128), func=AF.Gelu_apprx_tanh),
                        r=[bk], w=[gs_])
            yield
            fb = self.nbank()
            for cc in range(4):
                for k in range(31):
                    op("pe", lambda e, cc=cc, k=k: e.matmul(
                        fb.ap[:, cc * 128:(cc + 1) * 128], lhsT=fdiag.ap[:, cc * 31 + k, :],
                        rhs=cfin_.ap[:, cc, k:k + 128], start=(k == 0), stop=False), r=[fdiag, cfin_], w=[fb])
                op("pe", lambda e, cc=cc: e.matmul(
                    fb.ap[:, cc * 128:(cc + 1) * 128], lhsT=brow.ap[0:1, 1024 + cc * 128:1024 + (cc + 1) * 128],
                    rhs=onesb.ap[0:1, :], start=False, stop=True), r=[brow, onesb], w=[fb])
            op("act", lambda e: e.activation(out=hc.ap[:, :, :], in_=v3(fb.ap[:, :], 128), func=AF.Identity),
               r=[fb], w=[hc])
            op("act", lambda e: e.activation(out=sq.ap[:, :, :], in_=v3(fb.ap[:, :], 128), func=AF.Square),
               r=[fb], w=[sq])
            lb = [self.nbank(), self.nbank()]
            for cc in range(8):
                bk = lb[cc // 4]
                i = cc % 4
                for k in range(4):
                    op("pe", lambda e, bk=bk, i=i, cc=cc, k=k: e.matmul(
                        bk.ap[:, i * 128:(i + 1) * 128], lhsT=ldiag.ap[:, cc * 4 + k, :],
                        rhs=lrin_.ap[:, cc, k:k + 128], start=(k == 0), stop=False), r=[ldiag, lrin_], w=[bk])
                op("pe", lambda e, bk=bk, i=i, cc=cc: e.matmul(
                    bk.ap[:, i * 128:(i + 1) * 128], lhsT=brow.ap[0:1, cc * 128:(cc + 1) * 128],
                    rhs=onesb.ap[0:1, :], start=False, stop=True), r=[brow, onesb], w=[bk])
            for hh in range(2):
                op("act", lambda e, hh=hh: e.activation(out=xrc.ap[:, hh * 4:(hh + 1) * 4, :],
                                                        in_=v3(lb[hh].ap[:, :], 128), func=AF.Identity),
                   r=[lb[hh]], w=[xrc])
            yield
            sb_ = self.nbank()
            for cc in range(4):
                op("pe", lambda e, cc=cc: e.matmul(sb_.ap[:, 0:128], lhsT=ones.ap[:, :], rhs=hc.ap[:, cc, :],
                                                   start=(cc == 0), stop=(cc == 3)), r=[ones, hc], w=[sb_])
            for cc in range(4):
                op("pe", lambda e, cc=cc: e.matmul(sb_.ap[:, 128:256], lhsT=ones.ap[:, :], rhs=sq.ap[:, cc, :],
                                                   start=(cc == 0), stop=(cc == 3)), r=[ones, sq], w=[sb_])
            for (wg, prow, dstg) in ((wa, 32, R), (wx, 64, I)):
                gb = [self.nbank(), self.nbank()]
                for h in range(8):
                    bk = gb[h // 4]
                    i = h % 4
                    op("pe", lambda e, bk=bk, i=i, h=h, wg=wg: e.matmul(
                        bk.ap[:, i * 128:(i + 1) * 128], lhsT=wg.ap[:, h, :], rhs=xrc.ap[:, h, :],
                        start=True, stop=False), r=[wg, xrc], w=[bk])
                    op("pe", lambda e, bk=bk, i=i, h=h, prow=prow: e.matmul(
                        bk.ap[:, i * 128:(i + 1) * 128], lhsT=brow.ap[prow:prow + 1, h * 128:(h + 1) * 128],
                        rhs=onesb.ap[prow:prow + 1, :], start=False, stop=True), r=[brow, onesb], w=[bk])
                for hh in range(2):
                    op("act", lambda e, hh=hh, gb=gb, dstg=dstg: e.activation(
                        out=dstg.ap[:, hh * 4:(hh + 1) * 4, :], in_=v3(gb[hh].ap[:, :], 128), func=AF.Sigmoid),
                        r=[gb[hh]], w=[dstg])
            op("dve", lambda e: e.tensor_scalar(out=mean.ap[:, :], in0=sb_.ap[:, 0:128], scalar1=1.0 / 512, scalar2=None,
                                                op0=ALU.mult), r=[sb_], w=[mean])
            op("dve", lambda e: e.tensor_tensor(out=msq.ap[:, :], in0=mean.ap[:, :], in1=mean.ap[:, :], op=ALU.mult),
               r=[mean], w=[msq])
            op("dve", lambda e: e.scalar_tensor_tensor(out=rstd.ap[:, :], in0=sb_.ap[:, 128:256], scalar=1.0 / 512,
                                                       in1=msq.ap[:, :], op0=ALU.mult, op1=ALU.subtract),
               r=[sb_, msq], w=[rstd])
            op("pool", lambda e: e.tensor_scalar(out=rstd.ap[:, :], in0=rstd.ap[:, :], scalar1=EPS, scalar2=None,
                                                 op0=ALU.add), r=[rstd], w=[rstd])
            op("pool", lambda e: e.tensor_tensor(out=rstd.ap[:, :], in0=rstd.ap[:, :], in1=self.m05.ap[:, :],
                                                 op=ALU.pow), r=[rstd, self.m05], w=[rstd])
            op("dve", lambda e: e.tensor_tensor(out=hc.ap[:, :, :], in0=hc.ap[:, :, :],
                                                in1=mean.ap[:, :].unsqueeze(1).broadcast_to([128, 4, 128]),
                                                op=ALU.subtract), r=[hc, mean], w=[hc])
            op("dve", lambda e: e.tensor_tensor(out=hc.ap[:, :, :], in0=hc.ap[:, :, :],
                                                in1=rstd.ap[:, :].unsqueeze(1).broadcast_to([128, 4, 128]),
                                                op=ALU.mult), r=[hc, rstd], w=[hc])
            for cc in range(4):
                op("act", lambda e, cc=cc: e.activation(out=ycT_.ap[:, cc, :], in_=hc.ap[:, cc, :], func=AF.Silu,
                                                        bias=lnb.ap[:, cc:cc + 1], scale=lng.ap[:, cc:cc + 1]),
                   r=[hc, lng, lnb], w=[ycT_])
            yield
            if c == 0:
                op("pool", lambda e: e.memset(hprev.ap[:, :], 0.0), w=[hprev])
            op("dve", lambda e: e.tensor_tensor(out=R.ap[:, :, :], in0=R.ap[:, :, :], in1=bc(cl.ap[:, :], 128),
                                                op=ALU.mult), r=[R, cl], w=[R])
            op("act", lambda e: e.activation(out=A.ap[:, :, :], in_=R.ap[:, :, :], func=AF.Exp), r=[R], w=[A])
            op("act", lambda e: e.activation(out=R.ap[:, :, :], in_=R.ap[:, :, :], func=AF.Exp, scale=2.0),
               r=[R], w=[R])
            op("act", lambda e: e.activation(out=R.ap[:, :, :], in_=R.ap[:, :, :], func=AF.Ln, scale=-1.0, bias=1.0),
               r=[R], w=[R])
            op("act", lambda e: e.activation(out=R.ap[:, :, :], in_=R.ap[:, :, :], func=AF.Exp, scale=0.5),
               r=[R], w=[R])
            op("dve", lambda e: e.tensor_tensor(out=I.ap[:, :, :], in0=I.ap[:, :, :], in1=xrc.ap[:, :, :], op=ALU.mult),
               r=[I, xrc], w=[I])
            op("dve", lambda e: e.tensor_tensor(out=I.ap[:, :, :], in0=I.ap[:, :, :], in1=R.ap[:, :, :], op=ALU.mult),
               r=[I, R], w=[I])
            for cc in range(8):
                op("dve", lambda e, cc=cc: e.tensor_tensor_scan(
                    out=R.ap[:, cc, :], data0=A.ap[:, cc, :], data1=I.ap[:, cc, :], initial=hprev.ap[:, cc:cc + 1],
                    op0=ALU.mult, op1=ALU.add), r=[A, I, hprev], w=[R])
            op("dve", lambda e: e.tensor_copy(out=hprev.ap[:, :], in_=R.ap[:, :, 127]), r=[R], w=[hprev])
            op("pool", lambda e: e.tensor_tensor(out=ydT.ap[:, :, :], in0=R.ap[:, :, :], in1=gs_.ap[:, :, :], op=ALU.mult),
               r=[R, gs_], w=[ydT])
            yg = Tl(None, [self.ybuf_g[t]])
            op("act", lambda e: e.dma_start(out=self.dr["yT"][t, :, 0:512], in_=ycT_.ap[:, :, :].rearrange("p a b -> p (a b)")),
               r=[ycT_], w=[yg], dma=True)
            op("pool", lambda e: e.dma_start(out=self.dr["yT"][t, :, 512:1536], in_=ydT.ap[:, :, :].rearrange("p a b -> p (a b)")),
               r=[ydT], w=[yg], dma=True)
            yield

        self.pipeline([tile(t) for t in range(NT)])
        self.mixer_out(l, src, dst, dr["od_w_out"][o_])


def host_layout(inp, n_cores, seq):
    f = lambda a: np.ascontiguousarray(np.asarray(a, dtype=np.float32))
    shared = {}
    shared["ada_w"] = f(inp["ada_w"])
    shared["ada_b"] = f(inp["ada_b"])
    shared["ada_bT"] = f(np.asarray(inp["ada_b"]).reshape(L_FULL, 72, 128).transpose(0, 2, 1))
    for k in ("ln_g", "ln_b", "ffn_w_in", "ffn_w_out", "ev_w_in", "ssd_conv_b", "ssd_dt_bias", "ssd_a_log",
              "ssd_d", "pool_scale", "ev_w_out", "od_w_in", "conf_dw_b", "lru_conv_b",
              "lru_ba", "lru_bx", "od_w_out"):
        shared[k] = f(inp[k])
    shared["ssd_conv_wT"] = f(np.asarray(inp["ssd_conv_w"]).reshape(-1, 4, 12, 128).transpose(0, 3, 2, 1))
    shared["ssd_norm_gT"] = f(np.asarray(inp["ssd_norm_g"]).reshape(-1, 8, 128).transpose(0, 2, 1))
    shared["pool_wT"] = f(np.asarray(inp["pool_w"]).transpose(0, 2, 1, 3))
    shared["conf_dw_wT"] = f(np.asarray(inp["conf_dw_w"]).reshape(-1, 31, 4, 128).transpose(0, 3, 2, 1))
    shared["conf_ln_gT"] = f(np.asarray(inp["conf_ln_g"]).reshape(-1, 4, 128).transpose(0, 2, 1))
    shared["conf_ln_bT"] = f(np.asarray(inp["conf_ln_b"]).reshape(-1, 4, 128).transpose(0, 2, 1))
    shared["lru_conv_wT"] = f(np.asarray(inp["lru_conv_w"]).reshape(-1, 4, 8, 128).transpose(0, 3, 2, 1))
    shared["lru_waT"] = f(np.asarray(inp["lru_wa"]).transpose(0, 2, 1, 3))
    shared["lru_wxT"] = f(np.asarray(inp["lru_wx"]).transpose(0, 2, 1, 3))
    shared["lru_lamT"] = f(np.asarray(inp["lru_lambda"]).reshape(-1, 8, 128).transpose(0, 2, 1))
    x = np.asarray(inp["x"], dtype=np.float32)
    c = np.asarray(inp["c"], dtype=np.float32)
    maps = []
    for i in range(n_cores):
        m = dict(shared)
        m["x"] = np.ascontiguousarray(x[2 * i:2 * i + 2].reshape(2 * seq, D))
        m["cT"] = np.ascontiguousarray(c[2 * i:2 * i + 2].reshape(2, 8, 128).transpose(2, 1, 0))
        maps.append(m)
    return maps


_NC_CACHE = {}


def kernel(**inputs):
    x = np.asarray(inputs["x"])
    bsz, seq, _ = x.shape
    n_cores = bsz // 2
    maps = host_layout(inputs, n_cores, seq)
    key = (seq,)
    if key not in _NC_CACHE:
        _NC_CACHE[key] = Builder(seq, list(range(L_FULL))).build()
    nc = _NC_CACHE[key]
    res = run_bass_kernel_spmd(nc, maps, core_ids=list(range(n_cores)))
    outs = [np.asarray(r["out"]).reshape(2, seq, D) for r in res.results]
    return np.concatenate(outs, axis=0).astype(np.float32)
```
